# Optimizing a Trainium2 kernel written in Bass

```python
import math
import jax
import jax.numpy as jnp
from jax import lax
import numpy as np

D_MODEL = 1024
BATCH = 2
SEQ = 8192
DEPTH = 4
DEC_BATCH = 32
DEC_SEQ = 8
PAST_LEN = 8192
PAGE_SIZE = 128

N_A_LAYERS = DEPTH // 2
N_B_LAYERS = DEPTH - N_A_LAYERS
RET_HEADS = 4
RET_DK = D_MODEL // RET_HEADS
RET_DV = 2 * D_MODEL // RET_HEADS
RET_CHUNK = 128
N_HEADS = 16
N_KV = 4
HEAD_DIM = 64
GROUP = N_HEADS // N_KV
CMP_LEN = 32
CMP_STRIDE = 16
CMP_HID = 128
SEL_BLOCK = 64
SEL_TOPK = 16
WINDOW = 512
Q_BLOCK = 128
D_FF = 2816
CONV_W = 3
ROPE_THETA = 10000.0
EPS = 1e-6
NEG_INF = -1e30
TINY = 1e-30
SEL_FORCE = 1e6
SEL_NEG = -1e6

kernel_name = 'yoco_retnet_nsa_convffn_step'


def rmsnorm(x, g):
    xf = x.astype(jnp.float32)
    y = xf * lax.rsqrt(jnp.mean(xf * xf, axis=-1, keepdims=True) + EPS)
    return (y * g.astype(jnp.float32)).astype(x.dtype)


def head_rms(x):
    xf = x.astype(jnp.float32)
    return (xf * lax.rsqrt(jnp.mean(xf * xf, axis=-1, keepdims=True) + EPS)).astype(x.dtype)


def rope(x, pos):
    half = x.shape[-1] // 2
    inv = jnp.exp(-math.log(ROPE_THETA) * jnp.arange(half, dtype=jnp.float32) / half)
    ang = pos.astype(jnp.float32)[:, None] * inv[None, :]
    cos = jnp.cos(ang)[:, None, :]
    sin = jnp.sin(ang)[:, None, :]
    xf = x.astype(jnp.float32)
    x1, x2 = xf[..., :half], xf[..., half:]
    return jnp.concatenate([x1 * cos - x2 * sin, x2 * cos + x1 * sin], axis=-1).astype(x.dtype)


def masked_softmax(s, mask):
    s = jnp.where(mask, s, NEG_INF)
    m = jnp.max(s, axis=-1, keepdims=True)
    e = jnp.where(mask, jnp.exp(s - m), 0.0)
    return e / jnp.maximum(jnp.sum(e, axis=-1, keepdims=True), TINY)


def retention_chunk(S, q, k, v, log_gamma):
    L = q.shape[1]
    idx = jnp.arange(L, dtype=jnp.float32)
    diff = idx[:, None] - idx[None, :]
    decay = jnp.where(diff >= 0, jnp.exp(jnp.maximum(diff, 0.0)[None] * log_gamma[:, None, None]), 0.0)
    qf, kf, vf = q.astype(jnp.float32), k.astype(jnp.float32), v.astype(jnp.float32)
    scores = jnp.einsum('blhd,bmhd->bhlm', qf, kf) * decay[None]
    o = jnp.einsum('bhlm,bmhe->blhe', scores, vf)
    q_dec = jnp.exp((idx + 1.0)[:, None] * log_gamma[None, :])
    o = o + jnp.einsum('blhd,bhde->blhe', qf, S) * q_dec[None, :, :, None]
    k_dec = jnp.exp((L - 1.0 - idx)[:, None] * log_gamma[None, :])
    S = S * jnp.exp(L * log_gamma)[None, :, None, None] + jnp.einsum('blhd,blhe->bhde', kf * k_dec[None, :, :, None], vf)
    return S, o


def retention_mixer(h, S0, pos, w_in, w_out):
    B, T, _ = h.shape
    proj = h @ w_in
    q = proj[..., :D_MODEL].reshape(B, T, RET_HEADS, RET_DK)
    k = proj[..., D_MODEL:2 * D_MODEL].reshape(B, T, RET_HEADS, RET_DK)
    v = proj[..., 2 * D_MODEL:4 * D_MODEL].reshape(B, T, RET_HEADS, RET_DV)
    g = proj[..., 4 * D_MODEL:]
    q = rope(q, pos)
    k = rope(k, pos) * (RET_DK ** -0.5)
    log_gamma = jnp.log1p(-jnp.exp2(-5.0 - jnp.arange(RET_HEADS, dtype=jnp.float32)))
    chunk = RET_CHUNK if T % RET_CHUNK == 0 else T
    n = T // chunk

    def to_chunks(a):
        return a.reshape(B, n, chunk, RET_HEADS, a.shape[-1]).swapaxes(0, 1)

    def step(S, inp):
        return retention_chunk(S, inp[0], inp[1], inp[2], log_gamma)

    S, o = lax.scan(step, S0.astype(jnp.float32), (to_chunks(q), to_chunks(k), to_chunks(v)))
    o = o.swapaxes(0, 1).reshape(B, T, RET_HEADS, RET_DV)
    o = head_rms(o).reshape(B, T, RET_HEADS * RET_DV).astype(h.dtype)
    return (o * jax.nn.silu(g)) @ w_out, S


def conv_ffn(h, buf, w_in, conv_w, conv_b, w_out):
    T = h.shape[1]
    proj = h @ w_in
    u, gate = proj[..., :D_FF], proj[..., D_FF:]
    up = jnp.concatenate([buf.astype(u.dtype), u], axis=1)
    c = conv_b + conv_w[0] * up[:, 0:T]
    for j in range(1, CONV_W):
        c = c + conv_w[j] * up[:, j:j + T]
    out = (jax.nn.gelu(c) * gate) @ w_out
    return out, up[:, up.shape[1] - (CONV_W - 1):]


def compress(rows, cmp_pos, cmp_w1, cmp_w2, k_gain):
    B, L = rows.shape[0], rows.shape[1]
    R = CMP_LEN // CMP_STRIDE
    n_sub = L // CMP_STRIDE
    n_cmp = n_sub - R + 1
    sub = rows[:, :n_sub * CMP_STRIDE].reshape(B, n_sub, CMP_STRIDE, 2, N_KV, HEAD_DIM)
    w1 = cmp_w1.reshape(2, R, CMP_STRIDE, HEAD_DIM, CMP_HID)
    part = jnp.einsum('bnsckd,crsdh->rbnckh', sub, w1)
    hid = part[0][:, 0:n_cmp]
    for r in range(1, R):
        hid = hid + part[r][:, r:r + n_cmp]
    pos_bias = jnp.einsum('cld,cldh->ch', cmp_pos, cmp_w1.reshape(2, CMP_LEN, HEAD_DIM, CMP_HID))
    hid = jax.nn.gelu(hid + pos_bias[:, None, :].astype(hid.dtype))
    out = jnp.einsum('bnckh,chd->bnckd', hid, cmp_w2)
    return head_rms(out[:, :, 0]) * k_gain, out[:, :, 1]


def build_shared(x, pos, past_len, past_cmp, past_sel, past_win,
                 kv_norm, kv_w, kv_knorm, cmp_pos, cmp_w1, cmp_w2):
    B, T, _ = x.shape
    kv = (rmsnorm(x, kv_norm) @ kv_w).reshape(B, T, 6, N_KV, HEAD_DIM)
    new_cmp = kv[:, :, 0:2]
    new_sel = jnp.stack([rope(head_rms(kv[:, :, 2]) * kv_knorm[1], pos), kv[:, :, 3]], axis=2)
    new_win = jnp.stack([rope(head_rms(kv[:, :, 4]) * kv_knorm[2], pos), kv[:, :, 5]], axis=2)
    all_cmp = jnp.concatenate([past_cmp.astype(new_cmp.dtype), new_cmp], axis=1)
    cmp_k, cmp_v = compress(all_cmp, cmp_pos, cmp_w1, cmp_w2, kv_knorm[0])
    all_sel = jnp.concatenate([past_sel.astype(new_sel.dtype), new_sel], axis=1)
    L = all_sel.shape[1]
    pad = (-L) % SEL_BLOCK
    all_sel = jnp.pad(all_sel, ((0, 0), (0, pad), (0, 0), (0, 0), (0, 0)))
    blocks = all_sel.reshape(B, (L + pad) // SEL_BLOCK, SEL_BLOCK, 2, N_KV, HEAD_DIM).transpose(3, 0, 4, 1, 2, 5)
    all_win = jnp.concatenate([past_win.astype(new_win.dtype), new_win], axis=1)
    win_state = all_win[:, all_win.shape[1] - min(WINDOW, past_len + T):]
    win_keys = jnp.pad(all_win, ((0, 0), (WINDOW - past_win.shape[1], 0), (0, 0), (0, 0), (0, 0)))
    shared = (cmp_k, cmp_v, blocks[0], blocks[1], win_keys[:, :, 0], win_keys[:, :, 1])
    return shared, new_cmp, new_sel, win_state


def nsa_attend(q, gates, qpos, cmp_k, cmp_v, sel_k, sel_v, win_k, win_v, win_pos):
    B, Q = q.shape[0], q.shape[1]
    scale = HEAD_DIM ** -0.5
    q_c = q.reshape(B, Q, N_KV, GROUP, HEAD_DIM)
    q_r = rope(q, qpos).reshape(B, Q, N_KV, GROUP, HEAD_DIM)
    n_cmp = cmp_k.shape[1]
    c_start = jnp.arange(n_cmp) * CMP_STRIDE
    m_c = (c_start + CMP_LEN - 1)[None, :] <= qpos[:, None]
    s_c = jnp.einsum('bqkgd,bnkd->bkgqn', q_c, cmp_k).astype(jnp.float32) * scale
    p_c = masked_softmax(s_c, m_c)
    o_c = jnp.einsum('bkgqn,bnkd->bqkgd', p_c.astype(cmp_v.dtype), cmp_v)
    n_sel = sel_k.shape[2]
    s_start = jnp.arange(n_sel) * SEL_BLOCK
    overlap = (jnp.minimum(c_start[:, None] + CMP_LEN, s_start[None, :] + SEL_BLOCK)
               - jnp.maximum(c_start[:, None], s_start[None, :]))
    w_map = jnp.maximum(overlap, 0).astype(jnp.float32) / CMP_LEN
    imp = jnp.einsum('bkgqn,ns->bkqs', p_c, w_map)
    blk = jnp.arange(n_sel)[None, :]
    cur = (qpos // SEL_BLOCK)[:, None]
    forced = (blk == 0) | (blk == cur) | (blk == cur - 1)
    imp = jnp.where(forced, SEL_FORCE, jnp.where(s_start[None, :] <= qpos[:, None], imp, SEL_NEG))
    top_v, top_i = lax.top_k(imp, min(SEL_TOPK, n_sel))
    n_top = top_i.shape[-1]
    bi = jnp.arange(B)[:, None, None, None]
    ki = jnp.arange(N_KV)[None, :, None, None]
    g_k = sel_k[bi, ki, top_i]
    g_v = sel_v[bi, ki, top_i]
    tok = top_i[..., None] * SEL_BLOCK + jnp.arange(SEL_BLOCK)
    m_s = (top_v > 0.5 * SEL_NEG)[..., None] & (tok <= qpos[None, None, :, None, None])
    s_s = jnp.einsum('bqkgd,bkqtsd->bkgqts', q_r, g_k).astype(jnp.float32) * scale
    p_s = masked_softmax(s_s.reshape(B, N_KV, GROUP, Q, n_top * SEL_BLOCK),
                         m_s.reshape(B, N_KV, 1, Q, n_top * SEL_BLOCK))
    o_s = jnp.einsum('bkgqts,bkqtsd->bqkgd', p_s.reshape(s_s.shape).astype(sel_v.dtype), g_v)
    m_w = ((win_pos[None, :] <= qpos[:, None]) & (win_pos[None, :] > qpos[:, None] - WINDOW)
           & (win_pos[None, :] >= 0))
    s_w = jnp.einsum('bqkgd,bwkd->bkgqw', q_r, win_k).astype(jnp.float32) * scale
    p_w = masked_softmax(s_w, m_w)
    o_w = jnp.einsum('bkgqw,bwkd->bqkgd', p_w.astype(win_v.dtype), win_v)
    g = gates.reshape(B, Q, N_KV, GROUP, 3).astype(o_c.dtype)
    o = g[..., 0:1] * o_c + g[..., 1:2] * o_s + g[..., 2:3] * o_w
    return o.reshape(B, Q, N_HEADS * HEAD_DIM)


def nsa_mixer(h, past_len, shared, w_qg, q_gain, w_o):
    cmp_k, cmp_v, sel_k, sel_v, win_k, win_v = shared
    B, T, _ = h.shape
    proj = h @ w_qg
    q = head_rms(proj[..., :N_HEADS * HEAD_DIM].reshape(B, T, N_HEADS, HEAD_DIM)) * q_gain
    gates = jax.nn.sigmoid(proj[..., N_HEADS * HEAD_DIM:].astype(jnp.float32)).reshape(B, T, N_HEADS, 3)
    qb = Q_BLOCK if T % Q_BLOCK == 0 else T
    n_qb = T // qb
    q_blk = q.reshape(B, n_qb, qb, N_HEADS, HEAD_DIM).swapaxes(0, 1)
    g_blk = gates.reshape(B, n_qb, qb, N_HEADS, 3).swapaxes(0, 1)

    def one_block(args):
        i, qi, gi = args
        qs = i * qb
        qpos = past_len + qs + jnp.arange(qb)
        wk = lax.dynamic_slice_in_dim(win_k, qs, WINDOW + qb, axis=1)
        wv = lax.dynamic_slice_in_dim(win_v, qs, WINDOW + qb, axis=1)
        wpos = past_len - WINDOW + qs + jnp.arange(WINDOW + qb)
        return nsa_attend(qi, gi, qpos, cmp_k, cmp_v, sel_k, sel_v, wk, wv, wpos)

    o = lax.map(one_block, (jnp.arange(n_qb), q_blk, g_blk))
    o = o.swapaxes(0, 1).reshape(B, T, N_HEADS * HEAD_DIM)
    return o.astype(h.dtype) @ w_o


def run_trunk(x, past_len, ret_s0, conv0, past_cmp, past_sel, past_win, params):
    (norm_mix, norm_ffn, ret_w_in, ret_w_out, ffn_w_in, ffn_conv_w, ffn_conv_b, ffn_w_out,
     kv_norm, kv_w, kv_knorm, cmp_pos, cmp_w1, cmp_w2, nsa_w_qg, nsa_qnorm, nsa_w_o) = params
    T = x.shape[1]
    pos = past_len + jnp.arange(T)
    ret_states = []
    conv_states = []
    for layer in range(DEPTH):
        if layer == N_A_LAYERS:
            shared, new_cmp, new_sel, new_win = build_shared(
                x, pos, past_len, past_cmp, past_sel, past_win,
                kv_norm, kv_w, kv_knorm, cmp_pos, cmp_w1, cmp_w2)
        h = rmsnorm(x, norm_mix[layer])
        if layer < N_A_LAYERS:
            mix, s_new = retention_mixer(h, ret_s0[layer], pos, ret_w_in[layer], ret_w_out[layer])
            ret_states.append(s_new)
        else:
            j = layer - N_A_LAYERS
            mix = nsa_mixer(h, past_len, shared, nsa_w_qg[j], nsa_qnorm[j], nsa_w_o[j])
        x = x + mix
        f, buf = conv_ffn(rmsnorm(x, norm_ffn[layer]), conv0[layer], ffn_w_in[layer],
                          ffn_conv_w[layer], ffn_conv_b[layer], ffn_w_out[layer])
        conv_states.append(buf)
        x = x + f
    return x, jnp.stack(ret_states), jnp.stack(conv_states), new_cmp, new_sel, new_win


def setup_inputs(seed: int = 0) -> dict:
    key = jax.random.key(seed)
    ks = jax.random.split(key, 26)
    f32 = jnp.float32

    def nrm(k, shape, scale=1.0):
        return jax.random.normal(k, shape, f32) * scale

    n_pages = PAST_LEN // PAGE_SIZE
    n_used = DEC_BATCH * n_pages
    n_pool = n_used + max(n_used // 4, 1)
    win_buf = min(WINDOW, PAST_LEN)
    page_table = jax.random.permutation(ks[0], n_pool)[:n_used].reshape(DEC_BATCH, n_pages).astype(jnp.int32)
    qg_width = N_HEADS * HEAD_DIM + 3 * N_HEADS
    return {
        'x_prompt': nrm(ks[1], (BATCH, SEQ, D_MODEL)),
        'x_sample': nrm(ks[2], (DEC_BATCH, DEC_SEQ, D_MODEL)),
        'cache_cmp_kv': nrm(ks[3], (n_pool, PAGE_SIZE, 2, N_KV, HEAD_DIM)),
        'cache_sel_kv': nrm(ks[4], (n_pool, PAGE_SIZE, 2, N_KV, HEAD_DIM)),
        'cache_win_kv': nrm(ks[5], (DEC_BATCH, win_buf, 2, N_KV, HEAD_DIM)),
        'state_ret': nrm(ks[6], (N_A_LAYERS, DEC_BATCH, RET_HEADS, RET_DK, RET_DV), 0.5),
        'state_conv': nrm(ks[7], (DEPTH, DEC_BATCH, CONV_W - 1, D_FF)),
        'page_table': page_table,
        'norm_mix': 1.0 + nrm(ks[8], (DEPTH, D_MODEL), 0.01),
        'norm_ffn': 1.0 + nrm(ks[9], (DEPTH, D_MODEL), 0.01),
        'ret_w_in': nrm(ks[10], (N_A_LAYERS, D_MODEL, 6 * D_MODEL), D_MODEL ** -0.5),
        'ret_w_out': nrm(ks[11], (N_A_LAYERS, 2 * D_MODEL, D_MODEL), (2 * D_MODEL) ** -0.5),
        'ffn_w_in': nrm(ks[12], (DEPTH, D_MODEL, 2 * D_FF), D_MODEL ** -0.5),
        'ffn_conv_w': nrm(ks[13], (DEPTH, CONV_W, D_FF), CONV_W ** -0.5),
        'ffn_conv_b': nrm(ks[14], (DEPTH, D_FF), 0.01),
        'ffn_w_out': nrm(ks[15], (DEPTH, D_FF, D_MODEL), D_FF ** -0.5),
        'kv_norm': 1.0 + nrm(ks[16], (D_MODEL,), 0.01),
        'kv_w': nrm(ks[17], (D_MODEL, 6 * N_KV * HEAD_DIM), D_MODEL ** -0.5),
        'kv_knorm': 1.0 + nrm(ks[18], (3, HEAD_DIM), 0.01),
        'cmp_pos': nrm(ks[19], (2, CMP_LEN, HEAD_DIM), 0.5),
        'cmp_w1': nrm(ks[20], (2, CMP_LEN * HEAD_DIM, CMP_HID), (CMP_LEN * HEAD_DIM) ** -0.5),
        'cmp_w2': nrm(ks[21], (2, CMP_HID, HEAD_DIM), CMP_HID ** -0.5),
        'nsa_w_qg': nrm(ks[22], (N_B_LAYERS, D_MODEL, qg_width), D_MODEL ** -0.5),
        'nsa_qnorm': 1.0 + nrm(ks[23], (N_B_LAYERS, HEAD_DIM), 0.01),
        'nsa_w_o': nrm(ks[24], (N_B_LAYERS, N_HEADS * HEAD_DIM, D_MODEL), (N_HEADS * HEAD_DIM) ** -0.5),
    }


def reference(x_prompt, x_sample, cache_cmp_kv, cache_sel_kv, cache_win_kv, state_ret, state_conv,
              page_table, norm_mix, norm_ffn, ret_w_in, ret_w_out, ffn_w_in, ffn_conv_w, ffn_conv_b,
              ffn_w_out, kv_norm, kv_w, kv_knorm, cmp_pos, cmp_w1, cmp_w2, nsa_w_qg, nsa_qnorm, nsa_w_o):
    params = (norm_mix, norm_ffn, ret_w_in, ret_w_out, ffn_w_in, ffn_conv_w, ffn_conv_b, ffn_w_out,
              kv_norm, kv_w, kv_knorm, cmp_pos, cmp_w1, cmp_w2, nsa_w_qg, nsa_qnorm, nsa_w_o)
    B = x_prompt.shape[0]
    dt = x_prompt.dtype
    zero_ret = jnp.zeros((N_A_LAYERS, B, RET_HEADS, RET_DK, RET_DV), jnp.float32)
    zero_conv = jnp.zeros((DEPTH, B, CONV_W - 1, D_FF), dt)
    empty = jnp.zeros((B, 0, 2, N_KV, HEAD_DIM), dt)
    y_p, ret_p, conv_p, cmp_p, sel_p, win_p = run_trunk(
        x_prompt, 0, zero_ret, zero_conv, empty, empty, empty, params)
    db = x_sample.shape[0]
    past_len = page_table.shape[1] * PAGE_SIZE
    past_cmp = cache_cmp_kv[page_table].reshape(db, past_len, 2, N_KV, HEAD_DIM)
    past_sel = cache_sel_kv[page_table].reshape(db, past_len, 2, N_KV, HEAD_DIM)
    y_s, ret_s, conv_s, cmp_s, sel_s, win_s = run_trunk(
        x_sample, past_len, state_ret, state_conv, past_cmp, past_sel, cache_win_kv, params)
    return (y_p, y_s, ret_p, ret_s, conv_p, conv_s, cmp_p, cmp_s, sel_p, sel_s, win_p, win_s)
```

```python
import contextlib
import math
import numpy as np
import ml_dtypes
import concourse.bass as bass
import concourse.mybir as mybir
from concourse.bass_utils import run_bass_kernel_spmd

F32 = mybir.dt.float32
BF16 = mybir.dt.bfloat16
I32 = mybir.dt.int32
AF = mybir.ActivationFunctionType
ALU = mybir.AluOpType
AX = mybir.AxisListType

D = 1024
SEG = 2048
NS = 32
NT = SEG + NS
TT = 512
DFF = 2816
NFC = 22
EPS = 1e-6
GAM = [1.0 - 2.0 ** (-5.0 - h) for h in range(4)]
LOGG = [math.log1p(-(2.0 ** (-5.0 - h))) for h in range(4)]
SEMW = 30000
SIMMODE = False


class Prog:
    def __init__(self, nc, es):
        self.nc = nc
        self.es = es
        self.engs = ["pe", "act", "dve", "pool", "sp"]
        self.ops = {e: [] for e in self.engs}
        self.cnt = {e: 0 for e in self.engs}
        self.psems = {e: [] for e in self.engs}
        self.waited_c = {e: {x: 0 for x in self.engs} for e in self.engs}
        self.waited_d = {e: {} for e in self.engs}
        self.bufs = {}
        self.NDS = 12
        self.dq = ["sp", "pool", "act"]
        self.dsems = {q: [es.enter_context(nc.semaphore(f"d_{q}_{i}")) for i in range(self.NDS)] for q in self.dq}
        self.dval = {(q, i): 0 for q in self.dq for i in range(self.NDS)}
        self.dlast = {(q, i): None for q in self.dq for i in range(self.NDS)}
        self.dcnt = {q: 0 for q in self.dq}
        self.nps = 0

    def _psem(self, e, n):
        w = (n - 1) // SEMW
        while len(self.psems[e]) <= w:
            self.psems[e].append(self.es.enter_context(self.nc.semaphore(f"p_{e}_{len(self.psems[e])}")))
        return self.psems[e][w], (n - 1) % SEMW + 1

    def _need(self, e, tok, waits):
        if tok is None:
            return
        if tok[0] == "c":
            _, x, n = tok
            if x == "pe" and e == "pe":
                return
            if self.waited_c[e][x] >= n:
                return
            self.waited_c[e][x] = n
            waits.append(self._psem(x, n))
        else:
            _, q, i, v = tok
            if self.waited_d[e].get((q, i), 0) >= v:
                return
            self.waited_d[e][(q, i)] = v
            waits.append((self.dsems[q][i], v))

    def _deps(self, e, reads, writes, waits):
        for k in reads:
            b = self.bufs.get(k)
            if b:
                for t in b[0]:
                    self._need(e, t, waits)
        for k in writes:
            b = self.bufs.get(k)
            if b:
                for t in b[0]:
                    self._need(e, t, waits)
                for t in b[1].values():
                    self._need(e, t, waits)

    def _upd(self, tok, reads, writes, dma=False):
        rk = (tok[0], tok[1]) if tok[0] == "c" else (tok[0], tok[1], tok[2])
        for k in reads:
            b = self.bufs.setdefault(k, [[], {}])
            b[1][rk] = tok
        for k in writes:
            self.bufs[k] = [[tok], {}]

    def op(self, e, fn, reads=(), writes=()):
        waits = []
        self._deps(e, reads, writes, waits)
        n = self.cnt[e] + 1
        self.cnt[e] = n
        tok = ("c", e, n)
        self.ops[e].append((waits, fn, (self._psem(e, n)[0], 1)))
        self._upd(tok, reads, writes)
        return tok

    def dma(self, q, fn, reads=(), writes=()):
        waits = []
        self._deps(q, reads, writes, waits)
        i = self.dcnt[q] % self.NDS
        self.dcnt[q] += 1
        self._need(q, self.dlast[(q, i)], waits)
        v = self.dval[(q, i)] + 16
        self.dval[(q, i)] = v
        tok = ("d", q, i, v)
        self.dlast[(q, i)] = tok
        self.ops[q].append((waits, fn, (self.dsems[q][i], 16)))
        self._upd(tok, reads, writes)
        return tok

    def special(self, q, fn, sem, reads=(), writes=()):
        waits = []
        self._deps(q, reads, writes, waits)
        self.ops[q].append((waits, fn, (sem, 1)))
        tok = ("s", sem)
        return tok

    def wait_special(self, e, sem):
        self.ops[e].append(([(sem, 1)], None, None))

    def barrier(self):
        for e in self.engs:
            waits = []
            for x in self.engs:
                if self.cnt[x] > 0:
                    self._need(e, ("c", x, self.cnt[x]), waits)
            for q in self.dq:
                for i in range(self.NDS):
                    self._need(e, self.dlast[(q, i)], waits)
            self.ops[e].append((waits, None, None))
        self.bufs = {}

    def finish(self):
        waits = []
        for q in self.dq:
            for i in range(self.NDS):
                self._need("sp", self.dlast[(q, i)], waits)
        self.ops["sp"].append((waits, None, None))
        self.flush()

    def flush(self):
        nc = self.nc
        ops = self.ops
        self.ops = {e: [] for e in self.engs}

        def emit(E, e):
            for waits, fn, inc in ops[E]:
                for s, v in waits:
                    e.wait_ge(s, v)
                if fn is not None:
                    ins = fn(e)
                    ins.then_inc(inc[0], inc[1])

        with nc.Block() as block:
            @block.tensor
            def _(e):
                emit("pe", e)

            @block.scalar
            def _(e):
                emit("act", e)

            @block.vector
            def _(e):
                emit("dve", e)

            @block.gpsimd
            def _(e):
                emit("pool", e)

            @block.sync
            def _(e):
                emit("sp", e)


CST = {}
_off = 0
for _n, _w in [("decT", 512), ("qdec", 512), ("kdec128", 4), ("kdec8", 4), ("ident", 128), ("ones", 128),
               ("nmix", 32), ("nffn", 32), ("kvn", 8), ("cw", 4 * 3 * NFC), ("cb", 4 * NFC),
               ("scoef", 16), ("hcoef", 4), ("eps", 1), ("gk", 3 * 64), ("gq", 2 * 64)]:
    CST[_n] = (_off, _w)
    _off += _w
NCST = _off


CSTB = {}
_off = 0
for _n, _w in [("iota", 1), ("qrow", 128), ("tri_le", 128), ("tri_gt", 128), ("hi", 1), ("exrow_p", 128), ("first_p", 128),
               ("exrow_s", 128), ("first_s", 128), ("base_p", 4), ("base_s", 4), ("extile", 64), ("wmap", 512), ("gk0col", 1),
               ("Dtab", 128)]:
    CSTB[_n] = (_off, _w)
    _off += _w
NCB = _off
NV = 5


def host_tables_b(c, inp):
    j = c % 4
    k3 = 3 - j
    t = np.zeros((128, NCB), np.float32)

    def put(name, arr):
        o, w = CSTB[name]
        t[:, o:o + w] = np.asarray(arr, np.float32).reshape(128, w)

    p = np.arange(128)
    put("iota", p[:, None])
    put("qrow", np.broadcast_to(np.arange(128)[None, :], (128, 128)))
    put("tri_le", (p[:, None] <= p[None, :]))
    put("tri_gt", (p[:, None] > p[None, :]))
    put("hi", (p >= 64)[:, None])
    blk = np.arange(128)
    put("exrow_p", np.broadcast_to((blk >= 32 * k3)[None, :], (128, 128)))
    put("first_p", np.broadcast_to((blk == 32 * k3)[None, :], (128, 128)))
    put("exrow_s", np.ones((128, 128)))
    put("first_s", np.broadcast_to((blk == 0)[None, :], (128, 128)))
    n = (np.arange(4)[None, :] * 128 + p[:, None])
    ex = (16 * n >= 2048 * k3) & (n <= 510)
    put("base_p", np.where(ex, 16.0 * n + 31 - 6144, 1e9))
    put("base_s", np.where(n <= 510, -1.0, 1e9))
    put("extile", np.broadcast_to((np.arange(64) >= 16 * k3)[None, :], (128, 64)))
    sb_ = np.arange(128)[None, None, :]
    nn = n[:, :, None]
    ov = np.maximum(0, np.minimum(16 * nn + 32, 64 * sb_ + 64) - np.maximum(16 * nn, 64 * sb_)) / 32.0
    put("wmap", ov)
    put("gk0col", inp["kv_knorm"][0][p % 64][:, None])
    put("Dtab", p[:, None] - (p[None, :] >= 64))
    idx = np.zeros((128, 64), np.int32)
    for r in range(64):
        seg = r // 16 - k3
        tl = r % 16
        idx[:, r] = p if seg < 0 else seg * 256 + (tl % 2) * 128 + p
    return t, idx


def host_tables(c, inp):
    j = c % 4
    cst = np.zeros((128, NCST), np.float32)

    def put(name, arr):
        o, w = CST[name]
        cst[:, o:o + w] = np.asarray(arr, np.float32).reshape(128, w)

    m = np.arange(128)[:, None]
    l = np.arange(128)[None, :]
    decT = np.zeros((128, 4, 128), np.float64)
    qdec = np.zeros((128, 4, 128), np.float64)
    kd128 = np.zeros((128, 4), np.float64)
    kd8 = np.zeros((128, 4), np.float64)
    for h in range(4):
        decT[:, h, :] = np.where(l >= m, np.exp(np.maximum(l - m, 0) * LOGG[h]), 0.0) / 16.0
        qdec[:, h, :] = np.exp((l + 1.0) * LOGG[h])
        kd128[:, h] = np.exp((127.0 - np.arange(128)) * LOGG[h]) / 16.0
        kd8[:, h] = np.exp((7.0 - np.minimum(np.arange(128), 7)) * LOGG[h]) / 16.0
    put("decT", decT)
    put("qdec", qdec)
    put("kdec128", kd128)
    put("kdec8", kd8)
    put("eps", np.full((128, 1), EPS))
    put("ident", np.eye(128))
    put("ones", np.ones((128, 128)))
    put("nmix", inp["norm_mix"].reshape(4, 8, 128).transpose(2, 0, 1))
    put("nffn", inp["norm_ffn"].reshape(4, 8, 128).transpose(2, 0, 1))
    put("kvn", inp["kv_norm"].reshape(8, 128).T)
    put("cw", inp["ffn_conv_w"].reshape(4, 3, NFC, 128).transpose(3, 0, 1, 2))
    put("cb", inp["ffn_conv_b"].reshape(4, NFC, 128).transpose(2, 0, 1))
    sc = np.zeros((4, 4), np.float64)
    for r in range(4):
        if r < j:
            for h in range(4):
                sc[r, h] = math.exp(LOGG[h] * SEG * (j - 1 - r))
    put("scoef", np.broadcast_to(sc.reshape(1, 16), (128, 16)))
    hc = np.zeros(4)
    if j > 0:
        hc[j - 1] = 1.0
    put("hcoef", np.broadcast_to(hc.reshape(1, 4), (128, 4)))
    put("gk", np.broadcast_to(inp["kv_knorm"].reshape(1, 192), (128, 192)))
    put("gq", np.broadcast_to(inp["nsa_qnorm"].reshape(1, 128), (128, 128)))
    pos = np.concatenate([SEG * j + np.arange(SEG), 8192 + (np.arange(NS) % 8)]).astype(np.float32)
    invA = np.exp(-math.log(10000.0) * np.arange(128, dtype=np.float32) / 128).astype(np.float32)
    angA = (invA[:, None] * pos[None, :]).astype(np.float32)
    ropeA = np.stack([np.cos(angA), np.sin(angA)], 1).astype(np.float32)
    invB = np.exp(-math.log(10000.0) * np.arange(32, dtype=np.float32) / 32).astype(np.float32)
    angB = (pos[:, None] * invB[None, :]).astype(np.float32)
    ropeB = np.concatenate([np.cos(angB), np.sin(angB)], 1).astype(np.float32)
    return cst, ropeA, ropeB


def build_program(stage=99):
    nc = bass.Bass("TRN2", target_bir_lowering=False)
    es = contextlib.ExitStack()
    P = Prog(nc, es)

    def din(name, shape, dt=F32):
        return nc.dram_tensor(name, list(shape), dt, kind="ExternalInput").ap()

    def dout(name, shape, dt=F32):
        return nc.dram_tensor(name, list(shape), dt, kind="ExternalOutput").ap()

    def dscr(name, shape, dt):
        return nc.dram_tensor(name, list(shape), dt).ap()

    def sb(name, shape, dt):
        return es.enter_context(nc.sbuf_tensor("sb_" + name, list(shape), dt))

    def MM(out, lhsT, rhs, start, stop, r, w):
        P.op("pe", lambda e: e.matmul(out, lhsT, rhs, start=start, stop=stop), r, w)

    def TR(out, in_, ident, r, w):
        P.op("pe", lambda e: e.transpose(out, in_, ident), r, w)

    def ACT(out, in_, func, r, w, bias=None, scale=None):
        kw = {}
        if bias is not None:
            kw["bias"] = bias
        if scale is not None:
            kw["scale"] = scale
        P.op("act", lambda e: e.activation(out, in_, func, **kw), r, w)

    def TTn(eng, out, a, b, op, r, w):
        P.op(eng, lambda e: e.tensor_tensor(out, a, b, op), r, w)

    def TS(eng, out, a, s1, s2, op0, op1, r, w):
        if op1 is None:
            P.op(eng, lambda e: e.tensor_scalar(out, a, s1, None, op0=op0), r, w)
        else:
            P.op(eng, lambda e: e.tensor_scalar(out, a, s1, s2, op0=op0, op1=op1), r, w)

    def STT(eng, out, in0, scalar, in1, op0, op1, r, w):
        P.op(eng, lambda e: e.scalar_tensor_tensor(out=out, in0=in0, scalar=scalar, in1=in1, op0=op0, op1=op1), r, w)

    def RCP(out, in_, r, w):
        P.op("dve", lambda e: e.reciprocal(out, in_), r, w)

    def CP(eng, out, in_, r, w):
        P.op(eng, lambda e: e.tensor_copy(out, in_), r, w)

    def DMA(q, out, in_, r=(), w=()):
        P.dma(q, lambda e: e.dma_start(out=out, in_=in_), r, w)

    xp = din("xp", [SEG, D])
    xs = din("xs", [NS, D])
    cst_d = din("cst", [128, NCST])
    ropeA_d = din("ropeA", [128, 2, NT])
    ropeB_d = din("ropeB", [NT, 64])
    w_ret_in = din("ret_w_in", [2, D, 6 * D])
    w_ret_out = din("ret_w_out", [2, 2 * D, D])
    w_ffn_in = din("ffn_w_in", [4, D, 2 * DFF])
    w_ffn_out = din("ffn_w_out", [4, DFF, D])
    w_kv = din("kv_w", [D, 1536])
    st_ret = din("state_ret", [2, 4, 4, 256, 512])
    st_conv = din("state_conv", [4, 128, NFC, 4, 2])
    o_ret_p = dout("o_ret_p", [2, 4, 256, 512])
    o_ret_s = dout("o_ret_s", [2, 4, 4, 256, 512])
    o_conv_p = dout("o_conv_p", [4, 128, NFC, 2])
    o_conv_s = dout("o_conv_s", [4, 128, NFC, 4, 2])
    o_cmp = dout("o_cmp", [NT, 512])
    o_sel = dout("o_sel", [NT, 512])
    o_win = dout("o_win", [NT, 512])
    o_y = dout("o_y", [NT, D])
    xT_d = dscr("xT_d", [8, 128, NT], F32)
    wb_ret_in = dscr("wb_ret_in", [2, D, 6 * D], BF16)
    wb_ret_out = dscr("wb_ret_out", [2, 2 * D, D], BF16)
    wb_ffn_in = dscr("wb_ffn_in", [4, D, 2 * DFF], BF16)
    wb_ffn_out = dscr("wb_ffn_out", [4, DFF, D], BF16)
    wb_kv = dscr("wb_kv", [D, 1536], BF16)
    sl_in = [[dscr(f"sl_in{l}_{hf}", [128, 2048], F32) for hf in range(2)] for l in range(2)]
    sl_all = [[dscr(f"sl_all{l}_{hf}", [4 * 128, 2048], F32) for hf in range(2)] for l in range(2)]
    hx_in = [dscr(f"hx_in{l}", [128, 16], F32) for l in range(4)]
    hx_all = [dscr(f"hx_all{l}", [4 * 128, 16], F32) for l in range(4)]
    kv_loc = dscr("kv_loc", [NT, 1536], BF16)

    NPOOLR = 2560 * 128
    ccmp_d = din("cache_cmp", [NPOOLR, 512])
    csel_d = din("cache_sel", [NPOOLR, 512])
    cwin_d = din("cache_win", [4, 512, 512])
    ptab_d = din("ptab", [4, 64], I32)
    w1_d = din("cmp_w1", [2, 2048, 128])
    w2_d = din("cmp_w2", [2, 128, 64])
    posT_d = din("posT", [64, 64])
    w_qg = din("nsa_w_qg", [2, D, 1072])
    w_o = din("nsa_w_o", [2, D, D])
    cstB_d = din("cstB", [128, NCB])
    idxp_d = din("idxp", [128, 64], I32)
    o_wins = dout("o_wins", [4, 512, 512])
    wb_qg = dscr("wb_qg", [2, D, 1072], BF16)
    wb_o = dscr("wb_o", [2, D, D], BF16)
    kv_all = [dscr(f"kv_all{i}", [1024, 1536], BF16) for i in range(8)]
    selK_d = [dscr(f"selK{v}", [2, 128, 8320], BF16) for v in range(NV)]
    selV_d = [dscr(f"selV{v}", [65, 128, 260], BF16) for v in range(NV)]
    winK_d = [dscr(f"winK{v}", [2, 128, 8192 if v == 0 else 640], BF16) for v in range(NV)]
    winV_d = [dscr(f"winV{v}", [64 if v == 0 else 5, 128, 260], BF16) for v in range(NV)]
    cmpK_d = [dscr(f"cmpK{v}", [2, 128, 512], BF16) for v in range(NV)]
    cmpV_d = [dscr(f"cmpV{v}", [4, 128, 260], BF16) for v in range(NV)]

    MT = 256
    esA = contextlib.ExitStack()

    def sbA(name, shape, dt):
        return esA.enter_context(nc.sbuf_tensor("sbA_" + name, list(shape), dt))
    cst = sb("cst", [128, NCST], F32)
    ident_bf = sb("ident_bf", [128, 128], BF16)
    ones_bf = sb("ones_bf", [128, 128], BF16)
    xT = sb("xT", [128, 8, TT], F32)
    hT = sb("hT", [128, 8, TT], BF16)
    rstd = sb("rstd", [128, TT], F32)
    NWB = 4
    wbuf = [sb(f"wbuf{i}", [128, 4096], BF16) for i in range(NWB)]
    pre = sb("pre", [128, 4, TT + 8], F32)
    rt = [sb(f"rt{i}", [128, TT], F32) for i in range(4)]
    big = sb("big", [128, 32, TT], BF16)
    halo_p = sb("halo_p", [128, NFC, 2], F32)
    halo_s = sb("halo_s", [128, NFC, 4, 2], F32)
    xh = sb("xh", [128, 8, 2], F32)
    xh4 = sb("xh4", [128, 4, 16], F32)
    hTh = sb("hTh", [128, 8, 2], BF16)
    sqh = sb("sqh", [128, 8, 2], BF16)
    rstdh = sb("rstdh", [128, 2], F32)
    rowb = sb("rowb", [128, 1536], BF16)
    kss = sb("kss", [128, 4], F32)
    csB = sb("csB", [128, 64], F32)
    csA = sbA("csA", [128, 2, MT], F32)
    qT = sbA("qT", [128, 8, MT], BF16)
    kT = sbA("kT", [128, 8, MT], BF16)
    vtok = sbA("vtok", [128, 2, 2048], BF16)
    ktok = sbA("ktok", [128, 2, 1024], BF16)
    S32 = sbA("S32", [128, 8, 512], F32)
    Sbf = sbA("Sbf", [128, 8, 512], BF16)
    sTt = sbA("sTt", [128, 4, 128], BF16)
    qd = sbA("qd", [128, 8, 128], BF16)
    osq = [sbA(f"osq{i}", [128, 4, 128], BF16) for i in range(2)]
    orstd = [sbA(f"orstd{i}", [128, 128], F32) for i in range(2)]
    otmp = [sbA(f"otmp{i}", [128, 4, 128], F32) for i in range(2)]
    sq = big[:, 24:32, :]
    SQK = [("big", 24 + c) for c in range(8)]
    sgK = lambda i: ("big", i)
    ogK = lambda i: ("big", 16 + i)
    actK = lambda i: ("big", i)
    xin = [pre[:, 0:2, 0:512], pre[:, 2:4, 0:512]]
    xinK = [[("pre", 0), ("pre", 1)], [("pre", 2), ("pre", 3)]]
    print("sbuf remaining", nc.sbuf_bytes_remaining)

    psf = [es.enter_context(nc.psum_tensor(f"psf{i}", [128, 512], F32)) for i in range(6)]
    psb32 = [es.enter_context(nc.psum_tensor(f"psb{i}", [128, 512], F32)) for i in range(2)]
    psb = [t[:, :].bitcast(BF16) for t in psb32]
    ps_i = [0, 0]

    ps_n = [6]

    def psum():
        i = ps_i[0] % ps_n[0]
        ps_i[0] += 1
        return psf[i], ("psf", i)

    acc_i = [0]

    def psum_acc():
        i = 4 + acc_i[0] % 2
        acc_i[0] += 1
        return psf[i], ("psf", i)

    def psumb():
        i = ps_i[1] % 2
        ps_i[1] += 1
        return psb[i], ("psb", i)

    def C(name, a=None, b=None):
        o, w = CST[name]
        if a is None:
            return cst[:, o:o + w]
        return cst[:, o + a:o + b]

    cc_sems = [es.enter_context(nc.semaphore(f"cc{i}")) for i in range(20)]
    cc_i = [0]
    GROUPS = [[0, 1, 2, 3], [4, 5, 6, 7]]

    def allgather(src, dst, rkeys, wkeys):
        if SIMMODE:
            rows = src.shape[0]
            for r in range(4):
                DMA("sp", dst[r * rows:(r + 1) * rows, :], src, r=rkeys, w=wkeys)
            return
        sem = cc_sems[cc_i[0]]
        cc_i[0] += 1
        P.special("pool", lambda e: e.collective_compute("AllGather", ALU.bypass, replica_groups=GROUPS,
                                                         ins=[src], outs=[dst]), sem, reads=rkeys, writes=wkeys)
        for q in ("pool", "sp"):
            P.wait_special(q, sem)

    DMA("sp", cst[:], cst_d, w=["cst"])
    CP("dve", ident_bf[:], C("ident"), ["cst"], ["ident_bf"])
    CP("dve", ones_bf[:], C("ones"), ["cst"], ["ones_bf"])

    def cast_w(name, l, src2d, dst2d, rows, step=512):
        for r in range(0, rows, step):
            rr = min(step, rows - r)
            DMA("pool", dst2d[r:r + rr, :], src2d[r:r + rr, :],
                w=[("wb", name, l, k) for k in range(r // 128, (r + rr + 127) // 128)])

    def casts_for_layer(l):
        if l < 2:
            cast_w("ret_in", l, w_ret_in[l], wb_ret_in[l], D)
            cast_w("ret_out", l, w_ret_out[l], wb_ret_out[l], 2 * D)
        cast_w("ffn_in", l, w_ffn_in[l], wb_ffn_in[l], D)
        cast_w("ffn_out", l, w_ffn_out[l], wb_ffn_out[l], DFF, step=1408)

    casts_for_layer(0)
    casts_for_layer(1)
    cast_w("kv", 0, w_kv, wb_kv, D)
    if stage >= 4:
        for j2 in range(2):
            cast_w("wqg", j2, w_qg[j2], wb_qg[j2], D)
            cast_w("wo", j2, w_o[j2], wb_o[j2], D)
            casts_for_layer(2 + j2)

    wb_i = [0]

    def wload(name, l, dram2d, kc0, nkc, col0, ncols):
        i = wb_i[0] % NWB
        wb_i[0] += 1
        view = wbuf[i][:, 0:nkc * ncols].rearrange("p (k n) -> p k n", k=nkc)
        src = dram2d[kc0 * 128:(kc0 + nkc) * 128, col0:col0 + ncols].rearrange("(k p) n -> p k n", p=128)
        DMA("sp", view, src, r=[("wb", name, l, k) for k in range(kc0, kc0 + nkc)], w=[("wbuf", i)])
        return view, ("wbuf", i)

    def xkeys(tok0, N):
        return [("xTd", b) for b in range(tok0 // 128, (tok0 + N + 127) // 128)]

    def load_x(tok0, N):
        DMA("sp", xT[:, :, 0:N], xT_d[:, :, tok0:tok0 + N].rearrange("c p n -> p c n"), r=xkeys(tok0, N), w=["xT"])

    def store_x(tok0, N):
        DMA("sp", xT_d[:, :, tok0:tok0 + N].rearrange("c p n -> p c n"), xT[:, :, 0:N], r=["xT"], w=xkeys(tok0, N))

    def rmsnorm(src, N, gname, goff, dst, sqb, rsb, ks, kd, kq, kr):
        ACT(sqb[:, :, 0:N], src[:, :, 0:N], AF.Square, [ks], kq)
        ps, pk = psum()
        for c in range(8):
            MM(ps[:, 0:N], ones_bf[:], sqb[:, c, 0:N], c == 0, c == 7, kq + ["ones_bf"], [pk])
        ACT(rsb[:, 0:N], ps[:, 0:N], AF.Sqrt, [pk, "cst"], [kr], bias=C("eps"), scale=1.0 / D)
        RCP(rsb[:, 0:N], rsb[:, 0:N], [kr], [kr])
        for c in range(8):
            STT("dve", dst[:, c, 0:N], src[:, c, 0:N], C(gname, goff + c, goff + c + 1), rsb[:, 0:N],
                ALU.mult, ALU.mult, [ks, kr, "cst"], [(kd, c)])

    HK = [("hT", c) for c in range(8)]
    ident32 = C("ident")
    xi = [0]

    def in_transpose(src_rows, nrows, col):
        i = xi[0] % 2
        xi[0] += 1
        xb = xin[i]
        DMA("sp", xb[0:nrows], src_rows.rearrange("p (a b) -> p a b", a=2), w=xinK[i])
        for half in range(2):
            ps, pk = psum()
            for cc in range(4):
                c = half * 4 + cc
                TR(ps[:, cc * 128:cc * 128 + nrows], xb[0:nrows, c // 4, (c % 4) * 128:(c % 4 + 1) * 128], ident32[0:nrows, 0:nrows],
                   xinK[i] + ["cst"], [pk])
            ACT(xT[:, half * 4:half * 4 + 4, col:col + nrows], ps[:, :].rearrange("p (c n) -> p c n", c=4)[:, :, 0:nrows], AF.Copy,
                [pk], ["xT"])

    for t in range(SEG // TT):
        for bi in range(TT // 128):
            r0 = t * TT + bi * 128
            in_transpose(xp[r0:r0 + 128, :], 128, bi * 128)
        store_x(t * TT, TT)
    in_transpose(xs[:, :], NS, 0)
    store_x(SEG, NS)

    FT_TILES = [(t * TT, TT) for t in range(SEG // TT)] + [(SEG, NS)]
    MX_TILES = [(t * MT, MT) for t in range(SEG // MT)]
    MX_SAMPLE = [(SEG, 16), (SEG + 16, 16)]

    def ret_tile(l, tok0, N, mode):
        sample = tok0 >= SEG
        units = [(s * 8, 8) for s in range(2)] if sample else [(u * 128, 128) for u in range(N // 128)]
        w2d = wb_ret_in[l]
        load_x(tok0, N)
        rmsnorm(xT, N, "nmix", l * 8, hT, sq, rstd, "xT", "hT", SQK, "rstd")
        DMA("sp", csA[:, :, 0:N], ropeA_d[:, :, tok0:tok0 + N], w=["csA"])
        cos = csA[:, 0, 0:N]
        sin = csA[:, 1, 0:N]

        def proj_rope(blk, dstT, dkey):
            wv, wk = wload("ret_in", l, w2d, 0, 8, blk * 512, 512)
            for oc in range(4):
                ps, pk = psum()
                for kc in range(8):
                    MM(ps[:, 0:N], wv[:, kc, oc * 128:(oc + 1) * 128], hT[:, kc, 0:N], kc == 0, kc == 7, [wk] + HK, [pk])
                ACT(pre[:, oc, 0:N], ps[:, 0:N], AF.Copy, [pk], [("pre", oc)])
            for hh in range(2):
                c1, c2 = 2 * hh, 2 * hh + 1
                o1 = (blk % 2) * 4 + c1
                o2 = o1 + 1
                TTn("pool", rt[0][:, 0:N], pre[:, c1, 0:N], cos, ALU.mult, [("pre", c1), "csA"], [("rt", 0)])
                TTn("pool", rt[1][:, 0:N], pre[:, c2, 0:N], sin, ALU.mult, [("pre", c2), "csA"], [("rt", 1)])
                TTn("dve", dstT[:, o1, 0:N], rt[0][:, 0:N], rt[1][:, 0:N], ALU.subtract, [("rt", 0), ("rt", 1)], [(dkey, o1)])
                TTn("pool", rt[2][:, 0:N], pre[:, c2, 0:N], cos, ALU.mult, [("pre", c2), "csA"], [("rt", 2)])
                TTn("pool", rt[3][:, 0:N], pre[:, c1, 0:N], sin, ALU.mult, [("pre", c1), "csA"], [("rt", 3)])
                TTn("dve", dstT[:, o2, 0:N], rt[2][:, 0:N], rt[3][:, 0:N], ALU.add, [("rt", 2), ("rt", 3)], [(dkey, o2)])

        if mode == "full":
            proj_rope(0, qT, "qT")
            proj_rope(1, qT, "qT")
        proj_rope(2, kT, "kT")
        proj_rope(3, kT, "kT")
        for vb in range(4):
            wv, wk = wload("ret_in", l, w2d, 0, 8, 2048 + vb * 512, 512)
            for ui, (c0, L) in enumerate(units):
                ps, pk = psum()
                for kc in range(8):
                    MM(ps[0:L, :], hT[:, kc, c0:c0 + L], wv[:, kc, :], kc == 0, kc == 7, [wk] + HK, [pk])
                ACT(vtok[0:L, ui, vb * 512:(vb + 1) * 512], ps[0:L, :], AF.Copy, [pk], [("vtok", ui, vb)])
        if mode == "full":
            for gb in range(4):
                wv, wk = wload("ret_in", l, w2d, 0, 8, 4096 + gb * 512, 512)
                for oc in range(4):
                    ps, pk = psum()
                    for kc in range(8):
                        MM(ps[:, 0:N], wv[:, kc, oc * 128:(oc + 1) * 128], hT[:, kc, 0:N], kc == 0, kc == 7, [wk] + HK, [pk])
                    ACT(big[:, gb * 4 + oc, 0:N], ps[:, 0:N], AF.Silu, [pk], [sgK(gb * 4 + oc)])
        kdn = "kdec8" if sample else "kdec128"
        SK = [("S32", i) for i in range(8)]
        BK = [("Sbf", i) for i in range(8)]
        for ui, (c0, L) in enumerate(units):
            gl = [math.exp(LOGG[h] * L) for h in range(4)]
            if sample:
                s = (tok0 - SEG) // 8 + ui
                DMA("sp", S32[:], st_ret[l, s].rearrange("h (c p) e -> p (h c) e", p=128), w=SK)
                ACT(Sbf[:], S32[:], AF.Copy, SK, BK)
            pb, pbk = psumb()
            for dc in range(8):
                TR(pb[0:L, dc * 128:(dc + 1) * 128], kT[:, dc, c0:c0 + L], ident_bf[:], [("kT", dc), "ident_bf"], [pbk])
            for h in range(4):
                ACT(ktok[0:L, ui, h * 256:(h + 1) * 256], pb[0:L, h * 256:(h + 1) * 256], AF.Copy, [pbk, "cst"], [("ktok", ui, h)],
                    scale=C(kdn, h, h + 1)[0:L, :])
            for h in range(4):
                if mode == "full":
                    ps, pk = psum()
                    for dc in range(2):
                        MM(ps[0:L, 0:L], kT[:, 2 * h + dc, c0:c0 + L], qT[:, 2 * h + dc, c0:c0 + L], dc == 0, dc == 1,
                           [("kT", 2 * h + dc), ("qT", 2 * h + dc)], [pk])
                    TTn("dve", sTt[0:L, h, 0:L], ps[0:L, 0:L], C("decT", h * 128, h * 128 + L)[0:L, :], ALU.mult, [pk, "cst"], [("sT", h)])
                    TTn("pool", qd[:, 2 * h:2 * h + 2, 0:L], qT[:, 2 * h:2 * h + 2, c0:c0 + L],
                        C("qdec", h * 128, h * 128 + L).unsqueeze(1).to_broadcast([128, 2, L]), ALU.mult,
                        [("qT", 2 * h), ("qT", 2 * h + 1), "cst"], [("qd", h)])
                    po, pok = psum()
                    po3 = po[:, :].rearrange("p (e n) -> p e n", e=4)
                    for ec in range(4):
                        MM(po3[:, ec, 0:L], vtok[0:L, ui, h * 512 + ec * 128:h * 512 + (ec + 1) * 128], sTt[0:L, h, 0:L], True, False,
                           [("vtok", ui, h), ("sT", h)], [pok])
                        for dc in range(2):
                            MM(po3[:, ec, 0:L], Sbf[:, 2 * h + dc, ec * 128:(ec + 1) * 128], qd[:, 2 * h + dc, 0:L], False, dc == 1,
                               [("Sbf", 2 * h + dc), ("qd", h)], [pok])
                    ob = h % 2
                    ACT(osq[ob][:, :, 0:L], po3[:, :, 0:L], AF.Square, [pok], [("osq", ob)])
                    pr, prk = psum()
                    for ec in range(4):
                        MM(pr[:, 0:L], ones_bf[:], osq[ob][:, ec, 0:L], ec == 0, ec == 3, [("osq", ob), "ones_bf"], [prk])
                    ACT(orstd[ob][:, 0:L], pr[:, 0:L], AF.Sqrt, [prk, "cst"], [("orstd", ob)], bias=C("eps"), scale=1.0 / 512)
                    RCP(orstd[ob][:, 0:L], orstd[ob][:, 0:L], [("orstd", ob)], [("orstd", ob)])
                    TTn("dve", otmp[ob][:, :, 0:L], po3[:, :, 0:L], big[:, 4 * h:4 * h + 4, c0:c0 + L], ALU.mult,
                        [pok] + [sgK(4 * h + i) for i in range(4)], [("otmp", ob)])
                    TTn("pool", big[:, 16 + 4 * h:16 + 4 * h + 4, c0:c0 + L], otmp[ob][:, :, 0:L],
                        orstd[ob][:, 0:L].unsqueeze(1).to_broadcast([128, 4, L]), ALU.mult,
                        [("otmp", ob), ("orstd", ob)], [ogK(4 * h + i) for i in range(4)])
                for dc in range(2):
                    idx = 2 * h + dc
                    pS, pSk = psum()
                    MM(pS[:, :], ktok[0:L, ui, idx * 128:(idx + 1) * 128], vtok[0:L, ui, h * 512:(h + 1) * 512], True, True,
                       [("ktok", ui, h), ("vtok", ui, h)], [pSk])
                    STT("dve", S32[:, idx, :], S32[:, idx, :], float(gl[h]), pS[:, :], ALU.mult, ALU.add, [pSk, ("S32", idx)], [("S32", idx)])
                    if mode == "full" and not sample:
                        ACT(Sbf[:, idx, :], S32[:, idx, :], AF.Copy, [("S32", idx)], [("Sbf", idx)])
            if sample:
                DMA("sp", o_ret_s[l, s].rearrange("h (c p) e -> p (h c) e", p=128), S32[:], r=SK)
        if mode == "full":
            w2o = wb_ret_out[l]
            for ob_ in range(4):
                wv, wk = wload("ret_out", l, w2o, 0, 16, ob_ * 256, 256)
                for o2 in range(2):
                    oc = ob_ * 2 + o2
                    ps, pk = psum()
                    for kc in range(16):
                        MM(ps[:, 0:N], wv[:, kc, o2 * 128:(o2 + 1) * 128], big[:, 16 + kc, 0:N], kc == 0, kc == 15, [wk, ogK(kc)], [pk])
                    TTn("dve", xT[:, oc, 0:N], xT[:, oc, 0:N], ps[:, 0:N], ALU.add, [pk, "xT"], ["xT"])
            store_x(tok0, N)

    SK8 = [("S32", i) for i in range(8)]
    BK8 = [("Sbf", i) for i in range(8)]

    import os
    KCUT = int(os.environ.get("KCUT", "99"))

    def ret_layer(l):
        P.op("pool", lambda e: e.memset(S32[:], 0.0), (), SK8)
        for (tok0, N) in MX_TILES:
            ret_tile(l, tok0, N, "state")
        if KCUT <= 1:
            return
        for hf in range(2):
            DMA("sp", sl_in[l][hf].rearrange("p (c e) -> p c e", c=4), S32[:, 4 * hf:4 * hf + 4, :], r=SK8, w=[("sl_in", l, hf)])
            allgather(sl_in[l][hf], sl_all[l][hf], [("sl_in", l, hf)], [("sl_all", l, hf)])
        P.op("pool", lambda e: e.memset(S32[:], 0.0), (), SK8)
        for r in range(4):
            for hf in range(2):
                DMA("sp", xT[:, 4 * hf:4 * hf + 4, :], sl_all[l][hf][r * 128:(r + 1) * 128, :].rearrange("p (c e) -> p c e", c=4),
                    r=[("sl_all", l, hf)], w=["xT"])
            for h in range(4):
                STT("dve", S32[:, 2 * h:2 * h + 2, :], xT[:, 2 * h:2 * h + 2, :], C("scoef", r * 4 + h, r * 4 + h + 1),
                    S32[:, 2 * h:2 * h + 2, :], ALU.mult, ALU.add,
                    ["xT", "cst", ("S32", 2 * h), ("S32", 2 * h + 1)], [("S32", 2 * h), ("S32", 2 * h + 1)])
        ACT(Sbf[:], S32[:], AF.Copy, SK8, BK8)
        if KCUT <= 2:
            return
        for (tok0, N) in MX_TILES:
            ret_tile(l, tok0, N, "full")
        if KCUT <= 3:
            return
        DMA("sp", o_ret_p[l].rearrange("h (c p) e -> p (h c) e", p=128), S32[:], r=SK8)
        for (tok0, N) in MX_SAMPLE:
            ret_tile(l, tok0, N, "full")

    def ffn_layer(l):
        ps_n[0] = 6
        DMA("sp", xh[:], xT_d[:, :, SEG - 2:SEG].rearrange("c p n -> p c n"), r=xkeys(SEG - 128, 128), w=["xh"])
        DMA("sp", hx_in[l].rearrange("p (c n) -> p c n", c=8), xh[:], r=["xh"], w=[("hx_in", l)])
        allgather(hx_in[l], hx_all[l], [("hx_in", l)], [("hx_all", l)])
        DMA("sp", xh4[:], hx_all[l].rearrange("(r p) n -> p r n", p=128), r=[("hx_all", l)], w=["xh4"])
        xhf = xh[:].rearrange("p c n -> p (c n)")
        TS("dve", xhf, xh4[:, 0, :], C("hcoef", 0, 1), None, ALU.mult, None, ["xh4", "cst", "xh"], ["xh"])
        for r in range(1, 4):
            STT("dve", xhf, xh4[:, r, :], C("hcoef", r, r + 1), xhf, ALU.mult, ALU.add, ["xh4", "cst", "xh"], ["xh"])
        rmsnorm(xh, 2, "nffn", l * 8, hTh, sqh, rstdh, "xh", "hTh", ["sqh"], "rstdh")
        HHK = [("hTh", c) for c in range(8)]
        DMA("sp", halo_s[:], st_conv[l], w=[("halo_s", i) for i in range(NFC)])
        w2i = wb_ffn_in[l]
        w2o = wb_ffn_out[l]
        for ti, (tok0, N) in enumerate(FT_TILES):
            sample = tok0 >= SEG
            load_x(tok0, N)
            rmsnorm(xT, N, "nffn", l * 8, hT, sq, rstd, "xT", "hT", SQK, "rstd")
            for cb in range(6):
                ncol = 512 if cb < 5 else 256
                wu, wuk = wload("ffn_in", l, w2i, 0, 8, cb * 512, ncol)
                wg, wgk = wload("ffn_in", l, w2i, 0, 8, DFF + cb * 512, ncol)
                for oc in range(ncol // 128):
                    i = cb * 4 + oc
                    if ti == 0:
                        ph, phk = psum()
                        for kc in range(8):
                            MM(ph[:, 0:2], wu[:, kc, oc * 128:(oc + 1) * 128], hTh[:, kc, 0:2], kc == 0, kc == 7, [wuk] + HHK, [phk])
                        ACT(halo_p[:, i, :], ph[:, 0:2], AF.Copy, [phk], [("halo_p", i)])
                    pu, puk = psum()
                    pg, pgk = psum()
                    for kc in range(8):
                        MM(pu[:, 0:N], wu[:, kc, oc * 128:(oc + 1) * 128], hT[:, kc, 0:N], kc == 0, kc == 7, [wuk] + HK, [puk])
                    for kc in range(8):
                        MM(pg[:, 0:N], wg[:, kc, oc * 128:(oc + 1) * 128], hT[:, kc, 0:N], kc == 0, kc == 7, [wgk] + HK, [pgk])
                    ub = i % 3
                    u = pre[:, ub, :]
                    f = rt[ub]
                    uk = ("pre", ub)
                    fk = ("rt", ub)
                    if sample:
                        u3 = u[:, 0:40].rearrange("p (s n) -> p s n", s=4)
                        uv = [u3[:, :, sh:sh + 8] for sh in range(3)]
                        pu_v = pu[:, 0:N].rearrange("p (s n) -> p s n", s=4)
                        pg_v = pg[:, 0:N].rearrange("p (s n) -> p s n", s=4)
                        f_v = f[:, 0:N].rearrange("p (s n) -> p s n", s=4)
                        a_v = big[:, i, 0:N].rearrange("p (s n) -> p s n", s=4)
                        halo_src = halo_s[:, i, :, :]
                        halo_dst = u3[:, :, 0:2]
                        new_halo = u3[:, :, 8:10]
                        hk = ("halo_s", i)
                    else:
                        uv = [u[:, sh:sh + N] for sh in range(3)]
                        pu_v = pu[:, 0:N]
                        pg_v = pg[:, 0:N]
                        f_v = f[:, 0:N]
                        a_v = big[:, i, 0:N]
                        halo_src = halo_p[:, i, :]
                        halo_dst = u[:, 0:2]
                        new_halo = u[:, N:N + 2]
                        hk = ("halo_p", i)
                    cw = [C("cw", (l * 3 + jj) * NFC + i, (l * 3 + jj) * NFC + i + 1) for jj in range(3)]
                    cbias = C("cb", l * NFC + i, l * NFC + i + 1)
                    ACT(uv[2], pu_v, AF.Copy, [puk], [uk])
                    CP("pool", halo_dst, halo_src, [hk], [uk])
                    ACT(f_v, pu_v, AF.Identity, [puk, "cst"], [fk], bias=cbias, scale=cw[2])
                    STT("dve", f_v, uv[1], cw[1], f_v, ALU.mult, ALU.add, [uk, fk, "cst"], [fk])
                    STT("dve", f_v, uv[0], cw[0], f_v, ALU.mult, ALU.add, [uk, fk, "cst"], [fk])
                    CP("pool", halo_src, new_halo, [uk], [hk])
                    ACT(f_v, f_v, AF.Gelu_apprx_tanh, [fk], [fk])
                    TTn("dve", a_v, f_v, pg_v, ALU.mult, [fk, pgk], [actK(i)])
            for cbk in range(4):
                pss = [psum() for _ in range(2)]
                for kh in range(2):
                    wv, wk = wload("ffn_out", l, w2o, kh * 11, 11, cbk * 256, 256)
                    for oc in range(2):
                        ps, pk = pss[oc]
                        for kc in range(11):
                            kk = kh * 11 + kc
                            MM(ps[:, 0:N], wv[:, kc, oc * 128:(oc + 1) * 128], big[:, kk, 0:N], kk == 0, kk == NFC - 1, [wk, actK(kk)], [pk])
                for oc in range(2):
                    ps, pk = pss[oc]
                    og_ = cbk * 2 + oc
                    TTn("dve", xT[:, og_, 0:N], xT[:, og_, 0:N], ps[:, 0:N], ALU.add, [pk, "xT"], ["xT"])
            store_x(tok0, N)
            if ti == len(FT_TILES) - 2:
                DMA("sp", o_conv_p[l], halo_p[:], r=[("halo_p", i) for i in range(NFC)])
        DMA("sp", o_conv_s[l], halo_s[:], r=[("halo_s", i) for i in range(NFC)])

    kvt = [rt[2][:, 0:256], rt[3][:, 0:256], rt[3][:, 256:512]]
    kvtK = [("rt", 2), ("rt", 3), ("rt", 3)]

    def kv_build():
        for (tok0, N) in FT_TILES:
            sample = tok0 >= SEG
            units = [(0, NS)] if sample else [(u * 128, 128) for u in range(N // 128)]
            load_x(tok0, N)
            rmsnorm(xT, N, "kvn", 0, hT, sq, rstd, "xT", "hT", SQK, "rstd")
            wvs = [wload("kv", 0, wb_kv, 0, 8, cb * 512, 512) for cb in range(3)]
            for (c0, L) in units:
                r0 = tok0 + c0
                DMA("sp", csB[0:L, :], ropeB_d[r0:r0 + L, :], w=["csB"])
                for cb in range(3):
                    wv, wk = wvs[cb]
                    ps, pk = psum()
                    for kc in range(8):
                        MM(ps[0:L, :], hT[:, kc, c0:c0 + L], wv[:, kc, :], kc == 0, kc == 7, [wk] + HK, [pk])
                    rf = rt[cb % 2]
                    rk = ("rt", cb % 2)
                    if cb == 0:
                        ACT(rf[0:L, :], ps[0:L, :], AF.Copy, [pk], [rk])
                    else:
                        ACT(rf[0:L, 256:512], ps[0:L, 256:512], AF.Copy, [pk], [rk])
                        ACT(kvt[0][0:L, :], ps[0:L, 0:256], AF.Square, [pk], [kvtK[0]])
                        P.op("dve", lambda e, L=L: e.tensor_reduce(out=kss[0:L, :], in_=kvt[0][0:L, :].rearrange("p (h d) -> p h d", h=4),
                                                                   axis=AX.X, op=ALU.add), [kvtK[0]], ["kss"])
                        ACT(kss[0:L, :], kss[0:L, :], AF.Sqrt, ["kss", "cst"], ["kss"], bias=C("eps")[0:L, :], scale=1.0 / 64)
                        RCP(kss[0:L, :], kss[0:L, :], ["kss"], ["kss"])
                        k3 = kvt[1][0:L, :].rearrange("p (h d) -> p h d", h=4)
                        TTn("dve", k3, ps[0:L, 0:256].rearrange("p (h d) -> p h d", h=4), kss[0:L, :].unsqueeze(2).to_broadcast([L, 4, 64]),
                            ALU.mult, [pk, "kss"], [kvtK[1]])
                        TTn("dve", k3, k3, C("gk", cb * 64, cb * 64 + 64)[0:L, :].unsqueeze(1).to_broadcast([L, 4, 64]), ALU.mult,
                            [kvtK[1], "cst"], [kvtK[1]])
                        cosb = csB[0:L, 0:32].unsqueeze(1).to_broadcast([L, 4, 32])
                        sinb = csB[0:L, 32:64].unsqueeze(1).to_broadcast([L, 4, 32])
                        x1 = k3[:, :, 0:32]
                        x2 = k3[:, :, 32:64]
                        t3 = kvt[2][0:L, :].rearrange("p (h d) -> p h d", h=4)
                        r3 = rf[0:L, 0:256].rearrange("p (h d) -> p h d", h=4)
                        TTn("pool", t3[:, :, 0:32], x1, cosb, ALU.mult, [kvtK[1], "csB"], [kvtK[2]])
                        TTn("pool", t3[:, :, 32:64], x2, sinb, ALU.mult, [kvtK[1], "csB"], [kvtK[2]])
                        TTn("dve", r3[:, :, 0:32], t3[:, :, 0:32], t3[:, :, 32:64], ALU.subtract, [kvtK[2]], [rk])
                        TTn("pool", t3[:, :, 0:32], x2, cosb, ALU.mult, [kvtK[1], "csB", rk], [kvtK[2]])
                        TTn("pool", t3[:, :, 32:64], x1, sinb, ALU.mult, [kvtK[1], "csB"], [kvtK[2]])
                        TTn("dve", r3[:, :, 32:64], t3[:, :, 0:32], t3[:, :, 32:64], ALU.add, [kvtK[2]], [rk])
                    dst = [o_cmp, o_sel, o_win][cb]
                    DMA("sp", dst[r0:r0 + L, :], rf[0:L, :], r=[rk])
                    if sample and cb == 2:
                        for s_ in range(4):
                            DMA("sp", o_wins[s_, 504:512, :], rf[8 * s_:8 * s_ + 8, :], r=[rk])
                            DMA("sp", o_wins[s_, 0:504, :], cwin_d[s_, 8:512, :])
                    ACT(rowb[0:L, cb * 512:(cb + 1) * 512], rf[0:L, :], AF.Copy, [rk], [("rowb", cb)])
                DMA("sp", kv_loc[r0:r0 + L, :], rowb[0:L, :], r=[("rowb", i) for i in range(3)], w=[("kv_loc", r0)])

    def out_y():
        for (tok0, N) in FT_TILES:
            load_x(tok0, N)
            nb = [(0, NS)] if tok0 >= SEG else [(u * 128, 128) for u in range(N // 128)]
            for (c0, L) in nb:
                i = xi[0] % 2
                xi[0] += 1
                xb = xin[i]
                for half in range(2):
                    ps, pk = psum()
                    for cc in range(4):
                        c = half * 4 + cc
                        TR(ps[0:L, cc * 128:(cc + 1) * 128], xT[:, c, c0:c0 + L], ident32, ["xT", "cst"], [pk])
                    ACT(xb[0:L, half, :], ps[0:L, :], AF.Copy, [pk], xinK[i])
                DMA("sp", o_y[tok0 + c0:tok0 + c0 + L, :].rearrange("p (a b) -> p a b", a=2), xb[0:L], r=xinK[i])

    def phase_b_build():
        esB = contextlib.ExitStack()

        def sbB(name, shape, dt):
            return esB.enter_context(nc.sbuf_tensor("sbB_" + name, list(shape), dt))

        cB = sbB("cB", [128, NCB], F32)
        DMA("sp", cB[:], cstB_d, w=["cB"])

        def CB(name, a=None, b=None):
            o, w = CSTB[name]
            if a is None:
                return cB[:, o:o + w]
            return cB[:, o + a:o + b]

        idxp = sbB("idxp", [128, 64], I32)
        DMA("sp", idxp[:], idxp_d, w=["idxp"])
        ptf = sbB("ptf", [128, 256], F32)
        pti = sbB("pti", [128, 256], I32)
        idxs = sbB("idxs", [128, 256], I32)
        DMA("sp", pti[:], ptab_d.rearrange("s r -> (s r)").partition_broadcast(128), w=["pti"])
        CP("dve", ptf[:], pti[:], ["pti"], ["ptf"])
        STT("dve", ptf[:], ptf[:], 128.0, CB("iota").to_broadcast([128, 256]), ALU.mult, ALU.add, ["ptf", "cB"], ["ptf"])
        CP("dve", idxs[:], ptf[:], ["ptf"], ["idxs"])
        gA = [sbB(f"gA{i}", [128, 512], F32) for i in range(3)]
        gB = [sbB(f"gB{i}", [128, 512], F32) for i in range(3)]
        gW = sbB("gW", [128, 512], F32)
        rb = [sbB(f"rb{i}", [128, 1536], BF16) for i in range(3)]
        cT = [[sbB(f"cT{pg}{c}", [128, 2, 2064], BF16) for c in range(2)] for pg in range(2)]
        ktile = [sbB(f"ktile{i}", [128, 2, 128], BF16) for i in range(4)]
        vaug = [sbB(f"vaug{i}", [128, 4, 65], BF16) for i in range(4)]
        cvaug = sbB("cvaug", [128, 4, 65], BF16)
        w1s = sbB("w1s", [128, 2, 32, 128], BF16)
        w2s = sbB("w2s", [128, 2, 64], BF16)
        posT = sbB("posT", [128, 64], F32)
        posTb = sbB("posTb", [128, 64], BF16)
        pbias = sbB("pbias", [128, 2], F32)
        hidg = [sbB(f"hidg{i}", [128, 128], BF16) for i in range(2)]
        csq = sbB("csq", [64, 128], BF16)
        crs = sbB("crs", [64, 128], F32)
        cko = [sbB(f"cko{i}", [64, 128], BF16) for i in range(2)]
        cvo = [sbB(f"cvo{i}", [64, 128], BF16) for i in range(2)]
        for i in range(4):
            P.op("pool", lambda e, i=i: e.memset(vaug[i][:], 1.0), (), [("vaug", i)])
        P.op("pool", lambda e: e.memset(cvaug[:], 1.0), (), ["cvaug"])
        for pg in range(2):
            for c in range(2):
                P.op("pool", lambda e, pg=pg, c=c: e.memset(cT[pg][c][:], 0.0), (), [("cT", pg, c)])
        for half in range(2):
            DMA("pool", w1s[half * 64:(half + 1) * 64], w1_d.rearrange("c (l d) h -> d c l h", d=64), w=["w1s"])
            DMA("sp", posT[half * 64:(half + 1) * 64, :], posT_d, w=["posT"])
        DMA("pool", w2s[:], w2_d.rearrange("c h d -> h c d"), w=["w2s"])
        CP("dve", posTb[:], posT[:], ["posT"], ["posTb"])
        for c in range(2):
            pp, ppk = psum()
            for l_ in range(32):
                MM(pp[:, 0:1], w1s[0:64, c, l_, :], posTb[0:64, c * 32 + l_:c * 32 + l_ + 1], l_ == 0, l_ == 31, ["w1s", "posTb"], [ppk])
            ACT(pbias[:, c:c + 1], pp[:, 0:1], AF.Copy, [ppk], ["pbias"])

        kt_i = [0]

        def compress_chunk(v, ch):
            pg = ch % 2
            for c in range(2):
                for kv in range(4):
                    pair, half = kv // 2, kv % 2
                    rows = slice(half * 64, half * 64 + 64)
                    ph, phk = psum()
                    n_mm = 0
                    for r_ in range(2):
                        for s_ in range(16):
                            o = 16 * r_ + s_
                            MM(ph[:, 0:128], w1s[rows, c, r_ * 16 + s_, :], cT[pg][c][rows, pair, o:o + 16 * 127 + 1:16], n_mm == 0, n_mm == 31,
                               ["w1s", ("cT", pg, c)], [phk])
                            n_mm += 1
                    hb = (c * 4 + kv) % 2
                    ACT(hidg[hb][:], ph[:, 0:128], AF.Gelu_apprx_tanh, [phk, "pbias"], [("hidg", hb)], bias=pbias[:, c:c + 1])
                    po, pok = psum()
                    MM(po[0:64, 0:128], w2s[:, c, :], hidg[hb][:], True, True, ["w2s", ("hidg", hb)], [pok])
                    if c == 0:
                        ACT(csq[:], po[0:64, 0:128], AF.Square, [pok], ["csq"])
                        pr, prk = psum()
                        MM(pr[0:64, 0:128], ones_bf[0:64, 0:64], csq[:], True, True, ["csq", "ones_bf"], [prk])
                        ACT(crs[:], pr[0:64, 0:128], AF.Sqrt, [prk, "cst"], ["crs"], bias=C("eps")[0:64, :], scale=1.0 / 64)
                        RCP(crs[:], crs[:], ["crs"], ["crs"])
                        TTn("dve", crs[:], crs[:], po[0:64, 0:128], ALU.mult, ["crs", pok], ["crs"])
                        TS("dve", cko[kv % 2][:], crs[:], CB("gk0col")[0:64, :], None, ALU.mult, None, ["crs", "cB"], [("cko", kv % 2)])
                        DMA("sp", cmpK_d[v][pair, rows, ch * 128:(ch + 1) * 128], cko[kv % 2][:], r=[("cko", kv % 2)], w=[("cmpK_d", v)])
                    else:
                        ACT(cvo[kv % 2][:], po[0:64, 0:128], AF.Copy, [pok], [("cvo", kv % 2)])
                        pb, pbk = psumb()
                        TR(pb[:, 0:64], cvo[kv % 2][:], ident_bf[0:64, 0:64], [("cvo", kv % 2), "ident_bf"], [pbk])
                        ACT(cvaug[:, kv, 0:64], pb[:, 0:64], AF.Copy, [pbk], ["cvaug"])
                if c == 1:
                    DMA("sp", cmpV_d[v][ch].rearrange("p (k d) -> p k d", k=4), cvaug[:], r=["cvaug"], w=[("cmpV_d", v)])

        def kv_part(v, r, rbt, rbk, c0, Kd, Vd, kcol, vt, L=128):
            i = kt_i[0] % 4
            kt_i[0] += 1
            pb, pbk = psumb()
            for pr_ in range(2):
                TR(pb[:, pr_ * 128:pr_ * 128 + L], rbt[0:L, c0 + pr_ * 128:c0 + (pr_ + 1) * 128], ident_bf[0:L, 0:L], [rbk, "ident_bf"], [pbk])
            ACT(ktile[i][:, :, 0:L], pb[:, 0:256].rearrange("p (a n) -> p a n", a=2)[:, :, 0:L], AF.Copy, [pbk], [("ktile", i)])
            DMA("sp", Kd[:, :, kcol:kcol + L].rearrange("a p n -> p a n"), ktile[i][:, :, 0:L], r=[("ktile", i)], w=[("Kd", id(Kd))])
            CP("pool", vaug[i][0:L, :, 0:64], rbt[0:L, c0 + 256:c0 + 512].rearrange("p (k d) -> p k d", k=4), [rbk], [("vaug", i)])
            DMA("sp", Vd[vt, 0:L, :].rearrange("p (k d) -> p k d", k=4), vaug[i][0:L], r=[("vaug", i)], w=[("Vd", id(Vd))])

        ri = [0]
        for v in range(NV):
            for r in range(64):
                bi = ri[0] % 3
                ri[0] += 1
                rbt = rb[bi]
                rbk = ("rb", bi)
                if v == 0:
                    src = kv_all[(r % 16) // 2]
                    P.dma("pool", lambda e, rbt=rbt, src=src, r=r: e.indirect_dma_start(
                        out=rbt[:], out_offset=None, in_=src,
                        in_offset=bass.IndirectOffsetOnAxis(ap=idxp[:, r:r + 1], axis=0)), ["idxp", "kv_all"], [rbk])
                else:
                    s = v - 1
                    P.dma("pool", lambda e, bi=bi, s=s, r=r: e.indirect_dma_start(
                        out=gA[bi][:], out_offset=None, in_=ccmp_d,
                        in_offset=bass.IndirectOffsetOnAxis(ap=idxs[:, s * 64 + r:s * 64 + r + 1], axis=0)), ["idxs"], [("gA", bi)])
                    P.dma("pool", lambda e, bi=bi, s=s, r=r: e.indirect_dma_start(
                        out=gB[bi][:], out_offset=None, in_=csel_d,
                        in_offset=bass.IndirectOffsetOnAxis(ap=idxs[:, s * 64 + r:s * 64 + r + 1], axis=0)), ["idxs"], [("gB", bi)])
                    ACT(rbt[:, 0:512], gA[bi][:], AF.Copy, [("gA", bi)], [rbk])
                    CP("dve", rbt[:, 512:1024], gB[bi][:], [("gB", bi)], [rbk])
                    if r >= 60:
                        DMA("sp", gW[:], cwin_d[s, (r - 60) * 128:(r - 59) * 128, :], w=["gW"])
                        CP("dve", rbt[:, 1024:1536], gW[:], ["gW"], [rbk])
                ch, off = r // 16, (r % 16) * 128
                pg = ch % 2
                pb, pbk = psumb()
                for q4 in range(4):
                    TR(pb[:, q4 * 128:(q4 + 1) * 128], rbt[:, q4 * 128:(q4 + 1) * 128], ident_bf[:], [rbk, "ident_bf"], [pbk])
                for c in range(2):
                    ACT(cT[pg][c][:, :, off:off + 128], pb[:, c * 256:(c + 1) * 256].rearrange("p (a n) -> p a n", a=2), AF.Copy,
                        [pbk], [("cT", pg, c)])
                    if r % 16 == 0 and r > 0:
                        CP("pool", cT[1 - pg][c][:, :, 2048:2064], cT[pg][c][:, :, 0:16], [("cT", pg, c)], [("cT", 1 - pg, c)])
                if r % 16 == 0 and r > 0:
                    compress_chunk(v, ch - 1)
                if r == 63:
                    for c in range(2):
                        P.op("pool", lambda e, pg=pg, c=c: e.memset(cT[pg][c][:, :, 2048:2064], 0.0), (), [("cT", pg, c)])
                    compress_chunk(v, 3)
                kv_part(v, r, rbt, rbk, 512, selK_d[v], selV_d[v], r * 128, r)
                if v == 0:
                    kv_part(v, r, rbt, rbk, 1024, winK_d[v], winV_d[v], r * 128, r)
                elif r >= 60:
                    kv_part(v, r, rbt, rbk, 1024, winK_d[v], winV_d[v], (r - 60) * 128, r - 60)
            if v >= 1:
                s = v - 1
                bi = ri[0] % 3
                ri[0] += 1
                rbt = rb[bi]
                rbk = ("rb", bi)
                DMA("sp", rbt[0:8, :], kv_loc[SEG + 8 * s:SEG + 8 * s + 8, :], r=[("kv_loc", SEG)], w=[rbk])
                kv_part(v, 64, rbt, rbk, 512, selK_d[v], selV_d[v], 8192, 64, L=8)
                kv_part(v, 64, rbt, rbk, 1024, winK_d[v], winV_d[v], 512, 4, L=8)
        return esB

    def phase_b_attend():
        esC = contextlib.ExitStack()

        def sbC(name, shape, dt):
            return esC.enter_context(nc.sbuf_tensor("sbC_" + name, list(shape), dt))

        cB = sbC("cB", [128, NCB], F32)
        DMA("sp", cB[:], cstB_d, w=["cB"])

        def CB(name, a=None, b=None):
            o, w = CSTB[name]
            if a is None:
                return cB[:, o:o + w]
            return cB[:, o + a:o + b]

        selK = sbC("selK", [128, 2, 8320], BF16)
        selV = sbC("selV", [128, 65, 260], BF16)
        winK = sbC("winK", [128, 2, 640], BF16)
        winV = sbC("winV", [128, 6, 260], BF16)
        cmpK = sbC("cmpK", [128, 2, 512], BF16)
        cmpV = sbC("cmpV", [128, 5, 260], BF16)
        EK = lambda i: ("big", i)
        q8 = lambda c0: big[:, c0:c0 + 2, :].rearrange("p a (b n) -> p (a b) n", b=4)
        QZ = {"c": [(q8(20), [EK(20), EK(21)]), (q8(4), [EK(4), EK(5)])],
              "r": [(q8(22), [EK(22), EK(23)]), (q8(6), [EK(6), EK(7)])]}
        wmap = sbC("wmap", [128, 4, 128], BF16)
        Dtab = CB("Dtab")
        Er = [sbC(f"Er{i}", [128, 128], BF16) for i in range(2)]
        selT4 = sbC("selT4", [128, 4, 128], BF16)
        msk = [sbC(f"msk{i}", [128, 128], BF16) for i in range(4)]
        accs = [rt[2], rt[3]]
        otok = [sbC(f"otok{i}", [128, 4, 65], F32) for i in range(2)]
        o_tok2 = [rt[0][:, :].rearrange("p (h d) -> p h d", d=64), rt[1][:, :].rearrange("p (h d) -> p h d", d=64)]
        o_bf = big[:, 16:18, :].rearrange("p a (h d) -> p (a h) d", d=64)
        oT = big[:, 18:20, :].rearrange("p a (b n) -> p (a b) n", b=4)
        gts = sbC("gts", [128, 48], F32)
        sm = sbC("sm", [128, 64], F32)
        impb = [pre[:, 3, i * 128:(i + 1) * 128] for i in range(3)]
        vmask = pre[:, 2, 0:512].rearrange("p (a n) -> p a n", a=4)
        sel01 = sbC("sel01", [128, 128], BF16)
        negB = sbC("negB", [128, 8], F32)
        mq = sbC("mq", [128, 8], F32)
        CP("dve", wmap[:].rearrange("p a b -> p (a b)"), CB("wmap"), ["cB"], ["wmap"])
        P.op("pool", lambda e: e.memset(winV[:, 5, :], 0.0), (), ["winV"])
        P.op("pool", lambda e: e.memset(cmpV[:, 4, :], 0.0), (), ["cmpV"])
        P.op("pool", lambda e: e.memset(selV[:, 64, :], 0.0), (), ["selV"])
        for i, (nm, a) in enumerate([("gq", 0), ("gq", 64), ("gk", 0), ("gk", 64), ("gk", 128)]):
            P.op("dve", lambda e, i=i, nm=nm, a=a: e.tensor_reduce(out=mq[:, i:i + 1], in_=C(nm, a, a + 64), axis=AX.X, op=ALU.max,
                                                                   apply_absolute_value=True), ["cst", "mq"], ["mq"])
        for j2 in range(2):
            for br in range(3):
                STT("dve", negB[:, j2 * 3 + br:j2 * 3 + br + 1], mq[:, j2:j2 + 1], -8.0, mq[:, 2 + br:3 + br], ALU.mult, ALU.mult,
                    ["mq", "negB"], ["negB"])
        EK = lambda i: ("big", i)
        KEYOF = {id(selK): ("selK", "selV"), id(cmpK): ("cmpK", "cmpV"), id(winK): ("winK", "winV")}
        SC = 0.125

        def load_view(v):
            DMA("sp", selK[:], selK_d[v].rearrange("a p n -> p a n"), r=[("Kd", id(selK_d[v]))], w=["selK"])
            DMA("sp", selV[:], selV_d[v].rearrange("t p n -> p t n"), r=[("Vd", id(selV_d[v]))], w=["selV"])
            DMA("sp", cmpK[:], cmpK_d[v].rearrange("a p n -> p a n"), r=[("cmpK_d", v)], w=["cmpK"])
            DMA("sp", cmpV[:, 0:4, :], cmpV_d[v].rearrange("t p n -> p t n"), r=[("cmpV_d", v)], w=["cmpV"])
            if v >= 1:
                DMA("sp", winK[:], winK_d[v].rearrange("a p n -> p a n"), r=[("Kd", id(winK_d[v]))], w=["winK"])
                DMA("sp", winV[:, 0:5, :], winV_d[v].rearrange("t p n -> p t n"), r=[("Vd", id(winV_d[v]))], w=["winV"])

        def qblock(l, j2, c0, L, v, qi, wq):
            nq = L
            NQ = 4 * nq
            sample = v >= 1
            pq = [psum(), psum()]
            pgt, pgk = psum_acc()
            for hb in range(2):
                for kc in range(8):
                    MM(pq[hb][0][0:L, :], hT[:, kc, c0:c0 + L], wq[hb][0][:, kc, :], kc == 0, kc == 7, [wq[hb][1]] + HK, [pq[hb][1]])
            for kc in range(8):
                MM(pgt[0:L, 0:48], hT[:, kc, c0:c0 + L], wq[2][0][:, kc, :], kc == 0, kc == 7, [wq[2][1]] + HK, [pgk])
            ACT(gts[0:L, :], pgt[0:L, 0:48], AF.Sigmoid, [pgk], ["gts"])
            qsq = pre[0:L, 0:2, 0:512]
            for hb in range(2):
                ACT(qsq[:, hb, :], pq[hb][0][0:L, :], AF.Square, [pq[hb][1]], [("pre", hb)])
            for hb in range(2):
                P.op("dve", lambda e, hb=hb: e.tensor_reduce(out=sm[0:L, 8 + 8 * hb:16 + 8 * hb], in_=qsq[:, hb, :].rearrange("p (h d) -> p h d", d=64),
                                                            axis=AX.X, op=ALU.add), [("pre", hb), "sm"], ["sm"])
            ACT(sm[0:L, 24:40], sm[0:L, 8:24], AF.Sqrt, ["sm", "cst"], ["sm"], bias=C("eps")[0:L, :], scale=1.0 / 64)
            RCP(sm[0:L, 24:40], sm[0:L, 24:40], ["sm"], ["sm"])
            qn = [rt[0], rt[1]]
            qr_ = [rt[2], rt[3]]
            for hb in range(2):
                q3 = qn[hb][0:L, :].rearrange("p (h d) -> p h d", d=64)
                TTn("dve", q3, pq[hb][0][0:L, :].rearrange("p (h d) -> p h d", d=64),
                    sm[0:L, 24 + 8 * hb:32 + 8 * hb].unsqueeze(2).to_broadcast([L, 8, 64]), ALU.mult, [pq[hb][1], "sm"], [("rt", hb)])
                TTn("pool", q3, q3, C("gq", j2 * 64, j2 * 64 + 64)[0:L, :].unsqueeze(1).to_broadcast([L, 8, 64]), ALU.mult,
                    [("rt", hb), "cst"], [("rt", hb)])
                r3 = qr_[hb][0:L, :].rearrange("p (h d) -> p h d", d=64)
                t3 = pre[0:L, 2 + hb, 0:512].rearrange("p (h d) -> p h d", d=64)
                cosb = csB[0:L, 0:32].unsqueeze(1).to_broadcast([L, 8, 32])
                sinb = csB[0:L, 32:64].unsqueeze(1).to_broadcast([L, 8, 32])
                x1, x2 = q3[:, :, 0:32], q3[:, :, 32:64]
                TTn("pool", t3[:, :, 0:32], x1, cosb, ALU.mult, [("rt", hb), "csB"], [("pre", 2 + hb)])
                TTn("pool", t3[:, :, 32:64], x2, sinb, ALU.mult, [("rt", hb), "csB"], [("pre", 2 + hb)])
                TTn("dve", r3[:, :, 0:32], t3[:, :, 0:32], t3[:, :, 32:64], ALU.subtract, [("pre", 2 + hb)], [("rt", 2 + hb)])
                TTn("pool", t3[:, :, 0:32], x2, cosb, ALU.mult, [("rt", hb), "csB", ("rt", 2 + hb)], [("pre", 2 + hb)])
                TTn("pool", t3[:, :, 32:64], x1, sinb, ALU.mult, [("rt", hb), "csB"], [("pre", 2 + hb)])
                TTn("dve", r3[:, :, 32:64], t3[:, :, 0:32], t3[:, :, 32:64], ALU.add, [("pre", 2 + hb)], [("rt", 2 + hb)])
            for ver, (srcs, vn) in enumerate([(qn, "c"), (qr_, "r")]):
                qp = big[0:L, 12 + 2 * ver:14 + 2 * ver, :]
                for pair in range(2):
                    src4 = srcs[pair][0:L, :].rearrange("p (a g d) -> p a g d", a=2, g=4)
                    dst4 = qp[:, pair, :].rearrange("p (g a d) -> p a g d", a=2, g=4)
                    CP("pool" if pair else "dve", dst4, src4, [("rt", 2 * ver + pair)], [EK(12 + 2 * ver + pair)])
                pb, pbk = psumb()
                for pg_ in range(8):
                    TR(pb[:, pg_ * 128:pg_ * 128 + L], qp[:, pg_ // 4, (pg_ % 4) * 128:(pg_ % 4 + 1) * 128], ident_bf[0:L, 0:L],
                       [EK(12 + 2 * ver), EK(13 + 2 * ver), "ident_bf"], [pbk])
                pb8 = pb[:, :].rearrange("p (a n) -> p a n", a=8)
                ACT(QZ[vn][0][0][0:64, :, 0:L], pb8[0:64, :, 0:L], AF.Copy, [pbk], QZ[vn][0][1])
                ACT(QZ[vn][1][0][64:128, :, 0:L], pb8[64:128, :, 0:L], AF.Copy, [pbk], QZ[vn][1][1])
            exr = CB("exrow_s" if sample else "exrow_p")[0:L, :]
            fir = CB("first_s" if sample else "first_p")[0:L, :]
            curv = 128.0 if sample else float(96 + 2 * qi)
            tt = impb[2][0:L, :]
            if sample:
                TS("dve", tt, CB("qrow")[0:L, :], curv, None, ALU.subtract, None, ["cB"], [("pre", 3)])
            else:
                TS("dve", tt, CB("qrow")[0:L, :], CB("hi")[0:L, :], curv, ALU.subtract, ALU.subtract, ["cB"], [("pre", 3)])
            VM = [("pre", 2)]
            val, av, nf, fb = vmask[0:L, 0, :], vmask[0:L, 1, :], vmask[0:L, 2, :], vmask[0:L, 3, :]
            TS("dve", val, tt, 0.0, None, ALU.is_le, None, [("pre", 3)], VM)
            TTn("dve", val, val, exr, ALU.mult, VM + ["cB"], VM)
            TS("dve", av, val, 1e6, -1e6, ALU.mult, ALU.add, VM, VM)
            TS("dve", fb, tt, 0.0, None, ALU.is_equal, None, [("pre", 3)], VM)
            TS("dve", nf, tt, -1.0, None, ALU.is_equal, None, [("pre", 3)], VM)
            TTn("dve", fb, fb, nf, ALU.add, VM, VM)
            TTn("dve", fb, fb, fir, ALU.add, VM + ["cB"], VM)
            TTn("dve", fb, fb, exr, ALU.mult, VM + ["cB"], VM)
            TS("dve", fb, fb, 1.0, None, ALU.min, None, VM, VM)
            TS("dve", nf, fb, -1.0, 1.0, ALU.mult, ALU.add, VM, VM)
            TS("dve", fb, fb, 1e6, None, ALU.mult, None, VM, VM)
            if not sample:
                r0 = 44 + qi
                DMA("sp", winK[:], winK_d[0][:, :, r0 * 128:(r0 + 5) * 128].rearrange("a p n -> p a n"), r=[("Kd", id(winK_d[0]))], w=["winK"])
                DMA("sp", winV[:, 0:5, :], winV_d[0][r0:r0 + 5].rearrange("t p n -> p t n"), r=[("Vd", id(winV_d[0]))], w=["winV"])
            qoff = 0.0 if sample else float(128 * qi)
            basen = "base_s" if sample else "base_p"
            ei = [0]

            def qsel(ver, kv):
                pair, half = kv // 2, kv % 2
                t, keys = QZ[ver][half]
                return t[:, pair * 4:pair * 4 + 4, 0:nq], keys

            def vwide(Vt, t, kv):
                flat = Vt[:, :, :].rearrange("p t n -> p (t n)")
                o = t * 260 + kv * 65
                if not sample:
                    return flat[:, o:o + 128]
                return flat[:, o:o + 65]

            def run_units(specs, D=2):
                def front(i):
                    sp = specs[i]
                    sc, sck = psf[i % 4], ("psf", i % 4)
                    nk = sp["nk"]
                    MM(sc[0:nk, 0:NQ].rearrange("p (g n) -> p g n", g=4), sp["K"], sp["q"][0], True, True, [sp["kkey"]] + sp["q"][1], [sck])
                    et = big[0:nk, sp["ti"], 0:NQ]
                    ACT(et, sc[0:nk, 0:NQ], AF.Exp, [sck, "negB"], [EK(sp["ti"])], bias=negB[0:nk, sp["bidx"]:sp["bidx"] + 1], scale=SC)
                    return sp["pre"]() if sp.get("pre") else None

                def back(i, aux):
                    sp = specs[i]
                    nk = sp["nk"]
                    et = big[0:nk, sp["ti"], 0:NQ]
                    sp["mask"](et.rearrange("p (g n) -> p g n", g=4), EK(sp["ti"]), aux)
                    vw = sp["V"]
                    MM(sp["acc"][0][0:vw.shape[1], 0:NQ], vw, et, sp["first"], sp["last"], [sp["vkey"], EK(sp["ti"])], [sp["acc"][1]])

                n_ = len(specs)
                pend = [front(k) for k in range(min(D, n_))]
                for i in range(n_):
                    if i + D < n_:
                        pend.append(front(i + D))
                    back(i, pend[i])

            def finish_branch(acc, kv, br, first_branch):
                ai = (kv * 3 + br) % 2
                ACT(accs[ai][0:65, 0:NQ], acc[0][0:65, 0:NQ], AF.Copy, [acc[1]], [("rt", 2 + ai)])
                pt_, ptk = psum()
                for g in range(4):
                    TR(pt_[0:nq, g * 65:(g + 1) * 65], accs[ai][0:65, g * nq:(g + 1) * nq], ident32[0:65, 0:65], [("rt", 2 + ai), "cst"], [ptk])
                ACT(otok[ai][0:nq].rearrange("p g d -> p (g d)"), pt_[0:nq, 0:260], AF.Copy, [ptk], [("otok", ai)])
                TS("dve", sm[0:nq, 0:4], otok[ai][0:nq, :, 64], 1e-30, None, ALU.max, None, [("otok", ai)], ["sm"])
                RCP(sm[0:nq, 0:4], sm[0:nq, 0:4], ["sm"], ["sm"])
                gsl = gts[0:nq, :].rearrange("p (h b) -> p h b", b=3)[:, 4 * kv:4 * kv + 4, br]
                TTn("dve", sm[0:nq, 4:8], sm[0:nq, 0:4], gsl, ALU.mult, ["sm", "gts"], ["sm"])
                dst = o_tok2[kv // 2][0:nq, 4 * (kv % 2):4 * (kv % 2) + 4, :]
                fbc = sm[0:nq, 4:8].unsqueeze(2).to_broadcast([nq, 4, 64])
                if first_branch:
                    TTn("dve", dst, otok[ai][0:nq, :, 0:64], fbc, ALU.mult, [("otok", ai), "sm"], [("rt", kv // 2)])
                else:
                    TTn("dve", otok[ai][0:nq, :, 0:64], otok[ai][0:nq, :, 0:64], fbc, ALU.mult, [("otok", ai), "sm"], [("otok", ai)])
                    TTn("dve", dst, dst, otok[ai][0:nq, :, 0:64], ALU.add, [("otok", ai), ("rt", kv // 2)], [("rt", kv // 2)])
                return ai

            for kv in range(4):
                pair, half = kv // 2, kv % 2
                rows = slice(half * 64, half * 64 + 64)
                acc = psum_acc()
                specs = []
                for c in range(4):
                    def pre_c(c=c):
                        mk = msk[c]
                        TS("dve", mk[:, 0:nq], CB("qrow")[:, 0:nq], qoff, CB(basen, c, c + 1), ALU.add, ALU.is_ge, ["cB"], [("msk", c)])
                        return mk

                    def m_c(e3, ek, mk, c=c):
                        TTn("dve", e3, e3, mk[:, 0:nq].unsqueeze(1).to_broadcast([128, 4, nq]), ALU.mult, [ek, ("msk", c)], [ek])
                    specs.append(dict(K=cmpK[:, pair, c * 128:(c + 1) * 128], q=qsel("c", kv), nk=128, bidx=j2 * 3 + 0, pre=pre_c, mask=m_c,
                                      V=vwide(cmpV, c, kv), acc=acc, first=(c == 0), last=(c == 3), ti=c, kkey="cmpK", vkey="cmpV"))
                run_units(specs)
                ai = finish_branch(acc, kv, 0, True)
                pim, pimk = psum()
                for g in range(4):
                    for c in range(4):
                        MM(pim[0:nq, g * 128:(g + 1) * 128], big[:, c, g * nq:(g + 1) * nq], wmap[:, c, :], c == 0, c == 3, [EK(c), "wmap"], [pimk])
                imp = impb[0][0:L, :]
                TS("dve", imp, pim[0:nq, 0:128], sm[0:nq, 0:1], None, ALU.mult, None, [pimk, "sm"], [("pre", 3)])
                for g in range(1, 4):
                    STT("dve", imp, pim[0:nq, g * 128:(g + 1) * 128], sm[0:nq, g:g + 1], imp, ALU.mult, ALU.add, [pimk, "sm", ("pre", 3)], [("pre", 3)])
                TTn("dve", imp, imp, val, ALU.mult, [("pre", 3)] + VM, [("pre", 3)])
                TTn("dve", imp, imp, av, ALU.add, [("pre", 3)] + VM, [("pre", 3)])
                TTn("dve", imp, imp, nf, ALU.mult, [("pre", 3)] + VM, [("pre", 3)])
                TTn("dve", imp, imp, fb, ALU.add, [("pre", 3)] + VM, [("pre", 3)])
                P.op("dve", lambda e: e.max(sm[0:L, 40:48], imp), [("pre", 3)], ["sm"])
                P.op("dve", lambda e: e.match_replace(impb[1][0:L, :], sm[0:L, 40:48], imp, -3e6), [("pre", 3), "sm"], [("pre", 3)])
                P.op("dve", lambda e: e.max(sm[0:L, 48:56], impb[1][0:L, :]), [("pre", 3)], ["sm"])
                kth = 54 if sample else 55
                TS("dve", impb[1][0:L, :], imp, sm[0:L, kth:kth + 1], None, ALU.is_ge, None, [("pre", 3), "sm"], [("pre", 3)])
                STT("dve", sel01[0:L, :], imp, -5e5, impb[1][0:L, :], ALU.is_gt, ALU.mult, [("pre", 3)], ["sel01"])
                pb, pbk = psumb()
                TR(pb[:, 0:L], sel01[0:L, :], ident_bf[0:L, 0:L], ["sel01", "ident_bf"], [pbk])
                ACT(selT4[:, kv, 0:L], pb[:, 0:L], AF.Copy, [pbk], ["selT4"])
            ntile = 64 if sample else 49 + qi
            for kvp in range(2):
                kvs = (2 * kvp, 2 * kvp + 1)
                pair = kvp
                accS = {kv: (psf[4 + kv % 2], ("psf", 4 + kv % 2)) for kv in kvs}
                specs = []
                pmr = {}
                for r in range(ntile):
                    for kv in kvs:
                        def pre_s(r=r, kv=kv, kvs=kvs):
                            if kv == kvs[0]:
                                eb = r % 2
                                TS("dve", Er[eb][:], Dtab, float(2 * r), None, ALU.is_equal, None, ["cB"], [("Er", eb)])
                                pbi = ps_i[1] % 2
                                ps_i[1] += 1
                                pm2 = psb32[pbi][:, 0:2 * nq].rearrange("p (k n) -> p k n", k=2)
                                MM(pm2, Er[eb][:], selT4[:, kvs[0]:kvs[0] + 2, 0:nq], True, True, [("Er", eb), "selT4"], [("psb", pbi)])
                                pmr[r] = (pm2, ("psb", pbi))
                            return pmr[r]

                        def m_s(e3, ek, aux, r=r, kv=kv):
                            pm2, pmk = aux
                            if (not sample) and r == 48 + qi:
                                mi = kv % 2
                                TTn("dve", msk[mi][:, 0:nq], pm2[:, kv % 2, :], CB("tri_le")[:, 0:nq], ALU.mult, [pmk, "cB"], [("msk", mi)])
                                TTn("dve", e3, e3, msk[mi][:, 0:nq].unsqueeze(1).to_broadcast([128, 4, nq]), ALU.mult, [ek, ("msk", mi)], [ek])
                            else:
                                TTn("dve", e3, e3, pm2[:, kv % 2, :].unsqueeze(1).to_broadcast([128, 4, nq]), ALU.mult, [ek, pmk], [ek])
                        specs.append(dict(K=selK[:, pair, r * 128:(r + 1) * 128], q=qsel("r", kv), nk=128, bidx=j2 * 3 + 1, pre=pre_s, mask=m_s,
                                          V=vwide(selV, r, kv), acc=accS[kv], first=(r == 0), last=(r == ntile - 1 and not sample),
                                          ti=8 + len(specs) % 4, kkey="selK", vkey="selV"))
                if sample:
                    for kv in kvs:
                        def m_n(e3, ek, aux):
                            TTn("dve", e3, e3, CB("tri_le")[0:8, 0:nq].unsqueeze(1).to_broadcast([8, 4, nq]), ALU.mult, [ek, "cB"], [ek])
                        specs.append(dict(K=selK[:, pair, 8192:8200], q=qsel("r", kv), nk=8, bidx=j2 * 3 + 1, pre=None, mask=m_n,
                                          V=selV[0:8, 64, kv * 65:(kv + 1) * 65], acc=accS[kv], first=False, last=True,
                                          ti=8 + len(specs) % 4, kkey="selK", vkey="selV"))
                run_units(specs)
                for kv in kvs:
                    finish_branch(accS[kv], kv, 1, False)
                specs = []
                for kv in kvs:
                    for w in range(5):
                        if sample and w == 4:
                            def m_w(e3, ek, aux):
                                TTn("dve", e3, e3, CB("tri_le")[0:8, 0:nq].unsqueeze(1).to_broadcast([8, 4, nq]), ALU.mult, [ek, "cB"], [ek])
                            specs.append(dict(K=winK[:, pair, 512:520], q=qsel("r", kv), nk=8, bidx=j2 * 3 + 2, pre=None, mask=m_w,
                                              V=winV[0:8, 4, kv * 65:(kv + 1) * 65], acc=accS[kv], first=False, last=True,
                                              ti=8 + len(specs) % 4, kkey="winK", vkey="winV"))
                            continue

                        def m_w(e3, ek, aux, w=w):
                            if sample:
                                if w == 0:
                                    TTn("dve", e3, e3, CB("tri_gt")[:, 0:nq].unsqueeze(1).to_broadcast([128, 4, nq]), ALU.mult, [ek, "cB"], [ek])
                                return
                            exc = CB("extile", 44 + qi + w, 45 + qi + w)
                            if w == 0 or w == 4:
                                tri = CB("tri_gt" if w == 0 else "tri_le")[:, 0:nq].unsqueeze(1).to_broadcast([128, 4, nq])
                                STT("dve", e3, e3, exc, tri, ALU.mult, ALU.mult, [ek, "cB"], [ek])
                            else:
                                TS("dve", e3, e3, exc, None, ALU.mult, None, [ek, "cB"], [ek])
                        specs.append(dict(K=winK[:, pair, w * 128:(w + 1) * 128], q=qsel("r", kv), nk=128, bidx=j2 * 3 + 2, pre=None, mask=m_w,
                                          V=vwide(winV, w, kv), acc=accS[kv], first=(w == 0), last=(w == 4 and not sample),
                                          ti=8 + len(specs) % 4, kkey="winK", vkey="winV"))
                run_units(specs)
                for kv in kvs:
                    finish_branch(accS[kv], kv, 2, False)
            for hb in range(2):
                CP("dve", o_bf[0:L, 8 * hb:8 * hb + 8, :], o_tok2[hb][0:L], [("rt", hb)], [EK(16 + hb)])
            pb, pbk = psumb()
            for c in range(8):
                TR(pb[:, c * 128:c * 128 + L], o_bf[0:L, 2 * c:2 * c + 2, :].rearrange("p h d -> p (h d)"), ident_bf[0:L, 0:L], [EK(16), EK(17), "ident_bf"], [pbk])
            ACT(oT[:, :, 0:L], pb[:, :].rearrange("p (a n) -> p a n", a=8)[:, :, 0:L], AF.Copy, [pbk], [EK(18), EK(19)])

        def wo_apply(l, j2, c0, L):
            for ob_ in range(2):
                wv, wk = wload("wo", j2, wb_o[j2], 0, 8, ob_ * 512, 512)
                for o4 in range(4):
                    oc = ob_ * 4 + o4
                    ps, pk = psum()
                    for kc in range(8):
                        MM(ps[:, 0:L], wv[:, kc, o4 * 128:(o4 + 1) * 128], oT[:, kc, 0:L], kc == 0, kc == 7, [wk, EK(18), EK(19)], [pk])
                    TTn("dve", xT[:, oc, c0:c0 + L], xT[:, oc, c0:c0 + L], ps[:, 0:L], ALU.add, [pk, "xT"], ["xT"])

        def nsa_layer(l):
            j2 = l - 2
            ps_n[0] = 2
            for vn in ("c", "r"):
                P.op("pool", lambda e, vn=vn: e.memset(QZ[vn][0][0][64:128], 0.0), (), QZ[vn][0][1])
                P.op("pool", lambda e, vn=vn: e.memset(QZ[vn][1][0][0:64], 0.0), (), QZ[vn][1][1])
            load_view(0)
            for ti, (tok0, N) in enumerate(FT_TILES):
                sample = tok0 >= SEG
                load_x(tok0, N)
                rmsnorm(xT, N, "nmix", l * 8, hT, sq, rstd, "xT", "hT", SQK, "rstd")
                def wqs():
                    return [wload("wqg", j2, wb_qg[j2], 0, 8, 0, 512), wload("wqg", j2, wb_qg[j2], 0, 8, 512, 512),
                            wload("wqg", j2, wb_qg[j2], 0, 8, 1024, 48)]
                if not sample:
                    for qb in range(4):
                        qi = ti * 4 + qb
                        DMA("sp", csB[:, :], ropeB_d[tok0 + qb * 128:tok0 + (qb + 1) * 128, :], w=["csB"])
                        qblock(l, j2, qb * 128, 128, 0, qi, wqs())
                        wo_apply(l, j2, qb * 128, 128)
                else:
                    for s in range(4):
                        load_view(1 + s)
                        DMA("sp", csB[0:8, :], ropeB_d[tok0 + 8 * s:tok0 + 8 * s + 8, :], w=["csB"])
                        qblock(l, j2, 8 * s, 8, 1 + s, 0, wqs())
                        wo_apply(l, j2, 8 * s, 8)
                store_x(tok0, N)

        return esC, nsa_layer

    nl = min(stage, 2)
    for l in range(nl):
        ret_layer(l)
        if KCUT <= 4:
            break
        ffn_layer(l)
    if stage >= 3:
        kv_build()
    if stage >= 4:
        for ch in range(8):
            allgather(kv_loc[256 * ch:256 * (ch + 1), :], kv_all[ch], [("kv_loc", 256 * ch + 128 * i) for i in range(2)], ["kv_all"])
        P.barrier()
        P.flush()
        esA.close()
        esB = phase_b_build()
        P.barrier()
        P.flush()
        esB.close()
        if stage >= 5:
            esC, nsa_layer = phase_b_attend()
            for l in range(2, min(stage - 3, 4)):
                nsa_layer(l)
                ffn_layer(l)
    out_y()
    P.finish()
    print("ops:", {e: P.cnt[e] for e in P.engs})
    return nc, es


def make_in_maps(inp):
    maps = []
    for c in range(8):
        b, j = c // 4, c % 4
        cst, ropeA, ropeB = host_tables(c, inp)
        sc = inp["state_conv"][:, 4 * c:4 * c + 4]
        sc = sc.reshape(4, 4, 2, NFC, 128).transpose(0, 4, 3, 1, 2)
        m = {
            "xp": np.ascontiguousarray(inp["x_prompt"][b, j * SEG:(j + 1) * SEG]),
            "xs": np.ascontiguousarray(inp["x_sample"][4 * c:4 * c + 4].reshape(NS, D)),
            "cst": cst, "ropeA": ropeA, "ropeB": ropeB,
            "ret_w_in": inp["ret_w_in"], "ret_w_out": inp["ret_w_out"],
            "ffn_w_in": inp["ffn_w_in"], "ffn_w_out": inp["ffn_w_out"], "kv_w": inp["kv_w"],
            "state_ret": np.ascontiguousarray(inp["state_ret"][:, 4 * c:4 * c + 4]),
            "state_conv": np.ascontiguousarray(sc),
            "cache_cmp": inp["cache_cmp_kv"].reshape(-1, 512), "cache_sel": inp["cache_sel_kv"].reshape(-1, 512),
            "cache_win": np.ascontiguousarray(inp["cache_win_kv"][4 * c:4 * c + 4].reshape(4, 512, 512)),
            "ptab": np.ascontiguousarray(inp["page_table"][4 * c:4 * c + 4]).astype(np.int32),
            "cmp_w1": inp["cmp_w1"], "cmp_w2": inp["cmp_w2"],
            "posT": np.ascontiguousarray(inp["cmp_pos"].transpose(2, 0, 1).reshape(64, 64)),
            "nsa_w_qg": inp["nsa_w_qg"], "nsa_w_o": inp["nsa_w_o"],
        }
        m["cstB"], m["idxp"] = host_tables_b(c, inp)
        maps.append(m)
    return maps


_CACHE = {}


def run_device(inp, stage=99):
    inp = {k: np.asarray(v) for k, v in inp.items()}
    if stage not in _CACHE:
        _CACHE[stage] = build_program(stage)
    nc, es = _CACHE[stage]
    res = run_bass_kernel_spmd(nc, make_in_maps(inp), core_ids=list(range(8)))
    return res.results


def kernel(**inputs):
    res = run_device(inputs)
    f32 = np.float32
    y_p = np.zeros((2, 8192, D), f32)
    y_s = np.zeros((32, 8, D), f32)
    ret_p = np.zeros((2, 2, 4, 256, 512), f32)
    ret_s = np.zeros((2, 32, 4, 256, 512), f32)
    conv_p = np.zeros((4, 2, 2, DFF), f32)
    conv_s = np.zeros((4, 32, 2, DFF), f32)
    cmp_p = np.zeros((2, 8192, 2, 4, 64), f32)
    cmp_s = np.zeros((32, 8, 2, 4, 64), f32)
    sel_p = np.zeros((2, 8192, 2, 4, 64), f32)
    sel_s = np.zeros((32, 8, 2, 4, 64), f32)
    win_p = np.zeros((2, 512, 2, 4, 64), f32)
    win_s = np.zeros((32, 512, 2, 4, 64), f32)
    for c in range(8):
        b, j = c // 4, c % 4
        r = res[c]
        sl = slice(j * SEG, (j + 1) * SEG)
        y_p[b, sl] = r["o_y"][:SEG]
        y_s[4 * c:4 * c + 4] = r["o_y"][SEG:].reshape(4, 8, D)
        ret_s[:, 4 * c:4 * c + 4] = r["o_ret_s"]
        conv_s[:, 4 * c:4 * c + 4] = r["o_conv_s"].transpose(0, 3, 4, 2, 1).reshape(4, 4, 2, DFF)
        cmp_p[b, sl] = r["o_cmp"][:SEG].reshape(SEG, 2, 4, 64)
        sel_p[b, sl] = r["o_sel"][:SEG].reshape(SEG, 2, 4, 64)
        cmp_s[4 * c:4 * c + 4] = r["o_cmp"][SEG:].reshape(4, 8, 2, 4, 64)
        sel_s[4 * c:4 * c + 4] = r["o_sel"][SEG:].reshape(4, 8, 2, 4, 64)
        win_s[4 * c:4 * c + 4] = r["o_wins"].reshape(4, 512, 2, 4, 64)
        if j == 3:
            ret_p[:, b] = r["o_ret_p"]
            conv_p[:, b] = r["o_conv_p"].transpose(0, 3, 2, 1).reshape(4, 2, DFF)
            win_p[b] = r["o_win"][SEG - 512:SEG].reshape(512, 2, 4, 64)
    return (y_p, y_s, ret_p, ret_s, conv_p, conv_s, cmp_p, cmp_s, sel_p, sel_s, win_p, win_s)
```

```python
import contextlib
import math
import numpy as np
import ml_dtypes
import concourse.bass as bass
import concourse.mybir as mybir
from concourse.bass_utils import run_bass_kernel_spmd

F32 = mybir.dt.float32
BF16 = mybir.dt.bfloat16
I32 = mybir.dt.int32
AF = mybir.ActivationFunctionType
ALU = mybir.AluOpType
AX = mybir.AxisListType

D = 1024
SEG = 2048
NS = 32
NT = SEG + NS
TT = 512
DFF = 2816
NFC = 22
EPS = 1e-6
GAM = [1.0 - 2.0 ** (-5.0 - h) for h in range(4)]
LOGG = [math.log1p(-(2.0 ** (-5.0 - h))) for h in range(4)]
SEMW = 30000
SIMMODE = False


class Prog:
    def __init__(self, nc, es):
        self.nc = nc
        self.es = es
        self.engs = ["pe", "act", "dve", "pool", "sp"]
        self.ops = {e: [] for e in self.engs}
        self.cnt = {e: 0 for e in self.engs}
        self.psems = {e: [] for e in self.engs}
        self.waited_c = {e: {x: 0 for x in self.engs} for e in self.engs}
        self.waited_d = {e: {} for e in self.engs}
        self.bufs = {}
        self.NDS = 12
        self.dq = ["sp", "pool", "act"]
        self.dsems = {q: [es.enter_context(nc.semaphore(f"d_{q}_{i}")) for i in range(self.NDS)] for q in self.dq}
        self.dval = {(q, i): 0 for q in self.dq for i in range(self.NDS)}
        self.dlast = {(q, i): None for q in self.dq for i in range(self.NDS)}
        self.dcnt = {q: 0 for q in self.dq}
        self.nps = 0

    def _psem(self, e, n):
        w = (n - 1) // SEMW
        while len(self.psems[e]) <= w:
            self.psems[e].append(self.es.enter_context(self.nc.semaphore(f"p_{e}_{len(self.psems[e])}")))
        return self.psems[e][w], (n - 1) % SEMW + 1

    def _need(self, e, tok, waits):
        if tok is None:
            return
        if tok[0] == "c":
            _, x, n = tok
            if x == "pe" and e == "pe":
                return
            if self.waited_c[e][x] >= n:
                return
            self.waited_c[e][x] = n
            waits.append(self._psem(x, n))
        else:
            _, q, i, v = tok
            if self.waited_d[e].get((q, i), 0) >= v:
                return
            self.waited_d[e][(q, i)] = v
            waits.append((self.dsems[q][i], v))

    def _deps(self, e, reads, writes, waits):
        for k in reads:
            b = self.bufs.get(k)
            if b:
                for t in b[0]:
                    self._need(e, t, waits)
        for k in writes:
            b = self.bufs.get(k)
            if b:
                for t in b[0]:
                    self._need(e, t, waits)
                for t in b[1].values():
                    self._need(e, t, waits)

    def _upd(self, tok, reads, writes, dma=False):
        rk = (tok[0], tok[1]) if tok[0] == "c" else (tok[0], tok[1], tok[2])
        for k in reads:
            b = self.bufs.setdefault(k, [[], {}])
            b[1][rk] = tok
        for k in writes:
            self.bufs[k] = [[tok], {}]

    def op(self, e, fn, reads=(), writes=()):
        waits = []
        self._deps(e, reads, writes, waits)
        n = self.cnt[e] + 1
        self.cnt[e] = n
        tok = ("c", e, n)
        self.ops[e].append((waits, fn, (self._psem(e, n)[0], 1)))
        self._upd(tok, reads, writes)
        return tok

    def dma(self, q, fn, reads=(), writes=()):
        waits = []
        self._deps(q, reads, writes, waits)
        i = self.dcnt[q] % self.NDS
        self.dcnt[q] += 1
        self._need(q, self.dlast[(q, i)], waits)
        v = self.dval[(q, i)] + 16
        self.dval[(q, i)] = v
        tok = ("d", q, i, v)
        self.dlast[(q, i)] = tok
        self.ops[q].append((waits, fn, (self.dsems[q][i], 16)))
        self._upd(tok, reads, writes)
        return tok

    def special(self, q, fn, sem, reads=(), writes=()):
        waits = []
        self._deps(q, reads, writes, waits)
        self.ops[q].append((waits, fn, (sem, 1)))
        tok = ("s", sem)
        return tok

    def wait_special(self, e, sem):
        self.ops[e].append(([(sem, 1)], None, None))

    def barrier(self):
        for e in self.engs:
            waits = []
            for x in self.engs:
                if self.cnt[x] > 0:
                    self._need(e, ("c", x, self.cnt[x]), waits)
            for q in self.dq:
                for i in range(self.NDS):
                    self._need(e, self.dlast[(q, i)], waits)
            self.ops[e].append((waits, None, None))
        self.bufs = {}

    def finish(self):
        waits = []
        for q in self.dq:
            for i in range(self.NDS):
                self._need("sp", self.dlast[(q, i)], waits)
        self.ops["sp"].append((waits, None, None))
        self.flush()

    def flush(self):
        nc = self.nc
        ops = self.ops
        self.ops = {e: [] for e in self.engs}

        def emit(E, e):
            for waits, fn, inc in ops[E]:
                for s, v in waits:
                    e.wait_ge(s, v)
                if fn is not None:
                    ins = fn(e)
                    ins.then_inc(inc[0], inc[1])

        with nc.Block() as block:
            @block.tensor
            def _(e):
                emit("pe", e)

            @block.scalar
            def _(e):
                emit("act", e)

            @block.vector
            def _(e):
                emit("dve", e)

            @block.gpsimd
            def _(e):
                emit("pool", e)

            @block.sync
            def _(e):
                emit("sp", e)


CST = {}
_off = 0
for _n, _w in [("decT", 512), ("qdec", 512), ("kdec128", 4), ("kdec8", 4), ("ident", 128), ("ones", 128),
               ("nmix", 32), ("nffn", 32), ("kvn", 8), ("cw", 4 * 3 * NFC), ("cb", 4 * NFC),
               ("scoef", 16), ("hcoef", 4), ("eps", 1), ("gk", 3 * 64), ("gq", 2 * 64)]:
    CST[_n] = (_off, _w)
    _off += _w
NCST = _off


CSTB = {}
_off = 0
for _n, _w in [("iota", 1), ("qrow", 128), ("tri_le", 128), ("tri_gt", 128), ("hi", 1), ("exrow_p", 128), ("first_p", 128),
               ("exrow_s", 128), ("first_s", 128), ("base_p", 4), ("base_s", 4), ("extile", 64), ("wmap", 512), ("gk0col", 1),
               ("Dtab", 128)]:
    CSTB[_n] = (_off, _w)
    _off += _w
NCB = _off
NV = 5


def host_tables_b(c, inp):
    j = c % 4
    k3 = 3 - j
    t = np.zeros((128, NCB), np.float32)

    def put(name, arr):
        o, w = CSTB[name]
        t[:, o:o + w] = np.asarray(arr, np.float32).reshape(128, w)

    p = np.arange(128)
    put("iota", p[:, None])
    put("qrow", np.broadcast_to(np.arange(128)[None, :], (128, 128)))
    put("tri_le", (p[:, None] <= p[None, :]))
    put("tri_gt", (p[:, None] > p[None, :]))
    put("hi", (p >= 64)[:, None])
    blk = np.arange(128)
    put("exrow_p", np.broadcast_to((blk >= 32 * k3)[None, :], (128, 128)))
    put("first_p", np.broadcast_to((blk == 32 * k3)[None, :], (128, 128)))
    put("exrow_s", np.ones((128, 128)))
    put("first_s", np.broadcast_to((blk == 0)[None, :], (128, 128)))
    n = (np.arange(4)[None, :] * 128 + p[:, None])
    ex = (16 * n >= 2048 * k3) & (n <= 510)
    put("base_p", np.where(ex, 16.0 * n + 31 - 6144, 1e9))
    put("base_s", np.where(n <= 510, -1.0, 1e9))
    put("extile", np.broadcast_to((np.arange(64) >= 16 * k3)[None, :], (128, 64)))
    sb_ = np.arange(128)[None, None, :]
    nn = n[:, :, None]
    ov = np.maximum(0, np.minimum(16 * nn + 32, 64 * sb_ + 64) - np.maximum(16 * nn, 64 * sb_)) / 32.0
    put("wmap", ov)
    put("gk0col", inp["kv_knorm"][0][p % 64][:, None])
    put("Dtab", p[:, None] - (p[None, :] >= 64))
    idx = np.zeros((128, 64), np.int32)
    for r in range(64):
        seg = r // 16 - k3
        tl = r % 16
        idx[:, r] = p if seg < 0 else seg * 256 + (tl % 2) * 128 + p
    return t, idx


def host_tables(c, inp):
    j = c % 4
    cst = np.zeros((128, NCST), np.float32)

    def put(name, arr):
        o, w = CST[name]
        cst[:, o:o + w] = np.asarray(arr, np.float32).reshape(128, w)

    m = np.arange(128)[:, None]
    l = np.arange(128)[None, :]
    decT = np.zeros((128, 4, 128), np.float64)
    qdec = np.zeros((128, 4, 128), np.float64)
    kd128 = np.zeros((128, 4), np.float64)
    kd8 = np.zeros((128, 4), np.float64)
    for h in range(4):
        decT[:, h, :] = np.where(l >= m, np.exp(np.maximum(l - m, 0) * LOGG[h]), 0.0) / 16.0
        qdec[:, h, :] = np.exp((l + 1.0) * LOGG[h])
        kd128[:, h] = np.exp((127.0 - np.arange(128)) * LOGG[h]) / 16.0
        kd8[:, h] = np.exp((7.0 - np.minimum(np.arange(128), 7)) * LOGG[h]) / 16.0
    put("decT", decT)
    put("qdec", qdec)
    put("kdec128", kd128)
    put("kdec8", kd8)
    put("eps", np.full((128, 1), EPS))
    put("ident", np.eye(128))
    put("ones", np.ones((128, 128)))
    put("nmix", inp["norm_mix"].reshape(4, 8, 128).transpose(2, 0, 1))
    put("nffn", inp["norm_ffn"].reshape(4, 8, 128).transpose(2, 0, 1))
    put("kvn", inp["kv_norm"].reshape(8, 128).T)
    put("cw", inp["ffn_conv_w"].reshape(4, 3, NFC, 128).transpose(3, 0, 1, 2))
    put("cb", inp["ffn_conv_b"].reshape(4, NFC, 128).transpose(2, 0, 1))
    sc = np.zeros((4, 4), np.float64)
    for r in range(4):
        if r < j:
            for h in range(4):
                sc[r, h] = math.exp(LOGG[h] * SEG * (j - 1 - r))
    put("scoef", np.broadcast_to(sc.reshape(1, 16), (128, 16)))
    hc = np.zeros(4)
    if j > 0:
        hc[j - 1] = 1.0
    put("hcoef", np.broadcast_to(hc.reshape(1, 4), (128, 4)))
    put("gk", np.broadcast_to(inp["kv_knorm"].reshape(1, 192), (128, 192)))
    put("gq", np.broadcast_to(inp["nsa_qnorm"].reshape(1, 128), (128, 128)))
    pos = np.concatenate([SEG * j + np.arange(SEG), 8192 + (np.arange(NS) % 8)]).astype(np.float32)
    invA = np.exp(-math.log(10000.0) * np.arange(128, dtype=np.float32) / 128).astype(np.float32)
    angA = (invA[:, None] * pos[None, :]).astype(np.float32)
    ropeA = np.stack([np.cos(angA), np.sin(angA)], 1).astype(np.float32)
    invB = np.exp(-math.log(10000.0) * np.arange(32, dtype=np.float32) / 32).astype(np.float32)
    angB = (pos[:, None] * invB[None, :]).astype(np.float32)
    ropeB = np.concatenate([np.cos(angB), np.sin(angB)], 1).astype(np.float32)
    return cst, ropeA, ropeB


def build_program(stage=99):
    nc = bass.Bass("TRN2", target_bir_lowering=False)
    es = contextlib.ExitStack()
    P = Prog(nc, es)

    def din(name, shape, dt=F32):
        return nc.dram_tensor(name, list(shape), dt, kind="ExternalInput").ap()

    def dout(name, shape, dt=F32):
        return nc.dram_tensor(name, list(shape), dt, kind="ExternalOutput").ap()

    def dscr(name, shape, dt):
        return nc.dram_tensor(name, list(shape), dt).ap()

    def sb(name, shape, dt):
        return es.enter_context(nc.sbuf_tensor("sb_" + name, list(shape), dt))

    def MM(out, lhsT, rhs, start, stop, r, w):
        P.op("pe", lambda e: e.matmul(out, lhsT, rhs, start=start, stop=stop), r, w)

    def TR(out, in_, ident, r, w):
        P.op("pe", lambda e: e.transpose(out, in_, ident), r, w)

    def ACT(out, in_, func, r, w, bias=None, scale=None):
        kw = {}
        if bias is not None:
            kw["bias"] = bias
        if scale is not None:
            kw["scale"] = scale
        P.op("act", lambda e: e.activation(out, in_, func, **kw), r, w)

    def TTn(eng, out, a, b, op, r, w):
        P.op(eng, lambda e: e.tensor_tensor(out, a, b, op), r, w)

    def TS(eng, out, a, s1, s2, op0, op1, r, w):
        if op1 is None:
            P.op(eng, lambda e: e.tensor_scalar(out, a, s1, None, op0=op0), r, w)
        else:
            P.op(eng, lambda e: e.tensor_scalar(out, a, s1, s2, op0=op0, op1=op1), r, w)

    def STT(eng, out, in0, scalar, in1, op0, op1, r, w):
        P.op(eng, lambda e: e.scalar_tensor_tensor(out=out, in0=in0, scalar=scalar, in1=in1, op0=op0, op1=op1), r, w)

    def RCP(out, in_, r, w):
        P.op("dve", lambda e: e.reciprocal(out, in_), r, w)

    def CP(eng, out, in_, r, w):
        P.op(eng, lambda e: e.tensor_copy(out, in_), r, w)

    def DMA(q, out, in_, r=(), w=()):
        P.dma(q, lambda e: e.dma_start(out=out, in_=in_), r, w)

    xp = din("xp", [SEG, D])
    xs = din("xs", [NS, D])
    cst_d = din("cst", [128, NCST])
    ropeA_d = din("ropeA", [128, 2, NT])
    ropeB_d = din("ropeB", [NT, 64])
    w_ret_in = din("ret_w_in", [2, D, 6 * D])
    w_ret_out = din("ret_w_out", [2, 2 * D, D])
    w_ffn_in = din("ffn_w_in", [4, D, 2 * DFF])
    w_ffn_out = din("ffn_w_out", [4, DFF, D])
    w_kv = din("kv_w", [D, 1536])
    st_ret = din("state_ret", [2, 4, 4, 256, 512])
    st_conv = din("state_conv", [4, 128, NFC, 4, 2])
    o_ret_p = dout("o_ret_p", [2, 4, 256, 512])
    o_ret_s = dout("o_ret_s", [2, 4, 4, 256, 512])
    o_conv_p = dout("o_conv_p", [4, 128, NFC, 2])
    o_conv_s = dout("o_conv_s", [4, 128, NFC, 4, 2])
    o_cmp = dout("o_cmp", [NT, 512])
    o_sel = dout("o_sel", [NT, 512])
    o_win = dout("o_win", [NT, 512])
    o_y = dout("o_y", [NT, D])
    xT_d = dscr("xT_d", [8, 128, NT], F32)
    wb_ret_in = dscr("wb_ret_in", [2, D, 6 * D], BF16)
    wb_ret_out = dscr("wb_ret_out", [2, 2 * D, D], BF16)
    wb_ffn_in = dscr("wb_ffn_in", [4, D, 2 * DFF], BF16)
    wb_ffn_out = dscr("wb_ffn_out", [4, DFF, D], BF16)
    wb_kv = dscr("wb_kv", [D, 1536], BF16)
    sl_in = [[dscr(f"sl_in{l}_{hf}", [128, 2048], F32) for hf in range(2)] for l in range(2)]
    sl_all = [[dscr(f"sl_all{l}_{hf}", [4 * 128, 2048], F32) for hf in range(2)] for l in range(2)]
    hx_in = [dscr(f"hx_in{l}", [128, 16], F32) for l in range(4)]
    hx_all = [dscr(f"hx_all{l}", [4 * 128, 16], F32) for l in range(4)]
    kv_loc = dscr("kv_loc", [NT, 1536], BF16)

    NPOOLR = 2560 * 128
    ccmp_d = din("cache_cmp", [NPOOLR, 512])
    csel_d = din("cache_sel", [NPOOLR, 512])
    cwin_d = din("cache_win", [4, 512, 512])
    ptab_d = din("ptab", [4, 64], I32)
    w1_d = din("cmp_w1", [2, 2048, 128])
    w2_d = din("cmp_w2", [2, 128, 64])
    posT_d = din("posT", [64, 64])
    w_qg = din("nsa_w_qg", [2, D, 1072])
    w_o = din("nsa_w_o", [2, D, D])
    cstB_d = din("cstB", [128, NCB])
    idxp_d = din("idxp", [128, 64], I32)
    o_wins = dout("o_wins", [4, 512, 512])
    wb_qg = dscr("wb_qg", [2, D, 1072], BF16)
    wb_o = dscr("wb_o", [2, D, D], BF16)
    kv_all = [dscr(f"kv_all{i}", [1024, 1536], BF16) for i in range(8)]
    selK_d = [dscr(f"selK{v}", [2, 128, 8320], BF16) for v in range(NV)]
    selV_d = [dscr(f"selV{v}", [65, 128, 260], BF16) for v in range(NV)]
    winK_d = [dscr(f"winK{v}", [2, 128, 8192 if v == 0 else 640], BF16) for v in range(NV)]
    winV_d = [dscr(f"winV{v}", [64 if v == 0 else 5, 128, 260], BF16) for v in range(NV)]
    cmpK_d = [dscr(f"cmpK{v}", [2, 128, 512], BF16) for v in range(NV)]
    cmpV_d = [dscr(f"cmpV{v}", [4, 128, 260], BF16) for v in range(NV)]

    MT = 256
    esA = contextlib.ExitStack()

    def sbA(name, shape, dt):
        return esA.enter_context(nc.sbuf_tensor("sbA_" + name, list(shape), dt))
    cst = sb("cst", [128, NCST], F32)
    ident_bf = sb("ident_bf", [128, 128], BF16)
    ones_bf = sb("ones_bf", [128, 128], BF16)
    xT = sb("xT", [128, 8, TT], F32)
    hT = sb("hT", [128, 8, TT], BF16)
    rstd = sb("rstd", [128, TT], F32)
    NWB = 4
    wbuf = [sb(f"wbuf{i}", [128, 4096], BF16) for i in range(NWB)]
    pre = sb("pre", [128, 4, TT + 8], F32)
    rt = [sb(f"rt{i}", [128, TT], F32) for i in range(4)]
    big = sb("big", [128, 32, TT], BF16)
    halo_p = sb("halo_p", [128, NFC, 2], F32)
    halo_s = sb("halo_s", [128, NFC, 4, 2], F32)
    xh = sb("xh", [128, 8, 2], F32)
    xh4 = sb("xh4", [128, 4, 16], F32)
    hTh = sb("hTh", [128, 8, 2], BF16)
    sqh = sb("sqh", [128, 8, 2], BF16)
    rstdh = sb("rstdh", [128, 2], F32)
    rowb = sb("rowb", [128, 1536], BF16)
    kss = sb("kss", [128, 4], F32)
    csB = sb("csB", [128, 64], F32)
    csA = sbA("csA", [128, 2, MT], F32)
    qT = sbA("qT", [128, 8, MT], BF16)
    kT = sbA("kT", [128, 8, MT], BF16)
    vtok = sbA("vtok", [128, 2, 2048], BF16)
    ktok = sbA("ktok", [128, 2, 1024], BF16)
    S32 = sbA("S32", [128, 8, 512], F32)
    Sbf = sbA("Sbf", [128, 8, 512], BF16)
    sTt = sbA("sTt", [128, 4, 128], BF16)
    qd = sbA("qd", [128, 8, 128], BF16)
    osq = [sbA(f"osq{i}", [128, 4, 128], BF16) for i in range(2)]
    orstd = [sbA(f"orstd{i}", [128, 128], F32) for i in range(2)]
    otmp = [sbA(f"otmp{i}", [128, 4, 128], F32) for i in range(2)]
    sq = big[:, 24:32, :]
    SQK = [("big", 24 + c) for c in range(8)]
    sgK = lambda i: ("big", i)
    ogK = lambda i: ("big", 16 + i)
    actK = lambda i: ("big", i)
    xin = [pre[:, 0:2, 0:512], pre[:, 2:4, 0:512]]
    xinK = [[("pre", 0), ("pre", 1)], [("pre", 2), ("pre", 3)]]
    print("sbuf remaining", nc.sbuf_bytes_remaining)

    psbig = es.enter_context(nc.psum_tensor("psbig", [128, 2048], F32))
    psf = [psbig[:, i * 512:(i + 1) * 512] for i in range(4)] + \
          [es.enter_context(nc.psum_tensor(f"psf{i}", [128, 512], F32))[:, :] for i in range(4, 6)]
    psb32 = [es.enter_context(nc.psum_tensor(f"psb{i}", [128, 512], F32)) for i in range(2)]
    psb = [t[:, :].bitcast(BF16) for t in psb32]
    ps_i = [0, 0]

    ps_n = [6]

    def psum():
        i = ps_i[0] % ps_n[0]
        ps_i[0] += 1
        return psf[i], ("psf", i)

    acc_i = [0]

    def psum_acc():
        i = 4 + acc_i[0] % 2
        acc_i[0] += 1
        return psf[i], ("psf", i)

    def psumb():
        i = ps_i[1] % 2
        ps_i[1] += 1
        return psb[i], ("psb", i)

    def C(name, a=None, b=None):
        o, w = CST[name]
        if a is None:
            return cst[:, o:o + w]
        return cst[:, o + a:o + b]

    cc_sems = [es.enter_context(nc.semaphore(f"cc{i}")) for i in range(20)]
    cc_i = [0]
    GROUPS = [[0, 1, 2, 3], [4, 5, 6, 7]]

    def allgather(src, dst, rkeys, wkeys):
        if SIMMODE:
            rows = src.shape[0]
            for r in range(4):
                DMA("sp", dst[r * rows:(r + 1) * rows, :], src, r=rkeys, w=wkeys)
            return
        sem = cc_sems[cc_i[0]]
        cc_i[0] += 1
        P.special("pool", lambda e: e.collective_compute("AllGather", ALU.bypass, replica_groups=GROUPS,
                                                         ins=[src], outs=[dst]), sem, reads=rkeys, writes=wkeys)
        for q in ("pool", "sp"):
            P.wait_special(q, sem)

    DMA("sp", cst[:], cst_d, w=["cst"])
    CP("dve", ident_bf[:], C("ident"), ["cst"], ["ident_bf"])
    CP("dve", ones_bf[:], C("ones"), ["cst"], ["ones_bf"])

    def cast_w(name, l, src2d, dst2d, rows, step=512):
        for r in range(0, rows, step):
            rr = min(step, rows - r)
            DMA("pool", dst2d[r:r + rr, :], src2d[r:r + rr, :],
                w=[("wb", name, l, k) for k in range(r // 128, (r + rr + 127) // 128)])

    def casts_for_layer(l):
        if l < 2:
            cast_w("ret_in", l, w_ret_in[l], wb_ret_in[l], D)
            cast_w("ret_out", l, w_ret_out[l], wb_ret_out[l], 2 * D)
        cast_w("ffn_in", l, w_ffn_in[l], wb_ffn_in[l], D)
        cast_w("ffn_out", l, w_ffn_out[l], wb_ffn_out[l], DFF, step=1408)

    casts_for_layer(0)
    casts_for_layer(1)
    cast_w("kv", 0, w_kv, wb_kv, D)
    if stage >= 4:
        for j2 in range(2):
            cast_w("wqg", j2, w_qg[j2], wb_qg[j2], D)
            cast_w("wo", j2, w_o[j2], wb_o[j2], D)
            casts_for_layer(2 + j2)

    wb_i = [0]

    def wload(name, l, dram2d, kc0, nkc, col0, ncols):
        i = wb_i[0] % NWB
        wb_i[0] += 1
        view = wbuf[i][:, 0:nkc * ncols].rearrange("p (k n) -> p k n", k=nkc)
        src = dram2d[kc0 * 128:(kc0 + nkc) * 128, col0:col0 + ncols].rearrange("(k p) n -> p k n", p=128)
        DMA("sp", view, src, r=[("wb", name, l, k) for k in range(kc0, kc0 + nkc)], w=[("wbuf", i)])
        return view, ("wbuf", i)

    def xkeys(tok0, N):
        return [("xTd", b) for b in range(tok0 // 128, (tok0 + N + 127) // 128)]

    def load_x(tok0, N):
        DMA("sp", xT[:, :, 0:N], xT_d[:, :, tok0:tok0 + N].rearrange("c p n -> p c n"), r=xkeys(tok0, N), w=["xT"])

    def store_x(tok0, N):
        DMA("sp", xT_d[:, :, tok0:tok0 + N].rearrange("c p n -> p c n"), xT[:, :, 0:N], r=["xT"], w=xkeys(tok0, N))

    def rmsnorm(src, N, gname, goff, dst, sqb, rsb, ks, kd, kq, kr):
        ACT(sqb[:, :, 0:N], src[:, :, 0:N], AF.Square, [ks], kq)
        ps, pk = psum()
        for c in range(8):
            MM(ps[:, 0:N], ones_bf[:], sqb[:, c, 0:N], c == 0, c == 7, kq + ["ones_bf"], [pk])
        ACT(rsb[:, 0:N], ps[:, 0:N], AF.Sqrt, [pk, "cst"], [kr], bias=C("eps"), scale=1.0 / D)
        RCP(rsb[:, 0:N], rsb[:, 0:N], [kr], [kr])
        for c in range(8):
            STT("dve", dst[:, c, 0:N], src[:, c, 0:N], C(gname, goff + c, goff + c + 1), rsb[:, 0:N],
                ALU.mult, ALU.mult, [ks, kr, "cst"], [(kd, c)])

    HK = [("hT", c) for c in range(8)]
    ident32 = C("ident")
    xi = [0]

    def in_transpose(src_rows, nrows, col):
        i = xi[0] % 2
        xi[0] += 1
        xb = xin[i]
        DMA("sp", xb[0:nrows], src_rows.rearrange("p (a b) -> p a b", a=2), w=xinK[i])
        for half in range(2):
            ps, pk = psum()
            for cc in range(4):
                c = half * 4 + cc
                TR(ps[:, cc * 128:cc * 128 + nrows], xb[0:nrows, c // 4, (c % 4) * 128:(c % 4 + 1) * 128], ident32[0:nrows, 0:nrows],
                   xinK[i] + ["cst"], [pk])
            ACT(xT[:, half * 4:half * 4 + 4, col:col + nrows], ps[:, :].rearrange("p (c n) -> p c n", c=4)[:, :, 0:nrows], AF.Copy,
                [pk], ["xT"])

    for t in range(SEG // TT):
        for bi in range(TT // 128):
            r0 = t * TT + bi * 128
            in_transpose(xp[r0:r0 + 128, :], 128, bi * 128)
        store_x(t * TT, TT)
    in_transpose(xs[:, :], NS, 0)
    store_x(SEG, NS)

    FT_TILES = [(t * TT, TT) for t in range(SEG // TT)] + [(SEG, NS)]
    MX_TILES = [(t * MT, MT) for t in range(SEG // MT)]
    MX_SAMPLE = [(SEG, 16), (SEG + 16, 16)]

    def ret_tile(l, tok0, N, mode):
        sample = tok0 >= SEG
        units = [(s * 8, 8) for s in range(2)] if sample else [(u * 128, 128) for u in range(N // 128)]
        w2d = wb_ret_in[l]
        load_x(tok0, N)
        rmsnorm(xT, N, "nmix", l * 8, hT, sq, rstd, "xT", "hT", SQK, "rstd")
        DMA("sp", csA[:, :, 0:N], ropeA_d[:, :, tok0:tok0 + N], w=["csA"])
        cos = csA[:, 0, 0:N]
        sin = csA[:, 1, 0:N]

        def proj_rope(blk, dstT, dkey):
            wv, wk = wload("ret_in", l, w2d, 0, 8, blk * 512, 512)
            for oc in range(4):
                ps, pk = psum()
                for kc in range(8):
                    MM(ps[:, 0:N], wv[:, kc, oc * 128:(oc + 1) * 128], hT[:, kc, 0:N], kc == 0, kc == 7, [wk] + HK, [pk])
                ACT(pre[:, oc, 0:N], ps[:, 0:N], AF.Copy, [pk], [("pre", oc)])
            for hh in range(2):
                c1, c2 = 2 * hh, 2 * hh + 1
                o1 = (blk % 2) * 4 + c1
                o2 = o1 + 1
                TTn("pool", rt[0][:, 0:N], pre[:, c1, 0:N], cos, ALU.mult, [("pre", c1), "csA"], [("rt", 0)])
                TTn("pool", rt[1][:, 0:N], pre[:, c2, 0:N], sin, ALU.mult, [("pre", c2), "csA"], [("rt", 1)])
                TTn("dve", dstT[:, o1, 0:N], rt[0][:, 0:N], rt[1][:, 0:N], ALU.subtract, [("rt", 0), ("rt", 1)], [(dkey, o1)])
                TTn("pool", rt[2][:, 0:N], pre[:, c2, 0:N], cos, ALU.mult, [("pre", c2), "csA"], [("rt", 2)])
                TTn("pool", rt[3][:, 0:N], pre[:, c1, 0:N], sin, ALU.mult, [("pre", c1), "csA"], [("rt", 3)])
                TTn("dve", dstT[:, o2, 0:N], rt[2][:, 0:N], rt[3][:, 0:N], ALU.add, [("rt", 2), ("rt", 3)], [(dkey, o2)])

        if mode == "full":
            proj_rope(0, qT, "qT")
            proj_rope(1, qT, "qT")
        proj_rope(2, kT, "kT")
        proj_rope(3, kT, "kT")
        for vb in range(4):
            wv, wk = wload("ret_in", l, w2d, 0, 8, 2048 + vb * 512, 512)
            for ui, (c0, L) in enumerate(units):
                ps, pk = psum()
                for kc in range(8):
                    MM(ps[0:L, :], hT[:, kc, c0:c0 + L], wv[:, kc, :], kc == 0, kc == 7, [wk] + HK, [pk])
                ACT(vtok[0:L, ui, vb * 512:(vb + 1) * 512], ps[0:L, :], AF.Copy, [pk], [("vtok", ui, vb)])
        if mode == "full":
            for gb in range(4):
                wv, wk = wload("ret_in", l, w2d, 0, 8, 4096 + gb * 512, 512)
                for oc in range(4):
                    ps, pk = psum()
                    for kc in range(8):
                        MM(ps[:, 0:N], wv[:, kc, oc * 128:(oc + 1) * 128], hT[:, kc, 0:N], kc == 0, kc == 7, [wk] + HK, [pk])
                    ACT(big[:, gb * 4 + oc, 0:N], ps[:, 0:N], AF.Silu, [pk], [sgK(gb * 4 + oc)])
        kdn = "kdec8" if sample else "kdec128"
        SK = [("S32", i) for i in range(8)]
        BK = [("Sbf", i) for i in range(8)]
        for ui, (c0, L) in enumerate(units):
            gl = [math.exp(LOGG[h] * L) for h in range(4)]
            if sample:
                s = (tok0 - SEG) // 8 + ui
                DMA("sp", S32[:], st_ret[l, s].rearrange("h (c p) e -> p (h c) e", p=128), w=SK)
                ACT(Sbf[:], S32[:], AF.Copy, SK, BK)
            pb, pbk = psumb()
            for dc in range(8):
                TR(pb[0:L, dc * 128:(dc + 1) * 128], kT[:, dc, c0:c0 + L], ident_bf[:], [("kT", dc), "ident_bf"], [pbk])
            for h in range(4):
                ACT(ktok[0:L, ui, h * 256:(h + 1) * 256], pb[0:L, h * 256:(h + 1) * 256], AF.Copy, [pbk, "cst"], [("ktok", ui, h)],
                    scale=C(kdn, h, h + 1)[0:L, :])
            for h in range(4):
                if mode == "full":
                    ps, pk = psum()
                    for dc in range(2):
                        MM(ps[0:L, 0:L], kT[:, 2 * h + dc, c0:c0 + L], qT[:, 2 * h + dc, c0:c0 + L], dc == 0, dc == 1,
                           [("kT", 2 * h + dc), ("qT", 2 * h + dc)], [pk])
                    TTn("dve", sTt[0:L, h, 0:L], ps[0:L, 0:L], C("decT", h * 128, h * 128 + L)[0:L, :], ALU.mult, [pk, "cst"], [("sT", h)])
                    TTn("pool", qd[:, 2 * h:2 * h + 2, 0:L], qT[:, 2 * h:2 * h + 2, c0:c0 + L],
                        C("qdec", h * 128, h * 128 + L).unsqueeze(1).to_broadcast([128, 2, L]), ALU.mult,
                        [("qT", 2 * h), ("qT", 2 * h + 1), "cst"], [("qd", h)])
                    po, pok = psum()
                    po3 = po[:, :].rearrange("p (e n) -> p e n", e=4)
                    for ec in range(4):
                        MM(po3[:, ec, 0:L], vtok[0:L, ui, h * 512 + ec * 128:h * 512 + (ec + 1) * 128], sTt[0:L, h, 0:L], True, False,
                           [("vtok", ui, h), ("sT", h)], [pok])
                        for dc in range(2):
                            MM(po3[:, ec, 0:L], Sbf[:, 2 * h + dc, ec * 128:(ec + 1) * 128], qd[:, 2 * h + dc, 0:L], False, dc == 1,
                               [("Sbf", 2 * h + dc), ("qd", h)], [pok])
                    ob = h % 2
                    ACT(osq[ob][:, :, 0:L], po3[:, :, 0:L], AF.Square, [pok], [("osq", ob)])
                    pr, prk = psum()
                    for ec in range(4):
                        MM(pr[:, 0:L], ones_bf[:], osq[ob][:, ec, 0:L], ec == 0, ec == 3, [("osq", ob), "ones_bf"], [prk])
                    ACT(orstd[ob][:, 0:L], pr[:, 0:L], AF.Sqrt, [prk, "cst"], [("orstd", ob)], bias=C("eps"), scale=1.0 / 512)
                    RCP(orstd[ob][:, 0:L], orstd[ob][:, 0:L], [("orstd", ob)], [("orstd", ob)])
                    TTn("dve", otmp[ob][:, :, 0:L], po3[:, :, 0:L], big[:, 4 * h:4 * h + 4, c0:c0 + L], ALU.mult,
                        [pok] + [sgK(4 * h + i) for i in range(4)], [("otmp", ob)])
                    TTn("pool", big[:, 16 + 4 * h:16 + 4 * h + 4, c0:c0 + L], otmp[ob][:, :, 0:L],
                        orstd[ob][:, 0:L].unsqueeze(1).to_broadcast([128, 4, L]), ALU.mult,
                        [("otmp", ob), ("orstd", ob)], [ogK(4 * h + i) for i in range(4)])
                for dc in range(2):
                    idx = 2 * h + dc
                    pS, pSk = psum()
                    MM(pS[:, :], ktok[0:L, ui, idx * 128:(idx + 1) * 128], vtok[0:L, ui, h * 512:(h + 1) * 512], True, True,
                       [("ktok", ui, h), ("vtok", ui, h)], [pSk])
                    STT("dve", S32[:, idx, :], S32[:, idx, :], float(gl[h]), pS[:, :], ALU.mult, ALU.add, [pSk, ("S32", idx)], [("S32", idx)])
                    if mode == "full" and not sample:
                        ACT(Sbf[:, idx, :], S32[:, idx, :], AF.Copy, [("S32", idx)], [("Sbf", idx)])
            if sample:
                DMA("sp", o_ret_s[l, s].rearrange("h (c p) e -> p (h c) e", p=128), S32[:], r=SK)
        if mode == "full":
            w2o = wb_ret_out[l]
            for ob_ in range(4):
                wv, wk = wload("ret_out", l, w2o, 0, 16, ob_ * 256, 256)
                for o2 in range(2):
                    oc = ob_ * 2 + o2
                    ps, pk = psum()
                    for kc in range(16):
                        MM(ps[:, 0:N], wv[:, kc, o2 * 128:(o2 + 1) * 128], big[:, 16 + kc, 0:N], kc == 0, kc == 15, [wk, ogK(kc)], [pk])
                    TTn("dve", xT[:, oc, 0:N], xT[:, oc, 0:N], ps[:, 0:N], ALU.add, [pk, "xT"], ["xT"])
            store_x(tok0, N)

    SK8 = [("S32", i) for i in range(8)]
    BK8 = [("Sbf", i) for i in range(8)]

    import os
    KCUT = int(os.environ.get("KCUT", "99"))

    def ret_layer(l):
        P.op("pool", lambda e: e.memset(S32[:], 0.0), (), SK8)
        for (tok0, N) in MX_TILES:
            ret_tile(l, tok0, N, "state")
        if KCUT <= 1:
            return
        for hf in range(2):
            DMA("sp", sl_in[l][hf].rearrange("p (c e) -> p c e", c=4), S32[:, 4 * hf:4 * hf + 4, :], r=SK8, w=[("sl_in", l, hf)])
            allgather(sl_in[l][hf], sl_all[l][hf], [("sl_in", l, hf)], [("sl_all", l, hf)])
        P.op("pool", lambda e: e.memset(S32[:], 0.0), (), SK8)
        for r in range(4):
            for hf in range(2):
                DMA("sp", xT[:, 4 * hf:4 * hf + 4, :], sl_all[l][hf][r * 128:(r + 1) * 128, :].rearrange("p (c e) -> p c e", c=4),
                    r=[("sl_all", l, hf)], w=["xT"])
            for h in range(4):
                STT("dve", S32[:, 2 * h:2 * h + 2, :], xT[:, 2 * h:2 * h + 2, :], C("scoef", r * 4 + h, r * 4 + h + 1),
                    S32[:, 2 * h:2 * h + 2, :], ALU.mult, ALU.add,
                    ["xT", "cst", ("S32", 2 * h), ("S32", 2 * h + 1)], [("S32", 2 * h), ("S32", 2 * h + 1)])
        ACT(Sbf[:], S32[:], AF.Copy, SK8, BK8)
        if KCUT <= 2:
            return
        for (tok0, N) in MX_TILES:
            ret_tile(l, tok0, N, "full")
        if KCUT <= 3:
            return
        DMA("sp", o_ret_p[l].rearrange("h (c p) e -> p (h c) e", p=128), S32[:], r=SK8)
        for (tok0, N) in MX_SAMPLE:
            ret_tile(l, tok0, N, "full")

    def ffn_layer(l):
        ps_n[0] = 6
        DMA("sp", xh[:], xT_d[:, :, SEG - 2:SEG].rearrange("c p n -> p c n"), r=xkeys(SEG - 128, 128), w=["xh"])
        DMA("sp", hx_in[l].rearrange("p (c n) -> p c n", c=8), xh[:], r=["xh"], w=[("hx_in", l)])
        allgather(hx_in[l], hx_all[l], [("hx_in", l)], [("hx_all", l)])
        DMA("sp", xh4[:], hx_all[l].rearrange("(r p) n -> p r n", p=128), r=[("hx_all", l)], w=["xh4"])
        xhf = xh[:].rearrange("p c n -> p (c n)")
        TS("dve", xhf, xh4[:, 0, :], C("hcoef", 0, 1), None, ALU.mult, None, ["xh4", "cst", "xh"], ["xh"])
        for r in range(1, 4):
            STT("dve", xhf, xh4[:, r, :], C("hcoef", r, r + 1), xhf, ALU.mult, ALU.add, ["xh4", "cst", "xh"], ["xh"])
        rmsnorm(xh, 2, "nffn", l * 8, hTh, sqh, rstdh, "xh", "hTh", ["sqh"], "rstdh")
        HHK = [("hTh", c) for c in range(8)]
        DMA("sp", halo_s[:], st_conv[l], w=[("halo_s", i) for i in range(NFC)])
        w2i = wb_ffn_in[l]
        w2o = wb_ffn_out[l]
        for ti, (tok0, N) in enumerate(FT_TILES):
            sample = tok0 >= SEG
            load_x(tok0, N)
            rmsnorm(xT, N, "nffn", l * 8, hT, sq, rstd, "xT", "hT", SQK, "rstd")
            for cb in range(6):
                ncol = 512 if cb < 5 else 256
                wu, wuk = wload("ffn_in", l, w2i, 0, 8, cb * 512, ncol)
                wg, wgk = wload("ffn_in", l, w2i, 0, 8, DFF + cb * 512, ncol)
                for oc in range(ncol // 128):
                    i = cb * 4 + oc
                    if ti == 0:
                        ph, phk = psum()
                        for kc in range(8):
                            MM(ph[:, 0:2], wu[:, kc, oc * 128:(oc + 1) * 128], hTh[:, kc, 0:2], kc == 0, kc == 7, [wuk] + HHK, [phk])
                        ACT(halo_p[:, i, :], ph[:, 0:2], AF.Copy, [phk], [("halo_p", i)])
                    pu, puk = psum()
                    pg, pgk = psum()
                    for kc in range(8):
                        MM(pu[:, 0:N], wu[:, kc, oc * 128:(oc + 1) * 128], hT[:, kc, 0:N], kc == 0, kc == 7, [wuk] + HK, [puk])
                    for kc in range(8):
                        MM(pg[:, 0:N], wg[:, kc, oc * 128:(oc + 1) * 128], hT[:, kc, 0:N], kc == 0, kc == 7, [wgk] + HK, [pgk])
                    ub = i % 3
                    u = pre[:, ub, :]
                    f = rt[ub]
                    uk = ("pre", ub)
                    fk = ("rt", ub)
                    if sample:
                        u3 = u[:, 0:40].rearrange("p (s n) -> p s n", s=4)
                        uv = [u3[:, :, sh:sh + 8] for sh in range(3)]
                        pu_v = pu[:, 0:N].rearrange("p (s n) -> p s n", s=4)
                        pg_v = pg[:, 0:N].rearrange("p (s n) -> p s n", s=4)
                        f_v = f[:, 0:N].rearrange("p (s n) -> p s n", s=4)
                        a_v = big[:, i, 0:N].rearrange("p (s n) -> p s n", s=4)
                        halo_src = halo_s[:, i, :, :]
                        halo_dst = u3[:, :, 0:2]
                        new_halo = u3[:, :, 8:10]
                        hk = ("halo_s", i)
                    else:
                        uv = [u[:, sh:sh + N] for sh in range(3)]
                        pu_v = pu[:, 0:N]
                        pg_v = pg[:, 0:N]
                        f_v = f[:, 0:N]
                        a_v = big[:, i, 0:N]
                        halo_src = halo_p[:, i, :]
                        halo_dst = u[:, 0:2]
                        new_halo = u[:, N:N + 2]
                        hk = ("halo_p", i)
                    cw = [C("cw", (l * 3 + jj) * NFC + i, (l * 3 + jj) * NFC + i + 1) for jj in range(3)]
                    cbias = C("cb", l * NFC + i, l * NFC + i + 1)
                    ACT(uv[2], pu_v, AF.Copy, [puk], [uk])
                    CP("pool", halo_dst, halo_src, [hk], [uk])
                    ACT(f_v, pu_v, AF.Identity, [puk, "cst"], [fk], bias=cbias, scale=cw[2])
                    STT("dve", f_v, uv[1], cw[1], f_v, ALU.mult, ALU.add, [uk, fk, "cst"], [fk])
                    STT("dve", f_v, uv[0], cw[0], f_v, ALU.mult, ALU.add, [uk, fk, "cst"], [fk])
                    CP("pool", halo_src, new_halo, [uk], [hk])
                    ACT(f_v, f_v, AF.Gelu_apprx_tanh, [fk], [fk])
                    TTn("dve", a_v, f_v, pg_v, ALU.mult, [fk, pgk], [actK(i)])
            for cbk in range(4):
                pss = [psum() for _ in range(2)]
                for kh in range(2):
                    wv, wk = wload("ffn_out", l, w2o, kh * 11, 11, cbk * 256, 256)
                    for oc in range(2):
                        ps, pk = pss[oc]
                        for kc in range(11):
                            kk = kh * 11 + kc
                            MM(ps[:, 0:N], wv[:, kc, oc * 128:(oc + 1) * 128], big[:, kk, 0:N], kk == 0, kk == NFC - 1, [wk, actK(kk)], [pk])
                for oc in range(2):
                    ps, pk = pss[oc]
                    og_ = cbk * 2 + oc
                    TTn("dve", xT[:, og_, 0:N], xT[:, og_, 0:N], ps[:, 0:N], ALU.add, [pk, "xT"], ["xT"])
            store_x(tok0, N)
            if ti == len(FT_TILES) - 2:
                DMA("sp", o_conv_p[l], halo_p[:], r=[("halo_p", i) for i in range(NFC)])
        DMA("sp", o_conv_s[l], halo_s[:], r=[("halo_s", i) for i in range(NFC)])

    kvt = [rt[2][:, 0:256], rt[3][:, 0:256], rt[3][:, 256:512]]
    kvtK = [("rt", 2), ("rt", 3), ("rt", 3)]

    def kv_build():
        for (tok0, N) in FT_TILES:
            sample = tok0 >= SEG
            units = [(0, NS)] if sample else [(u * 128, 128) for u in range(N // 128)]
            load_x(tok0, N)
            rmsnorm(xT, N, "kvn", 0, hT, sq, rstd, "xT", "hT", SQK, "rstd")
            wvs = [wload("kv", 0, wb_kv, 0, 8, cb * 512, 512) for cb in range(3)]
            for (c0, L) in units:
                r0 = tok0 + c0
                DMA("sp", csB[0:L, :], ropeB_d[r0:r0 + L, :], w=["csB"])
                for cb in range(3):
                    wv, wk = wvs[cb]
                    ps, pk = psum()
                    for kc in range(8):
                        MM(ps[0:L, :], hT[:, kc, c0:c0 + L], wv[:, kc, :], kc == 0, kc == 7, [wk] + HK, [pk])
                    rf = rt[cb % 2]
                    rk = ("rt", cb % 2)
                    if cb == 0:
                        ACT(rf[0:L, :], ps[0:L, :], AF.Copy, [pk], [rk])
                    else:
                        ACT(rf[0:L, 256:512], ps[0:L, 256:512], AF.Copy, [pk], [rk])
                        ACT(kvt[0][0:L, :], ps[0:L, 0:256], AF.Square, [pk], [kvtK[0]])
                        P.op("dve", lambda e, L=L: e.tensor_reduce(out=kss[0:L, :], in_=kvt[0][0:L, :].rearrange("p (h d) -> p h d", h=4),
                                                                   axis=AX.X, op=ALU.add), [kvtK[0]], ["kss"])
                        ACT(kss[0:L, :], kss[0:L, :], AF.Sqrt, ["kss", "cst"], ["kss"], bias=C("eps")[0:L, :], scale=1.0 / 64)
                        RCP(kss[0:L, :], kss[0:L, :], ["kss"], ["kss"])
                        k3 = kvt[1][0:L, :].rearrange("p (h d) -> p h d", h=4)
                        TTn("dve", k3, ps[0:L, 0:256].rearrange("p (h d) -> p h d", h=4), kss[0:L, :].unsqueeze(2).to_broadcast([L, 4, 64]),
                            ALU.mult, [pk, "kss"], [kvtK[1]])
                        TTn("dve", k3, k3, C("gk", cb * 64, cb * 64 + 64)[0:L, :].unsqueeze(1).to_broadcast([L, 4, 64]), ALU.mult,
                            [kvtK[1], "cst"], [kvtK[1]])
                        cosb = csB[0:L, 0:32].unsqueeze(1).to_broadcast([L, 4, 32])
                        sinb = csB[0:L, 32:64].unsqueeze(1).to_broadcast([L, 4, 32])
                        x1 = k3[:, :, 0:32]
                        x2 = k3[:, :, 32:64]
                        t3 = kvt[2][0:L, :].rearrange("p (h d) -> p h d", h=4)
                        r3 = rf[0:L, 0:256].rearrange("p (h d) -> p h d", h=4)
                        TTn("pool", t3[:, :, 0:32], x1, cosb, ALU.mult, [kvtK[1], "csB"], [kvtK[2]])
                        TTn("pool", t3[:, :, 32:64], x2, sinb, ALU.mult, [kvtK[1], "csB"], [kvtK[2]])
                        TTn("dve", r3[:, :, 0:32], t3[:, :, 0:32], t3[:, :, 32:64], ALU.subtract, [kvtK[2]], [rk])
                        TTn("pool", t3[:, :, 0:32], x2, cosb, ALU.mult, [kvtK[1], "csB", rk], [kvtK[2]])
                        TTn("pool", t3[:, :, 32:64], x1, sinb, ALU.mult, [kvtK[1], "csB"], [kvtK[2]])
                        TTn("dve", r3[:, :, 32:64], t3[:, :, 0:32], t3[:, :, 32:64], ALU.add, [kvtK[2]], [rk])
                    dst = [o_cmp, o_sel, o_win][cb]
                    DMA("sp", dst[r0:r0 + L, :], rf[0:L, :], r=[rk])
                    if sample and cb == 2:
                        for s_ in range(4):
                            DMA("sp", o_wins[s_, 504:512, :], rf[8 * s_:8 * s_ + 8, :], r=[rk])
                            DMA("sp", o_wins[s_, 0:504, :], cwin_d[s_, 8:512, :])
                    ACT(rowb[0:L, cb * 512:(cb + 1) * 512], rf[0:L, :], AF.Copy, [rk], [("rowb", cb)])
                DMA("sp", kv_loc[r0:r0 + L, :], rowb[0:L, :], r=[("rowb", i) for i in range(3)], w=[("kv_loc", r0)])

    def out_y():
        for (tok0, N) in FT_TILES:
            load_x(tok0, N)
            nb = [(0, NS)] if tok0 >= SEG else [(u * 128, 128) for u in range(N // 128)]
            for (c0, L) in nb:
                i = xi[0] % 2
                xi[0] += 1
                xb = xin[i]
                for half in range(2):
                    ps, pk = psum()
                    for cc in range(4):
                        c = half * 4 + cc
                        TR(ps[0:L, cc * 128:(cc + 1) * 128], xT[:, c, c0:c0 + L], ident32, ["xT", "cst"], [pk])
                    ACT(xb[0:L, half, :], ps[0:L, :], AF.Copy, [pk], xinK[i])
                DMA("sp", o_y[tok0 + c0:tok0 + c0 + L, :].rearrange("p (a b) -> p a b", a=2), xb[0:L], r=xinK[i])

    def phase_b_build():
        esB = contextlib.ExitStack()

        def sbB(name, shape, dt):
            return esB.enter_context(nc.sbuf_tensor("sbB_" + name, list(shape), dt))

        cB = sbB("cB", [128, NCB], F32)
        DMA("sp", cB[:], cstB_d, w=["cB"])

        def CB(name, a=None, b=None):
            o, w = CSTB[name]
            if a is None:
                return cB[:, o:o + w]
            return cB[:, o + a:o + b]

        idxp = sbB("idxp", [128, 64], I32)
        DMA("sp", idxp[:], idxp_d, w=["idxp"])
        ptf = sbB("ptf", [128, 256], F32)
        pti = sbB("pti", [128, 256], I32)
        idxs = sbB("idxs", [128, 256], I32)
        DMA("sp", pti[:], ptab_d.rearrange("s r -> (s r)").partition_broadcast(128), w=["pti"])
        CP("dve", ptf[:], pti[:], ["pti"], ["ptf"])
        STT("dve", ptf[:], ptf[:], 128.0, CB("iota").to_broadcast([128, 256]), ALU.mult, ALU.add, ["ptf", "cB"], ["ptf"])
        CP("dve", idxs[:], ptf[:], ["ptf"], ["idxs"])
        gA = [sbB(f"gA{i}", [128, 512], F32) for i in range(3)]
        gB = [sbB(f"gB{i}", [128, 512], F32) for i in range(3)]
        gW = sbB("gW", [128, 512], F32)
        rb = [sbB(f"rb{i}", [128, 1536], BF16) for i in range(3)]
        cT = [[sbB(f"cT{pg}{c}", [128, 2, 2064], BF16) for c in range(2)] for pg in range(2)]
        ktile = [sbB(f"ktile{i}", [128, 2, 128], BF16) for i in range(4)]
        vaug = [sbB(f"vaug{i}", [128, 4, 65], BF16) for i in range(4)]
        cvaug = sbB("cvaug", [128, 4, 65], BF16)
        w1s = sbB("w1s", [128, 2, 32, 128], BF16)
        w2s = sbB("w2s", [128, 2, 64], BF16)
        posT = sbB("posT", [128, 64], F32)
        posTb = sbB("posTb", [128, 64], BF16)
        pbias = sbB("pbias", [128, 2], F32)
        hidg = [sbB(f"hidg{i}", [128, 128], BF16) for i in range(2)]
        csq = sbB("csq", [64, 128], BF16)
        crs = sbB("crs", [64, 128], F32)
        cko = [sbB(f"cko{i}", [64, 128], BF16) for i in range(2)]
        cvo = [sbB(f"cvo{i}", [64, 128], BF16) for i in range(2)]
        for i in range(4):
            P.op("pool", lambda e, i=i: e.memset(vaug[i][:], 1.0), (), [("vaug", i)])
        P.op("pool", lambda e: e.memset(cvaug[:], 1.0), (), ["cvaug"])
        for pg in range(2):
            for c in range(2):
                P.op("pool", lambda e, pg=pg, c=c: e.memset(cT[pg][c][:], 0.0), (), [("cT", pg, c)])
        for half in range(2):
            DMA("pool", w1s[half * 64:(half + 1) * 64], w1_d.rearrange("c (l d) h -> d c l h", d=64), w=["w1s"])
            DMA("sp", posT[half * 64:(half + 1) * 64, :], posT_d, w=["posT"])
        DMA("pool", w2s[:], w2_d.rearrange("c h d -> h c d"), w=["w2s"])
        CP("dve", posTb[:], posT[:], ["posT"], ["posTb"])
        for c in range(2):
            pp, ppk = psum()
            for l_ in range(32):
                MM(pp[:, 0:1], w1s[0:64, c, l_, :], posTb[0:64, c * 32 + l_:c * 32 + l_ + 1], l_ == 0, l_ == 31, ["w1s", "posTb"], [ppk])
            ACT(pbias[:, c:c + 1], pp[:, 0:1], AF.Copy, [ppk], ["pbias"])

        kt_i = [0]

        def compress_chunk(v, ch):
            pg = ch % 2
            for c in range(2):
                for kv in range(4):
                    pair, half = kv // 2, kv % 2
                    rows = slice(half * 64, half * 64 + 64)
                    ph, phk = psum()
                    n_mm = 0
                    for r_ in range(2):
                        for s_ in range(16):
                            o = 16 * r_ + s_
                            MM(ph[:, 0:128], w1s[rows, c, r_ * 16 + s_, :], cT[pg][c][rows, pair, o:o + 16 * 127 + 1:16], n_mm == 0, n_mm == 31,
                               ["w1s", ("cT", pg, c)], [phk])
                            n_mm += 1
                    hb = (c * 4 + kv) % 2
                    ACT(hidg[hb][:], ph[:, 0:128], AF.Gelu_apprx_tanh, [phk, "pbias"], [("hidg", hb)], bias=pbias[:, c:c + 1])
                    po, pok = psum()
                    MM(po[0:64, 0:128], w2s[:, c, :], hidg[hb][:], True, True, ["w2s", ("hidg", hb)], [pok])
                    if c == 0:
                        ACT(csq[:], po[0:64, 0:128], AF.Square, [pok], ["csq"])
                        pr, prk = psum()
                        MM(pr[0:64, 0:128], ones_bf[0:64, 0:64], csq[:], True, True, ["csq", "ones_bf"], [prk])
                        ACT(crs[:], pr[0:64, 0:128], AF.Sqrt, [prk, "cst"], ["crs"], bias=C("eps")[0:64, :], scale=1.0 / 64)
                        RCP(crs[:], crs[:], ["crs"], ["crs"])
                        TTn("dve", crs[:], crs[:], po[0:64, 0:128], ALU.mult, ["crs", pok], ["crs"])
                        TS("dve", cko[kv % 2][:], crs[:], CB("gk0col")[0:64, :], None, ALU.mult, None, ["crs", "cB"], [("cko", kv % 2)])
                        DMA("sp", cmpK_d[v][pair, rows, ch * 128:(ch + 1) * 128], cko[kv % 2][:], r=[("cko", kv % 2)], w=[("cmpK_d", v)])
                    else:
                        ACT(cvo[kv % 2][:], po[0:64, 0:128], AF.Copy, [pok], [("cvo", kv % 2)])
                        pb, pbk = psumb()
                        TR(pb[:, 0:64], cvo[kv % 2][:], ident_bf[0:64, 0:64], [("cvo", kv % 2), "ident_bf"], [pbk])
                        ACT(cvaug[:, kv, 0:64], pb[:, 0:64], AF.Copy, [pbk], ["cvaug"])
                if c == 1:
                    DMA("sp", cmpV_d[v][ch].rearrange("p (k d) -> p k d", k=4), cvaug[:], r=["cvaug"], w=[("cmpV_d", v)])

        def kv_part(v, r, rbt, rbk, c0, Kd, Vd, kcol, vt, L=128):
            i = kt_i[0] % 4
            kt_i[0] += 1
            pb, pbk = psumb()
            for pr_ in range(2):
                TR(pb[:, pr_ * 128:pr_ * 128 + L], rbt[0:L, c0 + pr_ * 128:c0 + (pr_ + 1) * 128], ident_bf[0:L, 0:L], [rbk, "ident_bf"], [pbk])
            ACT(ktile[i][:, :, 0:L], pb[:, 0:256].rearrange("p (a n) -> p a n", a=2)[:, :, 0:L], AF.Copy, [pbk], [("ktile", i)])
            DMA("sp", Kd[:, :, kcol:kcol + L].rearrange("a p n -> p a n"), ktile[i][:, :, 0:L], r=[("ktile", i)], w=[("Kd", id(Kd))])
            CP("pool", vaug[i][0:L, :, 0:64], rbt[0:L, c0 + 256:c0 + 512].rearrange("p (k d) -> p k d", k=4), [rbk], [("vaug", i)])
            DMA("sp", Vd[vt, 0:L, :].rearrange("p (k d) -> p k d", k=4), vaug[i][0:L], r=[("vaug", i)], w=[("Vd", id(Vd))])

        ri = [0]
        for v in range(NV):
            for r in range(64):
                bi = ri[0] % 3
                ri[0] += 1
                rbt = rb[bi]
                rbk = ("rb", bi)
                if v == 0:
                    src = kv_all[(r % 16) // 2]
                    P.dma("pool", lambda e, rbt=rbt, src=src, r=r: e.indirect_dma_start(
                        out=rbt[:], out_offset=None, in_=src,
                        in_offset=bass.IndirectOffsetOnAxis(ap=idxp[:, r:r + 1], axis=0)), ["idxp", "kv_all"], [rbk])
                else:
                    s = v - 1
                    P.dma("pool", lambda e, bi=bi, s=s, r=r: e.indirect_dma_start(
                        out=gA[bi][:], out_offset=None, in_=ccmp_d,
                        in_offset=bass.IndirectOffsetOnAxis(ap=idxs[:, s * 64 + r:s * 64 + r + 1], axis=0)), ["idxs"], [("gA", bi)])
                    P.dma("pool", lambda e, bi=bi, s=s, r=r: e.indirect_dma_start(
                        out=gB[bi][:], out_offset=None, in_=csel_d,
                        in_offset=bass.IndirectOffsetOnAxis(ap=idxs[:, s * 64 + r:s * 64 + r + 1], axis=0)), ["idxs"], [("gB", bi)])
                    ACT(rbt[:, 0:512], gA[bi][:], AF.Copy, [("gA", bi)], [rbk])
                    CP("dve", rbt[:, 512:1024], gB[bi][:], [("gB", bi)], [rbk])
                    if r >= 60:
                        DMA("sp", gW[:], cwin_d[s, (r - 60) * 128:(r - 59) * 128, :], w=["gW"])
                        CP("dve", rbt[:, 1024:1536], gW[:], ["gW"], [rbk])
                ch, off = r // 16, (r % 16) * 128
                pg = ch % 2
                pb, pbk = psumb()
                for q4 in range(4):
                    TR(pb[:, q4 * 128:(q4 + 1) * 128], rbt[:, q4 * 128:(q4 + 1) * 128], ident_bf[:], [rbk, "ident_bf"], [pbk])
                for c in range(2):
                    ACT(cT[pg][c][:, :, off:off + 128], pb[:, c * 256:(c + 1) * 256].rearrange("p (a n) -> p a n", a=2), AF.Copy,
                        [pbk], [("cT", pg, c)])
                    if r % 16 == 0 and r > 0:
                        CP("pool", cT[1 - pg][c][:, :, 2048:2064], cT[pg][c][:, :, 0:16], [("cT", pg, c)], [("cT", 1 - pg, c)])
                if r % 16 == 0 and r > 0:
                    compress_chunk(v, ch - 1)
                if r == 63:
                    for c in range(2):
                        P.op("pool", lambda e, pg=pg, c=c: e.memset(cT[pg][c][:, :, 2048:2064], 0.0), (), [("cT", pg, c)])
                    compress_chunk(v, 3)
                kv_part(v, r, rbt, rbk, 512, selK_d[v], selV_d[v], r * 128, r)
                if v == 0:
                    kv_part(v, r, rbt, rbk, 1024, winK_d[v], winV_d[v], r * 128, r)
                elif r >= 60:
                    kv_part(v, r, rbt, rbk, 1024, winK_d[v], winV_d[v], (r - 60) * 128, r - 60)
            if v >= 1:
                s = v - 1
                bi = ri[0] % 3
                ri[0] += 1
                rbt = rb[bi]
                rbk = ("rb", bi)
                DMA("sp", rbt[0:8, :], kv_loc[SEG + 8 * s:SEG + 8 * s + 8, :], r=[("kv_loc", SEG)], w=[rbk])
                kv_part(v, 64, rbt, rbk, 512, selK_d[v], selV_d[v], 8192, 64, L=8)
                kv_part(v, 64, rbt, rbk, 1024, winK_d[v], winV_d[v], 512, 4, L=8)
        return esB

    def phase_b_attend():
        esC = contextlib.ExitStack()

        def sbC(name, shape, dt):
            return esC.enter_context(nc.sbuf_tensor("sbC_" + name, list(shape), dt))

        cB = sbC("cB", [128, NCB], F32)
        DMA("sp", cB[:], cstB_d, w=["cB"])

        def CB(name, a=None, b=None):
            o, w = CSTB[name]
            if a is None:
                return cB[:, o:o + w]
            return cB[:, o + a:o + b]

        selK = sbC("selK", [128, 2, 8320], BF16)
        selV = sbC("selV", [128, 65, 260], BF16)
        winK = sbC("winK", [128, 2, 640], BF16)
        winV = sbC("winV", [128, 6, 260], BF16)
        cmpK = sbC("cmpK", [128, 2, 512], BF16)
        cmpV = sbC("cmpV", [128, 5, 260], BF16)
        EK = lambda i: ("big", i)
        q8 = lambda c0: big[:, c0:c0 + 2, :].rearrange("p a (b n) -> p (a b) n", b=4)
        QZ = {"c": [(q8(20), [EK(20), EK(21)]), (q8(4), [EK(4), EK(5)])],
              "r": [(q8(22), [EK(22), EK(23)]), (q8(6), [EK(6), EK(7)])]}
        wmap = sbC("wmap", [128, 4, 128], BF16)
        Dtab = CB("Dtab")
        Er = [sbC(f"Er{i}", [128, 128], BF16) for i in range(2)]
        selT4 = sbC("selT4", [128, 4, 128], BF16)
        msk = [sbC(f"msk{i}", [128, 128], BF16) for i in range(4)]
        accs = [rt[2], rt[3]]
        otok = [sbC(f"otok{i}", [128, 4, 65], F32) for i in range(2)]
        o_tok2 = [rt[0][:, :].rearrange("p (h d) -> p h d", d=64), rt[1][:, :].rearrange("p (h d) -> p h d", d=64)]
        o_bf = big[:, 16:18, :].rearrange("p a (h d) -> p (a h) d", d=64)
        oT = big[:, 18:20, :].rearrange("p a (b n) -> p (a b) n", b=4)
        gts = sbC("gts", [128, 48], F32)
        sm = sbC("sm", [128, 64], F32)
        impb = [pre[:, 3, i * 128:(i + 1) * 128] for i in range(3)]
        vmask = pre[:, 2, 0:512].rearrange("p (a n) -> p a n", a=4)
        sel01 = sbC("sel01", [128, 128], BF16)
        negB = sbC("negB", [128, 8], F32)
        mq = sbC("mq", [128, 8], F32)
        CP("dve", wmap[:].rearrange("p a b -> p (a b)"), CB("wmap"), ["cB"], ["wmap"])
        P.op("pool", lambda e: e.memset(winV[:, 5, :], 0.0), (), ["winV"])
        P.op("pool", lambda e: e.memset(cmpV[:, 4, :], 0.0), (), ["cmpV"])
        P.op("pool", lambda e: e.memset(selV[:, 64, :], 0.0), (), ["selV"])
        for i, (nm, a) in enumerate([("gq", 0), ("gq", 64), ("gk", 0), ("gk", 64), ("gk", 128)]):
            P.op("dve", lambda e, i=i, nm=nm, a=a: e.tensor_reduce(out=mq[:, i:i + 1], in_=C(nm, a, a + 64), axis=AX.X, op=ALU.max,
                                                                   apply_absolute_value=True), ["cst", "mq"], ["mq"])
        for j2 in range(2):
            for br in range(3):
                STT("dve", negB[:, j2 * 3 + br:j2 * 3 + br + 1], mq[:, j2:j2 + 1], -8.0, mq[:, 2 + br:3 + br], ALU.mult, ALU.mult,
                    ["mq", "negB"], ["negB"])
        EK = lambda i: ("big", i)
        KEYOF = {id(selK): ("selK", "selV"), id(cmpK): ("cmpK", "cmpV"), id(winK): ("winK", "winV")}
        SC = 0.125

        def load_view(v):
            DMA("sp", selK[:], selK_d[v].rearrange("a p n -> p a n"), r=[("Kd", id(selK_d[v]))], w=["selK"])
            DMA("sp", selV[:], selV_d[v].rearrange("t p n -> p t n"), r=[("Vd", id(selV_d[v]))], w=["selV"])
            DMA("sp", cmpK[:], cmpK_d[v].rearrange("a p n -> p a n"), r=[("cmpK_d", v)], w=["cmpK"])
            DMA("sp", cmpV[:, 0:4, :], cmpV_d[v].rearrange("t p n -> p t n"), r=[("cmpV_d", v)], w=["cmpV"])
            if v >= 1:
                DMA("sp", winK[:], winK_d[v].rearrange("a p n -> p a n"), r=[("Kd", id(winK_d[v]))], w=["winK"])
                DMA("sp", winV[:, 0:5, :], winV_d[v].rearrange("t p n -> p t n"), r=[("Vd", id(winV_d[v]))], w=["winV"])

        def qblock(l, j2, c0, L, v, qi, wq):
            nq = L
            NQ = 4 * nq
            sample = v >= 1
            pq = [psum(), psum()]
            pgt, pgk = psum_acc()
            for hb in range(2):
                for kc in range(8):
                    MM(pq[hb][0][0:L, :], hT[:, kc, c0:c0 + L], wq[hb][0][:, kc, :], kc == 0, kc == 7, [wq[hb][1]] + HK, [pq[hb][1]])
            for kc in range(8):
                MM(pgt[0:L, 0:48], hT[:, kc, c0:c0 + L], wq[2][0][:, kc, :], kc == 0, kc == 7, [wq[2][1]] + HK, [pgk])
            ACT(gts[0:L, :], pgt[0:L, 0:48], AF.Sigmoid, [pgk], ["gts"])
            qsq = pre[0:L, 0:2, 0:512]
            for hb in range(2):
                ACT(qsq[:, hb, :], pq[hb][0][0:L, :], AF.Square, [pq[hb][1]], [("pre", hb)])
            for hb in range(2):
                P.op("dve", lambda e, hb=hb: e.tensor_reduce(out=sm[0:L, 8 + 8 * hb:16 + 8 * hb], in_=qsq[:, hb, :].rearrange("p (h d) -> p h d", d=64),
                                                            axis=AX.X, op=ALU.add), [("pre", hb), "sm"], ["sm"])
            ACT(sm[0:L, 24:40], sm[0:L, 8:24], AF.Sqrt, ["sm", "cst"], ["sm"], bias=C("eps")[0:L, :], scale=1.0 / 64)
            RCP(sm[0:L, 24:40], sm[0:L, 24:40], ["sm"], ["sm"])
            qn = [rt[0], rt[1]]
            qr_ = [rt[2], rt[3]]
            for hb in range(2):
                q3 = qn[hb][0:L, :].rearrange("p (h d) -> p h d", d=64)
                TTn("dve", q3, pq[hb][0][0:L, :].rearrange("p (h d) -> p h d", d=64),
                    sm[0:L, 24 + 8 * hb:32 + 8 * hb].unsqueeze(2).to_broadcast([L, 8, 64]), ALU.mult, [pq[hb][1], "sm"], [("rt", hb)])
                TTn("pool", q3, q3, C("gq", j2 * 64, j2 * 64 + 64)[0:L, :].unsqueeze(1).to_broadcast([L, 8, 64]), ALU.mult,
                    [("rt", hb), "cst"], [("rt", hb)])
                r3 = qr_[hb][0:L, :].rearrange("p (h d) -> p h d", d=64)
                t3 = pre[0:L, 2 + hb, 0:512].rearrange("p (h d) -> p h d", d=64)
                cosb = csB[0:L, 0:32].unsqueeze(1).to_broadcast([L, 8, 32])
                sinb = csB[0:L, 32:64].unsqueeze(1).to_broadcast([L, 8, 32])
                x1, x2 = q3[:, :, 0:32], q3[:, :, 32:64]
                TTn("pool", t3[:, :, 0:32], x1, cosb, ALU.mult, [("rt", hb), "csB"], [("pre", 2 + hb)])
                TTn("pool", t3[:, :, 32:64], x2, sinb, ALU.mult, [("rt", hb), "csB"], [("pre", 2 + hb)])
                TTn("dve", r3[:, :, 0:32], t3[:, :, 0:32], t3[:, :, 32:64], ALU.subtract, [("pre", 2 + hb)], [("rt", 2 + hb)])
                TTn("pool", t3[:, :, 0:32], x2, cosb, ALU.mult, [("rt", hb), "csB", ("rt", 2 + hb)], [("pre", 2 + hb)])
                TTn("pool", t3[:, :, 32:64], x1, sinb, ALU.mult, [("rt", hb), "csB"], [("pre", 2 + hb)])
                TTn("dve", r3[:, :, 32:64], t3[:, :, 0:32], t3[:, :, 32:64], ALU.add, [("pre", 2 + hb)], [("rt", 2 + hb)])
            for ver, (srcs, vn) in enumerate([(qn, "c"), (qr_, "r")]):
                qp = big[0:L, 12 + 2 * ver:14 + 2 * ver, :]
                for pair in range(2):
                    src4 = srcs[pair][0:L, :].rearrange("p (a g d) -> p a g d", a=2, g=4)
                    dst4 = qp[:, pair, :].rearrange("p (g a d) -> p a g d", a=2, g=4)
                    CP("pool" if pair else "dve", dst4, src4, [("rt", 2 * ver + pair)], [EK(12 + 2 * ver + pair)])
                pb, pbk = psumb()
                for pg_ in range(8):
                    TR(pb[:, pg_ * 128:pg_ * 128 + L], qp[:, pg_ // 4, (pg_ % 4) * 128:(pg_ % 4 + 1) * 128], ident_bf[0:L, 0:L],
                       [EK(12 + 2 * ver), EK(13 + 2 * ver), "ident_bf"], [pbk])
                pb8 = pb[:, :].rearrange("p (a n) -> p a n", a=8)
                ACT(QZ[vn][0][0][0:64, :, 0:L], pb8[0:64, :, 0:L], AF.Copy, [pbk], QZ[vn][0][1])
                ACT(QZ[vn][1][0][64:128, :, 0:L], pb8[64:128, :, 0:L], AF.Copy, [pbk], QZ[vn][1][1])
            exr = CB("exrow_s" if sample else "exrow_p")[0:L, :]
            fir = CB("first_s" if sample else "first_p")[0:L, :]
            curv = 128.0 if sample else float(96 + 2 * qi)
            tt = impb[2][0:L, :]
            if sample:
                TS("dve", tt, CB("qrow")[0:L, :], curv, None, ALU.subtract, None, ["cB"], [("pre", 3)])
            else:
                TS("dve", tt, CB("qrow")[0:L, :], CB("hi")[0:L, :], curv, ALU.subtract, ALU.subtract, ["cB"], [("pre", 3)])
            VM = [("pre", 2)]
            val, av, nf, fb = vmask[0:L, 0, :], vmask[0:L, 1, :], vmask[0:L, 2, :], vmask[0:L, 3, :]
            TS("dve", val, tt, 0.0, None, ALU.is_le, None, [("pre", 3)], VM)
            TTn("dve", val, val, exr, ALU.mult, VM + ["cB"], VM)
            TS("dve", av, val, 1e6, -1e6, ALU.mult, ALU.add, VM, VM)
            TS("dve", fb, tt, 0.0, None, ALU.is_equal, None, [("pre", 3)], VM)
            TS("dve", nf, tt, -1.0, None, ALU.is_equal, None, [("pre", 3)], VM)
            TTn("dve", fb, fb, nf, ALU.add, VM, VM)
            TTn("dve", fb, fb, fir, ALU.add, VM + ["cB"], VM)
            TTn("dve", fb, fb, exr, ALU.mult, VM + ["cB"], VM)
            TS("dve", fb, fb, 1.0, None, ALU.min, None, VM, VM)
            TS("dve", nf, fb, -1.0, 1.0, ALU.mult, ALU.add, VM, VM)
            TS("dve", fb, fb, 1e6, None, ALU.mult, None, VM, VM)
            if not sample:
                r0 = 44 + qi
                DMA("sp", winK[:], winK_d[0][:, :, r0 * 128:(r0 + 5) * 128].rearrange("a p n -> p a n"), r=[("Kd", id(winK_d[0]))], w=["winK"])
                DMA("sp", winV[:, 0:5, :], winV_d[0][r0:r0 + 5].rearrange("t p n -> p t n"), r=[("Vd", id(winV_d[0]))], w=["winV"])
            qoff = 0.0 if sample else float(128 * qi)
            basen = "base_s" if sample else "base_p"
            ei = [0]

            def qsel(ver, kv):
                pair, half = kv // 2, kv % 2
                t, keys = QZ[ver][half]
                return t[:, pair * 4:pair * 4 + 4, 0:nq], keys

            def vwide(Vt, t, kv):
                flat = Vt[:, :, :].rearrange("p t n -> p (t n)")
                o = t * 260 + kv * 65
                if not sample:
                    return flat[:, o:o + 128]
                return flat[:, o:o + 65]

            def run_units(specs, D=2):
                def front(i):
                    sp = specs[i]
                    nt = sp.get("nt", 1)
                    nk = sp["nk"]
                    if nt == 2:
                        b0 = 2 * (i % 2)
                        sck = [("psf", b0), ("psf", b0 + 1)]
                        for t in range(2):
                            MM(psf[b0 + t][0:nk, 0:NQ].rearrange("p (g n) -> p g n", g=4), sp["K"][t], sp["q"][0], True, True,
                               [sp["kkey"]] + sp["q"][1], [sck[t]])
                        sc2 = psbig[0:nk, b0 * 512:(b0 + 2) * 512].rearrange("p (t n) -> p t n", t=2)[:, :, 0:NQ]
                        et = big[0:nk, sp["ti"]:sp["ti"] + 2, 0:NQ]
                        ACT(et, sc2, AF.Exp, sck + ["negB"], [EK(sp["ti"]), EK(sp["ti"] + 1)], bias=negB[0:nk, sp["bidx"]:sp["bidx"] + 1], scale=SC)
                    else:
                        sc, sck = psf[i % 4], ("psf", i % 4)
                        MM(sc[0:nk, 0:NQ].rearrange("p (g n) -> p g n", g=4), sp["K"], sp["q"][0], True, True, [sp["kkey"]] + sp["q"][1], [sck])
                        et = big[0:nk, sp["ti"], 0:NQ]
                        ACT(et, sc[0:nk, 0:NQ], AF.Exp, [sck, "negB"], [EK(sp["ti"])], bias=negB[0:nk, sp["bidx"]:sp["bidx"] + 1], scale=SC)
                    return sp["pre"]() if sp.get("pre") else None

                def back(i, aux):
                    sp = specs[i]
                    nk = sp["nk"]
                    if sp.get("nt", 1) == 2:
                        et = big[0:nk, sp["ti"]:sp["ti"] + 2, 0:NQ]
                        eks = [EK(sp["ti"]), EK(sp["ti"] + 1)]
                        sp["mask"](et.rearrange("p t (g n) -> p t g n", g=4), eks, aux)
                        for t in range(2):
                            vw = sp["V"][t]
                            MM(sp["acc"][0][0:vw.shape[1], 0:NQ], vw, big[0:nk, sp["ti"] + t, 0:NQ], sp["first"] and t == 0, sp["last"] and t == 1,
                               [sp["vkey"], eks[t]], [sp["acc"][1]])
                        return
                    et = big[0:nk, sp["ti"], 0:NQ]
                    sp["mask"](et.rearrange("p (g n) -> p g n", g=4), EK(sp["ti"]), aux)
                    vw = sp["V"]
                    MM(sp["acc"][0][0:vw.shape[1], 0:NQ], vw, et, sp["first"], sp["last"], [sp["vkey"], EK(sp["ti"])], [sp["acc"][1]])

                n_ = len(specs)
                pend = [front(k) for k in range(min(D, n_))]
                for i in range(n_):
                    if i + D < n_:
                        pend.append(front(i + D))
                    back(i, pend[i])

            def finish_branch(acc, kv, br, first_branch):
                ai = (kv * 3 + br) % 2
                ACT(accs[ai][0:65, 0:NQ], acc[0][0:65, 0:NQ], AF.Copy, [acc[1]], [("rt", 2 + ai)])
                pt_, ptk = psum()
                for g in range(4):
                    TR(pt_[0:nq, g * 65:(g + 1) * 65], accs[ai][0:65, g * nq:(g + 1) * nq], ident32[0:65, 0:65], [("rt", 2 + ai), "cst"], [ptk])
                ACT(otok[ai][0:nq].rearrange("p g d -> p (g d)"), pt_[0:nq, 0:260], AF.Copy, [ptk], [("otok", ai)])
                TS("dve", sm[0:nq, 0:4], otok[ai][0:nq, :, 64], 1e-30, None, ALU.max, None, [("otok", ai)], ["sm"])
                RCP(sm[0:nq, 0:4], sm[0:nq, 0:4], ["sm"], ["sm"])
                gsl = gts[0:nq, :].rearrange("p (h b) -> p h b", b=3)[:, 4 * kv:4 * kv + 4, br]
                TTn("dve", sm[0:nq, 4:8], sm[0:nq, 0:4], gsl, ALU.mult, ["sm", "gts"], ["sm"])
                dst = o_tok2[kv // 2][0:nq, 4 * (kv % 2):4 * (kv % 2) + 4, :]
                fbc = sm[0:nq, 4:8].unsqueeze(2).to_broadcast([nq, 4, 64])
                if first_branch:
                    TTn("dve", dst, otok[ai][0:nq, :, 0:64], fbc, ALU.mult, [("otok", ai), "sm"], [("rt", kv // 2)])
                else:
                    TTn("dve", otok[ai][0:nq, :, 0:64], otok[ai][0:nq, :, 0:64], fbc, ALU.mult, [("otok", ai), "sm"], [("otok", ai)])
                    TTn("dve", dst, dst, otok[ai][0:nq, :, 0:64], ALU.add, [("otok", ai), ("rt", kv // 2)], [("rt", kv // 2)])
                return ai

            for kv in range(4):
                pair, half = kv // 2, kv % 2
                rows = slice(half * 64, half * 64 + 64)
                acc = psum_acc()
                specs = []
                for c in range(4):
                    def pre_c(c=c):
                        mk = msk[c]
                        TS("dve", mk[:, 0:nq], CB("qrow")[:, 0:nq], qoff, CB(basen, c, c + 1), ALU.add, ALU.is_ge, ["cB"], [("msk", c)])
                        return mk

                    def m_c(e3, ek, mk, c=c):
                        TTn("dve", e3, e3, mk[:, 0:nq].unsqueeze(1).to_broadcast([128, 4, nq]), ALU.mult, [ek, ("msk", c)], [ek])
                    specs.append(dict(K=cmpK[:, pair, c * 128:(c + 1) * 128], q=qsel("c", kv), nk=128, bidx=j2 * 3 + 0, pre=pre_c, mask=m_c,
                                      V=vwide(cmpV, c, kv), acc=acc, first=(c == 0), last=(c == 3), ti=c, kkey="cmpK", vkey="cmpV"))
                run_units(specs)
                ai = finish_branch(acc, kv, 0, True)
                pim, pimk = psum()
                for g in range(4):
                    for c in range(4):
                        MM(pim[0:nq, g * 128:(g + 1) * 128], big[:, c, g * nq:(g + 1) * nq], wmap[:, c, :], c == 0, c == 3, [EK(c), "wmap"], [pimk])
                imp = impb[0][0:L, :]
                TS("dve", imp, pim[0:nq, 0:128], sm[0:nq, 0:1], None, ALU.mult, None, [pimk, "sm"], [("pre", 3)])
                for g in range(1, 4):
                    STT("dve", imp, pim[0:nq, g * 128:(g + 1) * 128], sm[0:nq, g:g + 1], imp, ALU.mult, ALU.add, [pimk, "sm", ("pre", 3)], [("pre", 3)])
                TTn("dve", imp, imp, val, ALU.mult, [("pre", 3)] + VM, [("pre", 3)])
                TTn("dve", imp, imp, av, ALU.add, [("pre", 3)] + VM, [("pre", 3)])
                TTn("dve", imp, imp, nf, ALU.mult, [("pre", 3)] + VM, [("pre", 3)])
                TTn("dve", imp, imp, fb, ALU.add, [("pre", 3)] + VM, [("pre", 3)])
                P.op("dve", lambda e: e.max(sm[0:L, 40:48], imp), [("pre", 3)], ["sm"])
                P.op("dve", lambda e: e.match_replace(impb[1][0:L, :], sm[0:L, 40:48], imp, -3e6), [("pre", 3), "sm"], [("pre", 3)])
                P.op("dve", lambda e: e.max(sm[0:L, 48:56], impb[1][0:L, :]), [("pre", 3)], ["sm"])
                kth = 54 if sample else 55
                TS("dve", impb[1][0:L, :], imp, sm[0:L, kth:kth + 1], None, ALU.is_ge, None, [("pre", 3), "sm"], [("pre", 3)])
                STT("dve", sel01[0:L, :], imp, -5e5, impb[1][0:L, :], ALU.is_gt, ALU.mult, [("pre", 3)], ["sel01"])
                pb, pbk = psumb()
                TR(pb[:, 0:L], sel01[0:L, :], ident_bf[0:L, 0:L], ["sel01", "ident_bf"], [pbk])
                ACT(selT4[:, kv, 0:L], pb[:, 0:L], AF.Copy, [pbk], ["selT4"])
            ntile = 64 if sample else 49 + qi
            for kvp in range(2):
                kvs = (2 * kvp, 2 * kvp + 1)
                pair = kvp
                accS = {kv: (psf[4 + kv % 2], ("psf", 4 + kv % 2)) for kv in kvs}
                specs = []
                pmr = {}
                DT = [0, 2, 8, 10]
                npair = 0 if sample else (ntile - 1) // 2
                for rp in range(npair):
                    r = 2 * rp
                    for kv in kvs:
                        def pre_d(r=r, kv=kv, kvs=kvs):
                            if kv == kvs[0]:
                                pbi = ps_i[1] % 2
                                ps_i[1] += 1
                                for t in range(2):
                                    eb = (r + t) % 2
                                    TS("dve", Er[eb][:], Dtab, float(2 * (r + t)), None, ALU.is_equal, None, ["cB"], [("Er", eb)])
                                    MM(psb32[pbi][:, t * 2 * nq:(t + 1) * 2 * nq].rearrange("p (k n) -> p k n", k=2), Er[eb][:],
                                       selT4[:, kvs[0]:kvs[0] + 2, 0:nq], True, True, [("Er", eb), "selT4"], [("psb", pbi)])
                                pmr[("d", r)] = (psb32[pbi][:, 0:4 * nq].rearrange("p (t k n) -> p t k n", t=2, k=2), ("psb", pbi))
                            return pmr[("d", r)]

                        def m_d(e4, eks, aux, kv=kv):
                            pm4, pmk = aux
                            TTn("dve", e4, e4, pm4[:, :, kv % 2, :].unsqueeze(2).to_broadcast([128, 2, 4, nq]), ALU.mult, eks + [pmk], eks)
                        specs.append(dict(nt=2, K=[selK[:, pair, (r + t) * 128:(r + t + 1) * 128] for t in range(2)], q=qsel("r", kv), nk=128,
                                          bidx=j2 * 3 + 1, pre=pre_d, mask=m_d, V=[vwide(selV, r + t, kv) for t in range(2)], acc=accS[kv],
                                          first=(r == 0), last=False, ti=DT[len(specs) % 4], kkey="selK", vkey="selV"))
                for r in range(2 * npair, ntile):
                    for kv in kvs:
                        def pre_s(r=r, kv=kv, kvs=kvs):
                            if kv == kvs[0]:
                                eb = r % 2
                                TS("dve", Er[eb][:], Dtab, float(2 * r), None, ALU.is_equal, None, ["cB"], [("Er", eb)])
                                pbi = ps_i[1] % 2
                                ps_i[1] += 1
                                pm2 = psb32[pbi][:, 0:2 * nq].rearrange("p (k n) -> p k n", k=2)
                                MM(pm2, Er[eb][:], selT4[:, kvs[0]:kvs[0] + 2, 0:nq], True, True, [("Er", eb), "selT4"], [("psb", pbi)])
                                pmr[r] = (pm2, ("psb", pbi))
                            return pmr[r]

                        def m_s(e3, ek, aux, r=r, kv=kv):
                            pm2, pmk = aux
                            if (not sample) and r == 48 + qi:
                                mi = kv % 2
                                TTn("dve", msk[mi][:, 0:nq], pm2[:, kv % 2, :], CB("tri_le")[:, 0:nq], ALU.mult, [pmk, "cB"], [("msk", mi)])
                                TTn("dve", e3, e3, msk[mi][:, 0:nq].unsqueeze(1).to_broadcast([128, 4, nq]), ALU.mult, [ek, ("msk", mi)], [ek])
                            else:
                                TTn("dve", e3, e3, pm2[:, kv % 2, :].unsqueeze(1).to_broadcast([128, 4, nq]), ALU.mult, [ek, pmk], [ek])
                        specs.append(dict(K=selK[:, pair, r * 128:(r + 1) * 128], q=qsel("r", kv), nk=128, bidx=j2 * 3 + 1, pre=pre_s, mask=m_s,
                                          V=vwide(selV, r, kv), acc=accS[kv], first=(r == 0), last=(r == ntile - 1 and not sample),
                                          ti=(DT[len(specs) % 4] if not sample else 8 + len(specs) % 4), kkey="selK", vkey="selV"))
                if sample:
                    for kv in kvs:
                        def m_n(e3, ek, aux):
                            TTn("dve", e3, e3, CB("tri_le")[0:8, 0:nq].unsqueeze(1).to_broadcast([8, 4, nq]), ALU.mult, [ek, "cB"], [ek])
                        specs.append(dict(K=selK[:, pair, 8192:8200], q=qsel("r", kv), nk=8, bidx=j2 * 3 + 1, pre=None, mask=m_n,
                                          V=selV[0:8, 64, kv * 65:(kv + 1) * 65], acc=accS[kv], first=False, last=True,
                                          ti=8 + len(specs) % 4, kkey="selK", vkey="selV"))
                run_units(specs)
                for kv in kvs:
                    finish_branch(accS[kv], kv, 1, False)
                specs = []
                for kv in kvs:
                    for w in range(5):
                        if sample and w == 4:
                            def m_w(e3, ek, aux):
                                TTn("dve", e3, e3, CB("tri_le")[0:8, 0:nq].unsqueeze(1).to_broadcast([8, 4, nq]), ALU.mult, [ek, "cB"], [ek])
                            specs.append(dict(K=winK[:, pair, 512:520], q=qsel("r", kv), nk=8, bidx=j2 * 3 + 2, pre=None, mask=m_w,
                                              V=winV[0:8, 4, kv * 65:(kv + 1) * 65], acc=accS[kv], first=False, last=True,
                                              ti=8 + len(specs) % 4, kkey="winK", vkey="winV"))
                            continue

                        def m_w(e3, ek, aux, w=w):
                            if sample:
                                if w == 0:
                                    TTn("dve", e3, e3, CB("tri_gt")[:, 0:nq].unsqueeze(1).to_broadcast([128, 4, nq]), ALU.mult, [ek, "cB"], [ek])
                                return
                            exc = CB("extile", 44 + qi + w, 45 + qi + w)
                            if w == 0 or w == 4:
                                tri = CB("tri_gt" if w == 0 else "tri_le")[:, 0:nq].unsqueeze(1).to_broadcast([128, 4, nq])
                                STT("dve", e3, e3, exc, tri, ALU.mult, ALU.mult, [ek, "cB"], [ek])
                            else:
                                TS("dve", e3, e3, exc, None, ALU.mult, None, [ek, "cB"], [ek])
                        specs.append(dict(K=winK[:, pair, w * 128:(w + 1) * 128], q=qsel("r", kv), nk=128, bidx=j2 * 3 + 2, pre=None, mask=m_w,
                                          V=vwide(winV, w, kv), acc=accS[kv], first=(w == 0), last=(w == 4 and not sample),
                                          ti=8 + len(specs) % 4, kkey="winK", vkey="winV"))
                run_units(specs)
                for kv in kvs:
                    finish_branch(accS[kv], kv, 2, False)
            for hb in range(2):
                CP("dve", o_bf[0:L, 8 * hb:8 * hb + 8, :], o_tok2[hb][0:L], [("rt", hb)], [EK(16 + hb)])
            pb, pbk = psumb()
            for c in range(8):
                TR(pb[:, c * 128:c * 128 + L], o_bf[0:L, 2 * c:2 * c + 2, :].rearrange("p h d -> p (h d)"), ident_bf[0:L, 0:L], [EK(16), EK(17), "ident_bf"], [pbk])
            ACT(oT[:, :, 0:L], pb[:, :].rearrange("p (a n) -> p a n", a=8)[:, :, 0:L], AF.Copy, [pbk], [EK(18), EK(19)])

        def wo_apply(l, j2, c0, L):
            for ob_ in range(2):
                wv, wk = wload("wo", j2, wb_o[j2], 0, 8, ob_ * 512, 512)
                for o4 in range(4):
                    oc = ob_ * 4 + o4
                    ps, pk = psum()
                    for kc in range(8):
                        MM(ps[:, 0:L], wv[:, kc, o4 * 128:(o4 + 1) * 128], oT[:, kc, 0:L], kc == 0, kc == 7, [wk, EK(18), EK(19)], [pk])
                    TTn("dve", xT[:, oc, c0:c0 + L], xT[:, oc, c0:c0 + L], ps[:, 0:L], ALU.add, [pk, "xT"], ["xT"])

        def nsa_layer(l):
            j2 = l - 2
            ps_n[0] = 2
            for vn in ("c", "r"):
                P.op("pool", lambda e, vn=vn: e.memset(QZ[vn][0][0][64:128], 0.0), (), QZ[vn][0][1])
                P.op("pool", lambda e, vn=vn: e.memset(QZ[vn][1][0][0:64], 0.0), (), QZ[vn][1][1])
            load_view(0)
            for ti, (tok0, N) in enumerate(FT_TILES):
                sample = tok0 >= SEG
                load_x(tok0, N)
                rmsnorm(xT, N, "nmix", l * 8, hT, sq, rstd, "xT", "hT", SQK, "rstd")
                def wqs():
                    return [wload("wqg", j2, wb_qg[j2], 0, 8, 0, 512), wload("wqg", j2, wb_qg[j2], 0, 8, 512, 512),
                            wload("wqg", j2, wb_qg[j2], 0, 8, 1024, 48)]
                if not sample:
                    for qb in range(4):
                        qi = ti * 4 + qb
                        DMA("sp", csB[:, :], ropeB_d[tok0 + qb * 128:tok0 + (qb + 1) * 128, :], w=["csB"])
                        qblock(l, j2, qb * 128, 128, 0, qi, wqs())
                        wo_apply(l, j2, qb * 128, 128)
                else:
                    for s in range(4):
                        load_view(1 + s)
                        DMA("sp", csB[0:8, :], ropeB_d[tok0 + 8 * s:tok0 + 8 * s + 8, :], w=["csB"])
                        qblock(l, j2, 8 * s, 8, 1 + s, 0, wqs())
                        wo_apply(l, j2, 8 * s, 8)
                store_x(tok0, N)

        return esC, nsa_layer

    nl = min(stage, 2)
    for l in range(nl):
        ret_layer(l)
        if KCUT <= 4:
            break
        ffn_layer(l)
    if stage >= 3:
        kv_build()
    if stage >= 4:
        for ch in range(8):
            allgather(kv_loc[256 * ch:256 * (ch + 1), :], kv_all[ch], [("kv_loc", 256 * ch + 128 * i) for i in range(2)], ["kv_all"])
        P.barrier()
        P.flush()
        esA.close()
        esB = phase_b_build()
        P.barrier()
        P.flush()
        esB.close()
        if stage >= 5:
            esC, nsa_layer = phase_b_attend()
            for l in range(2, min(stage - 3, 4)):
                nsa_layer(l)
                ffn_layer(l)
    out_y()
    P.finish()
    print("ops:", {e: P.cnt[e] for e in P.engs})
    return nc, es


def make_in_maps(inp):
    maps = []
    for c in range(8):
        b, j = c // 4, c % 4
        cst, ropeA, ropeB = host_tables(c, inp)
        sc = inp["state_conv"][:, 4 * c:4 * c + 4]
        sc = sc.reshape(4, 4, 2, NFC, 128).transpose(0, 4, 3, 1, 2)
        m = {
            "xp": np.ascontiguousarray(inp["x_prompt"][b, j * SEG:(j + 1) * SEG]),
            "xs": np.ascontiguousarray(inp["x_sample"][4 * c:4 * c + 4].reshape(NS, D)),
            "cst": cst, "ropeA": ropeA, "ropeB": ropeB,
            "ret_w_in": inp["ret_w_in"], "ret_w_out": inp["ret_w_out"],
            "ffn_w_in": inp["ffn_w_in"], "ffn_w_out": inp["ffn_w_out"], "kv_w": inp["kv_w"],
            "state_ret": np.ascontiguousarray(inp["state_ret"][:, 4 * c:4 * c + 4]),
            "state_conv": np.ascontiguousarray(sc),
            "cache_cmp": inp["cache_cmp_kv"].reshape(-1, 512), "cache_sel": inp["cache_sel_kv"].reshape(-1, 512),
            "cache_win": np.ascontiguousarray(inp["cache_win_kv"][4 * c:4 * c + 4].reshape(4, 512, 512)),
            "ptab": np.ascontiguousarray(inp["page_table"][4 * c:4 * c + 4]).astype(np.int32),
            "cmp_w1": inp["cmp_w1"], "cmp_w2": inp["cmp_w2"],
            "posT": np.ascontiguousarray(inp["cmp_pos"].transpose(2, 0, 1).reshape(64, 64)),
            "nsa_w_qg": inp["nsa_w_qg"], "nsa_w_o": inp["nsa_w_o"],
        }
        m["cstB"], m["idxp"] = host_tables_b(c, inp)
        maps.append(m)
    return maps


_CACHE = {}


def run_device(inp, stage=99):
    inp = {k: np.asarray(v) for k, v in inp.items()}
    if stage not in _CACHE:
        _CACHE[stage] = build_program(stage)
    nc, es = _CACHE[stage]
    res = run_bass_kernel_spmd(nc, make_in_maps(inp), core_ids=list(range(8)))
    return res.results


def kernel(**inputs):
    res = run_device(inputs)
    f32 = np.float32
    y_p = np.zeros((2, 8192, D), f32)
    y_s = np.zeros((32, 8, D), f32)
    ret_p = np.zeros((2, 2, 4, 256, 512), f32)
    ret_s = np.zeros((2, 32, 4, 256, 512), f32)
    conv_p = np.zeros((4, 2, 2, DFF), f32)
    conv_s = np.zeros((4, 32, 2, DFF), f32)
    cmp_p = np.zeros((2, 8192, 2, 4, 64), f32)
    cmp_s = np.zeros((32, 8, 2, 4, 64), f32)
    sel_p = np.zeros((2, 8192, 2, 4, 64), f32)
    sel_s = np.zeros((32, 8, 2, 4, 64), f32)
    win_p = np.zeros((2, 512, 2, 4, 64), f32)
    win_s = np.zeros((32, 512, 2, 4, 64), f32)
    for c in range(8):
        b, j = c // 4, c % 4
        r = res[c]
        sl = slice(j * SEG, (j + 1) * SEG)
        y_p[b, sl] = r["o_y"][:SEG]
        y_s[4 * c:4 * c + 4] = r["o_y"][SEG:].reshape(4, 8, D)
        ret_s[:, 4 * c:4 * c + 4] = r["o_ret_s"]
        conv_s[:, 4 * c:4 * c + 4] = r["o_conv_s"].transpose(0, 3, 4, 2, 1).reshape(4, 4, 2, DFF)
        cmp_p[b, sl] = r["o_cmp"][:SEG].reshape(SEG, 2, 4, 64)
        sel_p[b, sl] = r["o_sel"][:SEG].reshape(SEG, 2, 4, 64)
        cmp_s[4 * c:4 * c + 4] = r["o_cmp"][SEG:].reshape(4, 8, 2, 4, 64)
        sel_s[4 * c:4 * c + 4] = r["o_sel"][SEG:].reshape(4, 8, 2, 4, 64)
        win_s[4 * c:4 * c + 4] = r["o_wins"].reshape(4, 512, 2, 4, 64)
        if j == 3:
            ret_p[:, b] = r["o_ret_p"]
            conv_p[:, b] = r["o_conv_p"].transpose(0, 3, 2, 1).reshape(4, 2, DFF)
            win_p[b] = r["o_win"][SEG - 512:SEG].reshape(512, 2, 4, 64)
    return (y_p, y_s, ret_p, ret_s, conv_p, conv_s, cmp_p, cmp_s, sel_p, sel_s, win_p, win_s)
```

```python
import contextlib
import math
import numpy as np
import ml_dtypes
import concourse.bass as bass
import concourse.mybir as mybir
from concourse.bass_utils import run_bass_kernel_spmd

F32 = mybir.dt.float32
BF16 = mybir.dt.bfloat16
I32 = mybir.dt.int32
AF = mybir.ActivationFunctionType
ALU = mybir.AluOpType
AX = mybir.AxisListType

D = 1024
SEG = 2048
NS = 32
NT = SEG + NS
TT = 512
DFF = 2816
NFC = 22
EPS = 1e-6
GAM = [1.0 - 2.0 ** (-5.0 - h) for h in range(4)]
LOGG = [math.log1p(-(2.0 ** (-5.0 - h))) for h in range(4)]
SEMW = 30000
SIMMODE = False


class Prog:
    def __init__(self, nc, es):
        self.nc = nc
        self.es = es
        self.engs = ["pe", "act", "dve", "pool", "sp"]
        self.ops = {e: [] for e in self.engs}
        self.cnt = {e: 0 for e in self.engs}
        self.psems = {e: [] for e in self.engs}
        self.waited_c = {e: {x: 0 for x in self.engs} for e in self.engs}
        self.waited_d = {e: {} for e in self.engs}
        self.bufs = {}
        self.NDS = 12
        self.dq = ["sp", "pool", "act"]
        self.dsems = {q: [es.enter_context(nc.semaphore(f"d_{q}_{i}")) for i in range(self.NDS)] for q in self.dq}
        self.dval = {(q, i): 0 for q in self.dq for i in range(self.NDS)}
        self.dlast = {(q, i): None for q in self.dq for i in range(self.NDS)}
        self.dcnt = {q: 0 for q in self.dq}
        self.nps = 0

    def _psem(self, e, n):
        w = (n - 1) // SEMW
        while len(self.psems[e]) <= w:
            self.psems[e].append(self.es.enter_context(self.nc.semaphore(f"p_{e}_{len(self.psems[e])}")))
        return self.psems[e][w], (n - 1) % SEMW + 1

    def _need(self, e, tok, waits):
        if tok is None:
            return
        if tok[0] == "c":
            _, x, n = tok
            if x == "pe" and e == "pe":
                return
            if self.waited_c[e][x] >= n:
                return
            self.waited_c[e][x] = n
            waits.append(self._psem(x, n))
        else:
            _, q, i, v = tok
            if self.waited_d[e].get((q, i), 0) >= v:
                return
            self.waited_d[e][(q, i)] = v
            waits.append((self.dsems[q][i], v))

    def _deps(self, e, reads, writes, waits):
        for k in reads:
            b = self.bufs.get(k)
            if b:
                for t in b[0]:
                    self._need(e, t, waits)
        for k in writes:
            b = self.bufs.get(k)
            if b:
                for t in b[0]:
                    self._need(e, t, waits)
                for t in b[1].values():
                    self._need(e, t, waits)

    def _upd(self, tok, reads, writes, dma=False):
        rk = (tok[0], tok[1]) if tok[0] == "c" else (tok[0], tok[1], tok[2])
        for k in reads:
            b = self.bufs.setdefault(k, [[], {}])
            b[1][rk] = tok
        for k in writes:
            self.bufs[k] = [[tok], {}]

    def op(self, e, fn, reads=(), writes=()):
        waits = []
        self._deps(e, reads, writes, waits)
        n = self.cnt[e] + 1
        self.cnt[e] = n
        tok = ("c", e, n)
        self.ops[e].append((waits, fn, (self._psem(e, n)[0], 1)))
        self._upd(tok, reads, writes)
        return tok

    def dma(self, q, fn, reads=(), writes=()):
        waits = []
        self._deps(q, reads, writes, waits)
        i = self.dcnt[q] % self.NDS
        self.dcnt[q] += 1
        self._need(q, self.dlast[(q, i)], waits)
        v = self.dval[(q, i)] + 16
        self.dval[(q, i)] = v
        tok = ("d", q, i, v)
        self.dlast[(q, i)] = tok
        self.ops[q].append((waits, fn, (self.dsems[q][i], 16)))
        self._upd(tok, reads, writes)
        return tok

    def special(self, q, fn, sem, reads=(), writes=()):
        waits = []
        self._deps(q, reads, writes, waits)
        self.ops[q].append((waits, fn, (sem, 1)))
        tok = ("s", sem)
        return tok

    def wait_special(self, e, sem):
        self.ops[e].append(([(sem, 1)], None, None))

    def barrier(self):
        for e in self.engs:
            waits = []
            for x in self.engs:
                if self.cnt[x] > 0:
                    self._need(e, ("c", x, self.cnt[x]), waits)
            for q in self.dq:
                for i in range(self.NDS):
                    self._need(e, self.dlast[(q, i)], waits)
            self.ops[e].append((waits, None, None))
        self.bufs = {}

    def finish(self):
        waits = []
        for q in self.dq:
            for i in range(self.NDS):
                self._need("sp", self.dlast[(q, i)], waits)
        self.ops["sp"].append((waits, None, None))
        self.flush()

    def flush(self):
        nc = self.nc
        ops = self.ops
        self.ops = {e: [] for e in self.engs}

        def emit(E, e):
            for waits, fn, inc in ops[E]:
                for s, v in waits:
                    e.wait_ge(s, v)
                if fn is not None:
                    ins = fn(e)
                    ins.then_inc(inc[0], inc[1])

        with nc.Block() as block:
            @block.tensor
            def _(e):
                emit("pe", e)

            @block.scalar
            def _(e):
                emit("act", e)

            @block.vector
            def _(e):
                emit("dve", e)

            @block.gpsimd
            def _(e):
                emit("pool", e)

            @block.sync
            def _(e):
                emit("sp", e)


CST = {}
_off = 0
for _n, _w in [("decT", 512), ("qdec", 512), ("kdec128", 4), ("kdec8", 4), ("ident", 128), ("ones", 128),
               ("nmix", 32), ("nffn", 32), ("kvn", 8), ("cw", 4 * 3 * NFC), ("cb", 4 * NFC),
               ("scoef", 16), ("hcoef", 4), ("eps", 1), ("gk", 3 * 64), ("gq", 2 * 64)]:
    CST[_n] = (_off, _w)
    _off += _w
NCST = _off


CSTB = {}
_off = 0
for _n, _w in [("iota", 1), ("qrow", 128), ("tri_le", 128), ("tri_gt", 128), ("hi", 1), ("exrow_p", 128), ("first_p", 128),
               ("exrow_s", 128), ("first_s", 128), ("base_p", 4), ("base_s", 4), ("extile", 64), ("wmap", 512), ("gk0col", 1),
               ("Dtab", 128)]:
    CSTB[_n] = (_off, _w)
    _off += _w
NCB = _off
NV = 5


def host_tables_b(c, inp):
    j = c % 4
    k3 = 3 - j
    t = np.zeros((128, NCB), np.float32)

    def put(name, arr):
        o, w = CSTB[name]
        t[:, o:o + w] = np.asarray(arr, np.float32).reshape(128, w)

    p = np.arange(128)
    put("iota", p[:, None])
    put("qrow", np.broadcast_to(np.arange(128)[None, :], (128, 128)))
    put("tri_le", (p[:, None] <= p[None, :]))
    put("tri_gt", (p[:, None] > p[None, :]))
    put("hi", (p >= 64)[:, None])
    blk = np.arange(128)
    put("exrow_p", np.broadcast_to((blk >= 32 * k3)[None, :], (128, 128)))
    put("first_p", np.broadcast_to((blk == 32 * k3)[None, :], (128, 128)))
    put("exrow_s", np.ones((128, 128)))
    put("first_s", np.broadcast_to((blk == 0)[None, :], (128, 128)))
    n = (np.arange(4)[None, :] * 128 + p[:, None])
    ex = (16 * n >= 2048 * k3) & (n <= 510)
    put("base_p", np.where(ex, 16.0 * n + 31 - 6144, 1e9))
    put("base_s", np.where(n <= 510, -1.0, 1e9))
    put("extile", np.broadcast_to((np.arange(64) >= 16 * k3)[None, :], (128, 64)))
    sb_ = np.arange(128)[None, None, :]
    nn = n[:, :, None]
    ov = np.maximum(0, np.minimum(16 * nn + 32, 64 * sb_ + 64) - np.maximum(16 * nn, 64 * sb_)) / 32.0
    put("wmap", ov)
    put("gk0col", inp["kv_knorm"][0][p % 64][:, None])
    put("Dtab", p[:, None] - (p[None, :] >= 64))
    idx = np.zeros((128, 64), np.int32)
    for r in range(64):
        seg = r // 16 - k3
        tl = r % 16
        idx[:, r] = p if seg < 0 else seg * 256 + (tl % 2) * 128 + p
    return t, idx


def host_tables(c, inp):
    j = c % 4
    cst = np.zeros((128, NCST), np.float32)

    def put(name, arr):
        o, w = CST[name]
        cst[:, o:o + w] = np.asarray(arr, np.float32).reshape(128, w)

    m = np.arange(128)[:, None]
    l = np.arange(128)[None, :]
    decT = np.zeros((128, 4, 128), np.float64)
    qdec = np.zeros((128, 4, 128), np.float64)
    kd128 = np.zeros((128, 4), np.float64)
    kd8 = np.zeros((128, 4), np.float64)
    for h in range(4):
        decT[:, h, :] = np.where(l >= m, np.exp(np.maximum(l - m, 0) * LOGG[h]), 0.0) / 16.0
        qdec[:, h, :] = np.exp((l + 1.0) * LOGG[h])
        kd128[:, h] = np.exp((127.0 - np.arange(128)) * LOGG[h]) / 16.0
        kd8[:, h] = np.exp((7.0 - np.minimum(np.arange(128), 7)) * LOGG[h]) / 16.0
    put("decT", decT)
    put("qdec", qdec)
    put("kdec128", kd128)
    put("kdec8", kd8)
    put("eps", np.full((128, 1), EPS))
    put("ident", np.eye(128))
    put("ones", np.ones((128, 128)))
    put("nmix", inp["norm_mix"].reshape(4, 8, 128).transpose(2, 0, 1))
    put("nffn", inp["norm_ffn"].reshape(4, 8, 128).transpose(2, 0, 1))
    put("kvn", inp["kv_norm"].reshape(8, 128).T)
    put("cw", inp["ffn_conv_w"].reshape(4, 3, NFC, 128).transpose(3, 0, 1, 2))
    put("cb", inp["ffn_conv_b"].reshape(4, NFC, 128).transpose(2, 0, 1))
    sc = np.zeros((4, 4), np.float64)
    for r in range(4):
        if r < j:
            for h in range(4):
                sc[r, h] = math.exp(LOGG[h] * SEG * (j - 1 - r))
    put("scoef", np.broadcast_to(sc.reshape(1, 16), (128, 16)))
    hc = np.zeros(4)
    if j > 0:
        hc[j - 1] = 1.0
    put("hcoef", np.broadcast_to(hc.reshape(1, 4), (128, 4)))
    put("gk", np.broadcast_to(inp["kv_knorm"].reshape(1, 192), (128, 192)))
    put("gq", np.broadcast_to(inp["nsa_qnorm"].reshape(1, 128), (128, 128)))
    pos = np.concatenate([SEG * j + np.arange(SEG), 8192 + (np.arange(NS) % 8)]).astype(np.float32)
    invA = np.exp(-math.log(10000.0) * np.arange(128, dtype=np.float32) / 128).astype(np.float32)
    angA = (invA[:, None] * pos[None, :]).astype(np.float32)
    ropeA = np.stack([np.cos(angA), np.sin(angA)], 1).astype(np.float32)
    invB = np.exp(-math.log(10000.0) * np.arange(32, dtype=np.float32) / 32).astype(np.float32)
    angB = (pos[:, None] * invB[None, :]).astype(np.float32)
    ropeB = np.concatenate([np.cos(angB), np.sin(angB)], 1).astype(np.float32)
    return cst, ropeA, ropeB


def build_program(stage=99):
    nc = bass.Bass("TRN2", target_bir_lowering=False)
    es = contextlib.ExitStack()
    P = Prog(nc, es)

    def din(name, shape, dt=F32):
        return nc.dram_tensor(name, list(shape), dt, kind="ExternalInput").ap()

    def dout(name, shape, dt=F32):
        return nc.dram_tensor(name, list(shape), dt, kind="ExternalOutput").ap()

    def dscr(name, shape, dt):
        return nc.dram_tensor(name, list(shape), dt).ap()

    def sb(name, shape, dt):
        return es.enter_context(nc.sbuf_tensor("sb_" + name, list(shape), dt))

    def MM(out, lhsT, rhs, start, stop, r, w):
        P.op("pe", lambda e: e.matmul(out, lhsT, rhs, start=start, stop=stop), r, w)

    def TR(out, in_, ident, r, w):
        P.op("pe", lambda e: e.transpose(out, in_, ident), r, w)

    def ACT(out, in_, func, r, w, bias=None, scale=None):
        kw = {}
        if bias is not None:
            kw["bias"] = bias
        if scale is not None:
            kw["scale"] = scale
        P.op("act", lambda e: e.activation(out, in_, func, **kw), r, w)

    def TTn(eng, out, a, b, op, r, w):
        P.op(eng, lambda e: e.tensor_tensor(out, a, b, op), r, w)

    def TS(eng, out, a, s1, s2, op0, op1, r, w):
        if op1 is None:
            P.op(eng, lambda e: e.tensor_scalar(out, a, s1, None, op0=op0), r, w)
        else:
            P.op(eng, lambda e: e.tensor_scalar(out, a, s1, s2, op0=op0, op1=op1), r, w)

    def STT(eng, out, in0, scalar, in1, op0, op1, r, w):
        P.op(eng, lambda e: e.scalar_tensor_tensor(out=out, in0=in0, scalar=scalar, in1=in1, op0=op0, op1=op1), r, w)

    def RCP(out, in_, r, w):
        P.op("dve", lambda e: e.reciprocal(out, in_), r, w)

    def CP(eng, out, in_, r, w):
        P.op(eng, lambda e: e.tensor_copy(out, in_), r, w)

    def DMA(q, out, in_, r=(), w=()):
        P.dma(q, lambda e: e.dma_start(out=out, in_=in_), r, w)

    xp = din("xp", [SEG, D])
    xs = din("xs", [NS, D])
    cst_d = din("cst", [128, NCST])
    ropeA_d = din("ropeA", [128, 2, NT])
    ropeB_d = din("ropeB", [NT, 64])
    w_ret_in = din("ret_w_in", [2, D, 6 * D])
    w_ret_out = din("ret_w_out", [2, 2 * D, D])
    w_ffn_in = din("ffn_w_in", [4, D, 2 * DFF])
    w_ffn_out = din("ffn_w_out", [4, DFF, D])
    w_kv = din("kv_w", [D, 1536])
    st_ret = din("state_ret", [2, 4, 4, 256, 512])
    st_conv = din("state_conv", [4, 128, NFC, 4, 2])
    o_ret_p = dout("o_ret_p", [2, 4, 256, 512])
    o_ret_s = dout("o_ret_s", [2, 4, 4, 256, 512])
    o_conv_p = dout("o_conv_p", [4, 128, NFC, 2])
    o_conv_s = dout("o_conv_s", [4, 128, NFC, 4, 2])
    o_cmp = dout("o_cmp", [NT, 512])
    o_sel = dout("o_sel", [NT, 512])
    o_win = dout("o_win", [NT, 512])
    o_y = dout("o_y", [NT, D])
    xT_d = dscr("xT_d", [8, 128, NT], F32)
    wb_ret_in = dscr("wb_ret_in", [2, D, 6 * D], BF16)
    wb_ret_out = dscr("wb_ret_out", [2, 2 * D, D], BF16)
    wb_ffn_in = dscr("wb_ffn_in", [4, D, 2 * DFF], BF16)
    wb_ffn_out = dscr("wb_ffn_out", [4, DFF, D], BF16)
    wb_kv = dscr("wb_kv", [D, 1536], BF16)
    sl_in = [[dscr(f"sl_in{l}_{hf}", [128, 2048], F32) for hf in range(2)] for l in range(2)]
    sl_all = [[dscr(f"sl_all{l}_{hf}", [4 * 128, 2048], F32) for hf in range(2)] for l in range(2)]
    hx_in = [dscr(f"hx_in{l}", [128, 16], F32) for l in range(4)]
    hx_all = [dscr(f"hx_all{l}", [4 * 128, 16], F32) for l in range(4)]
    kv_loc = dscr("kv_loc", [NT, 1536], BF16)

    NPOOLR = 2560 * 128
    ccmp_d = din("cache_cmp", [NPOOLR, 512])
    csel_d = din("cache_sel", [NPOOLR, 512])
    cwin_d = din("cache_win", [4, 512, 512])
    ptab_d = din("ptab", [4, 64], I32)
    w1_d = din("cmp_w1", [2, 2048, 128])
    w2_d = din("cmp_w2", [2, 128, 64])
    posT_d = din("posT", [64, 64])
    w_qg = din("nsa_w_qg", [2, D, 1072])
    w_o = din("nsa_w_o", [2, D, D])
    cstB_d = din("cstB", [128, NCB])
    idxp_d = din("idxp", [128, 64], I32)
    o_wins = dout("o_wins", [4, 512, 512])
    wb_qg = dscr("wb_qg", [2, D, 1072], BF16)
    wb_o = dscr("wb_o", [2, D, D], BF16)
    kv_all = [dscr(f"kv_all{i}", [1024, 1536], BF16) for i in range(8)]
    selK_d = [dscr(f"selK{v}", [2, 128, 8320], BF16) for v in range(NV)]
    selV_d = [dscr(f"selV{v}", [65, 128, 260], BF16) for v in range(NV)]
    winK_d = [dscr(f"winK{v}", [2, 128, 8192 if v == 0 else 640], BF16) for v in range(NV)]
    winV_d = [dscr(f"winV{v}", [64 if v == 0 else 5, 128, 260], BF16) for v in range(NV)]
    cmpK_d = [dscr(f"cmpK{v}", [2, 128, 512], BF16) for v in range(NV)]
    cmpV_d = [dscr(f"cmpV{v}", [4, 128, 260], BF16) for v in range(NV)]

    MT = 256
    esA = contextlib.ExitStack()

    def sbA(name, shape, dt):
        return esA.enter_context(nc.sbuf_tensor("sbA_" + name, list(shape), dt))
    cst = sb("cst", [128, NCST], F32)
    ident_bf = sb("ident_bf", [128, 128], BF16)
    ones_bf = sb("ones_bf", [128, 128], BF16)
    xT = sb("xT", [128, 8, TT], F32)
    hT = sb("hT", [128, 8, TT], BF16)
    rstd = sb("rstd", [128, TT], F32)
    NWB = 4
    wbuf = [sb(f"wbuf{i}", [128, 4096], BF16) for i in range(NWB)]
    pre = sb("pre", [128, 4, TT + 8], F32)
    rt = [sb(f"rt{i}", [128, TT], F32) for i in range(4)]
    big = sb("big", [128, 32, TT], BF16)
    halo_p = sb("halo_p", [128, NFC, 2], F32)
    halo_s = sb("halo_s", [128, NFC, 4, 2], F32)
    xh = sb("xh", [128, 8, 2], F32)
    xh4 = sb("xh4", [128, 4, 16], F32)
    hTh = sb("hTh", [128, 8, 2], BF16)
    sqh = sb("sqh", [128, 8, 2], BF16)
    rstdh = sb("rstdh", [128, 2], F32)
    rowb = sb("rowb", [128, 1536], BF16)
    kss = sb("kss", [128, 4], F32)
    csB = sb("csB", [128, 64], F32)
    csA = sbA("csA", [128, 2, MT], F32)
    qT = sbA("qT", [128, 8, MT], BF16)
    kT = sbA("kT", [128, 8, MT], BF16)
    vtok = sbA("vtok", [128, 2, 2048], BF16)
    ktok = sbA("ktok", [128, 2, 1024], BF16)
    S32 = sbA("S32", [128, 8, 512], F32)
    Sbf = sbA("Sbf", [128, 8, 512], BF16)
    sTt = sbA("sTt", [128, 4, 128], BF16)
    qd = sbA("qd", [128, 8, 128], BF16)
    osq = [sbA(f"osq{i}", [128, 4, 128], BF16) for i in range(2)]
    orstd = [sbA(f"orstd{i}", [128, 128], F32) for i in range(2)]
    otmp = [sbA(f"otmp{i}", [128, 4, 128], F32) for i in range(2)]
    sq = big[:, 24:32, :]
    SQK = [("big", 24 + c) for c in range(8)]
    sgK = lambda i: ("big", i)
    ogK = lambda i: ("big", 16 + i)
    actK = lambda i: ("big", i)
    xin = [pre[:, 0:2, 0:512], pre[:, 2:4, 0:512]]
    xinK = [[("pre", 0), ("pre", 1)], [("pre", 2), ("pre", 3)]]
    print("sbuf remaining", nc.sbuf_bytes_remaining)

    psbig = es.enter_context(nc.psum_tensor("psbig", [128, 2048], F32))
    psf = [psbig[:, i * 512:(i + 1) * 512] for i in range(4)] + \
          [es.enter_context(nc.psum_tensor(f"psf{i}", [128, 512], F32))[:, :] for i in range(4, 6)]
    psb32 = [es.enter_context(nc.psum_tensor(f"psb{i}", [128, 512], F32)) for i in range(2)]
    psb = [t[:, :].bitcast(BF16) for t in psb32]
    ps_i = [0, 0]

    ps_n = [6]

    def psum():
        i = ps_i[0] % ps_n[0]
        ps_i[0] += 1
        return psf[i], ("psf", i)

    acc_i = [0]

    def psum_acc():
        i = 4 + acc_i[0] % 2
        acc_i[0] += 1
        return psf[i], ("psf", i)

    def psumb():
        i = ps_i[1] % 2
        ps_i[1] += 1
        return psb[i], ("psb", i)

    def C(name, a=None, b=None):
        o, w = CST[name]
        if a is None:
            return cst[:, o:o + w]
        return cst[:, o + a:o + b]

    cc_sems = [es.enter_context(nc.semaphore(f"cc{i}")) for i in range(20)]
    cc_i = [0]
    GROUPS = [[0, 1, 2, 3], [4, 5, 6, 7]]

    def allgather(src, dst, rkeys, wkeys):
        if SIMMODE:
            rows = src.shape[0]
            for r in range(4):
                DMA("sp", dst[r * rows:(r + 1) * rows, :], src, r=rkeys, w=wkeys)
            return
        sem = cc_sems[cc_i[0]]
        cc_i[0] += 1
        P.special("pool", lambda e: e.collective_compute("AllGather", ALU.bypass, replica_groups=GROUPS,
                                                         ins=[src], outs=[dst]), sem, reads=rkeys, writes=wkeys)
        for q in ("pool", "sp"):
            P.wait_special(q, sem)

    DMA("sp", cst[:], cst_d, w=["cst"])
    CP("dve", ident_bf[:], C("ident"), ["cst"], ["ident_bf"])
    CP("dve", ones_bf[:], C("ones"), ["cst"], ["ones_bf"])

    def cast_w(name, l, src2d, dst2d, rows, step=512):
        for r in range(0, rows, step):
            rr = min(step, rows - r)
            DMA("pool", dst2d[r:r + rr, :], src2d[r:r + rr, :],
                w=[("wb", name, l, k) for k in range(r // 128, (r + rr + 127) // 128)])

    def casts_for_layer(l):
        if l < 2:
            cast_w("ret_in", l, w_ret_in[l], wb_ret_in[l], D)
            cast_w("ret_out", l, w_ret_out[l], wb_ret_out[l], 2 * D)
        cast_w("ffn_in", l, w_ffn_in[l], wb_ffn_in[l], D)
        cast_w("ffn_out", l, w_ffn_out[l], wb_ffn_out[l], DFF, step=1408)

    casts_for_layer(0)
    casts_for_layer(1)
    cast_w("kv", 0, w_kv, wb_kv, D)
    if stage >= 4:
        for j2 in range(2):
            cast_w("wqg", j2, w_qg[j2], wb_qg[j2], D)
            cast_w("wo", j2, w_o[j2], wb_o[j2], D)
            casts_for_layer(2 + j2)

    wb_i = [0]

    def wload(name, l, dram2d, kc0, nkc, col0, ncols):
        i = wb_i[0] % NWB
        wb_i[0] += 1
        view = wbuf[i][:, 0:nkc * ncols].rearrange("p (k n) -> p k n", k=nkc)
        src = dram2d[kc0 * 128:(kc0 + nkc) * 128, col0:col0 + ncols].rearrange("(k p) n -> p k n", p=128)
        DMA("sp", view, src, r=[("wb", name, l, k) for k in range(kc0, kc0 + nkc)], w=[("wbuf", i)])
        return view, ("wbuf", i)

    def xkeys(tok0, N):
        return [("xTd", b) for b in range(tok0 // 128, (tok0 + N + 127) // 128)]

    def load_x(tok0, N):
        DMA("sp", xT[:, :, 0:N], xT_d[:, :, tok0:tok0 + N].rearrange("c p n -> p c n"), r=xkeys(tok0, N), w=["xT"])

    def store_x(tok0, N):
        DMA("sp", xT_d[:, :, tok0:tok0 + N].rearrange("c p n -> p c n"), xT[:, :, 0:N], r=["xT"], w=xkeys(tok0, N))

    def rmsnorm(src, N, gname, goff, dst, sqb, rsb, ks, kd, kq, kr):
        ACT(sqb[:, :, 0:N], src[:, :, 0:N], AF.Square, [ks], kq)
        ps, pk = psum()
        for c in range(8):
            MM(ps[:, 0:N], ones_bf[:], sqb[:, c, 0:N], c == 0, c == 7, kq + ["ones_bf"], [pk])
        ACT(rsb[:, 0:N], ps[:, 0:N], AF.Sqrt, [pk, "cst"], [kr], bias=C("eps"), scale=1.0 / D)
        RCP(rsb[:, 0:N], rsb[:, 0:N], [kr], [kr])
        for c in range(8):
            STT("dve", dst[:, c, 0:N], src[:, c, 0:N], C(gname, goff + c, goff + c + 1), rsb[:, 0:N],
                ALU.mult, ALU.mult, [ks, kr, "cst"], [(kd, c)])

    HK = [("hT", c) for c in range(8)]
    ident32 = C("ident")
    xi = [0]

    def in_transpose(src_rows, nrows, col):
        i = xi[0] % 2
        xi[0] += 1
        xb = xin[i]
        DMA("sp", xb[0:nrows], src_rows.rearrange("p (a b) -> p a b", a=2), w=xinK[i])
        for half in range(2):
            ps, pk = psum()
            for cc in range(4):
                c = half * 4 + cc
                TR(ps[:, cc * 128:cc * 128 + nrows], xb[0:nrows, c // 4, (c % 4) * 128:(c % 4 + 1) * 128], ident32[0:nrows, 0:nrows],
                   xinK[i] + ["cst"], [pk])
            ACT(xT[:, half * 4:half * 4 + 4, col:col + nrows], ps[:, :].rearrange("p (c n) -> p c n", c=4)[:, :, 0:nrows], AF.Copy,
                [pk], ["xT"])

    for t in range(SEG // TT):
        for bi in range(TT // 128):
            r0 = t * TT + bi * 128
            in_transpose(xp[r0:r0 + 128, :], 128, bi * 128)
        store_x(t * TT, TT)
    in_transpose(xs[:, :], NS, 0)
    store_x(SEG, NS)

    FT_TILES = [(t * TT, TT) for t in range(SEG // TT)] + [(SEG, NS)]
    MX_TILES = [(t * MT, MT) for t in range(SEG // MT)]
    MX_SAMPLE = [(SEG, 16), (SEG + 16, 16)]

    def ret_tile(l, tok0, N, mode):
        sample = tok0 >= SEG
        units = [(s * 8, 8) for s in range(2)] if sample else [(u * 128, 128) for u in range(N // 128)]
        w2d = wb_ret_in[l]
        load_x(tok0, N)
        rmsnorm(xT, N, "nmix", l * 8, hT, sq, rstd, "xT", "hT", SQK, "rstd")
        DMA("sp", csA[:, :, 0:N], ropeA_d[:, :, tok0:tok0 + N], w=["csA"])
        cos = csA[:, 0, 0:N]
        sin = csA[:, 1, 0:N]

        def proj_rope(blk, dstT, dkey):
            wv, wk = wload("ret_in", l, w2d, 0, 8, blk * 512, 512)
            for oc in range(4):
                ps, pk = psum()
                for kc in range(8):
                    MM(ps[:, 0:N], wv[:, kc, oc * 128:(oc + 1) * 128], hT[:, kc, 0:N], kc == 0, kc == 7, [wk] + HK, [pk])
                ACT(pre[:, oc, 0:N], ps[:, 0:N], AF.Copy, [pk], [("pre", oc)])
            for hh in range(2):
                c1, c2 = 2 * hh, 2 * hh + 1
                o1 = (blk % 2) * 4 + c1
                o2 = o1 + 1
                TTn("pool", rt[0][:, 0:N], pre[:, c1, 0:N], cos, ALU.mult, [("pre", c1), "csA"], [("rt", 0)])
                TTn("pool", rt[1][:, 0:N], pre[:, c2, 0:N], sin, ALU.mult, [("pre", c2), "csA"], [("rt", 1)])
                TTn("dve", dstT[:, o1, 0:N], rt[0][:, 0:N], rt[1][:, 0:N], ALU.subtract, [("rt", 0), ("rt", 1)], [(dkey, o1)])
                TTn("pool", rt[2][:, 0:N], pre[:, c2, 0:N], cos, ALU.mult, [("pre", c2), "csA"], [("rt", 2)])
                TTn("pool", rt[3][:, 0:N], pre[:, c1, 0:N], sin, ALU.mult, [("pre", c1), "csA"], [("rt", 3)])
                TTn("dve", dstT[:, o2, 0:N], rt[2][:, 0:N], rt[3][:, 0:N], ALU.add, [("rt", 2), ("rt", 3)], [(dkey, o2)])

        if mode == "full":
            proj_rope(0, qT, "qT")
            proj_rope(1, qT, "qT")
        proj_rope(2, kT, "kT")
        proj_rope(3, kT, "kT")
        for vb in range(4):
            wv, wk = wload("ret_in", l, w2d, 0, 8, 2048 + vb * 512, 512)
            for ui, (c0, L) in enumerate(units):
                ps, pk = psum()
                for kc in range(8):
                    MM(ps[0:L, :], hT[:, kc, c0:c0 + L], wv[:, kc, :], kc == 0, kc == 7, [wk] + HK, [pk])
                ACT(vtok[0:L, ui, vb * 512:(vb + 1) * 512], ps[0:L, :], AF.Copy, [pk], [("vtok", ui, vb)])
        if mode == "full":
            for gb in range(4):
                wv, wk = wload("ret_in", l, w2d, 0, 8, 4096 + gb * 512, 512)
                for oc in range(4):
                    ps, pk = psum()
                    for kc in range(8):
                        MM(ps[:, 0:N], wv[:, kc, oc * 128:(oc + 1) * 128], hT[:, kc, 0:N], kc == 0, kc == 7, [wk] + HK, [pk])
                    ACT(big[:, gb * 4 + oc, 0:N], ps[:, 0:N], AF.Silu, [pk], [sgK(gb * 4 + oc)])
        kdn = "kdec8" if sample else "kdec128"
        SK = [("S32", i) for i in range(8)]
        BK = [("Sbf", i) for i in range(8)]
        for ui, (c0, L) in enumerate(units):
            gl = [math.exp(LOGG[h] * L) for h in range(4)]
            if sample:
                s = (tok0 - SEG) // 8 + ui
                DMA("sp", S32[:], st_ret[l, s].rearrange("h (c p) e -> p (h c) e", p=128), w=SK)
                ACT(Sbf[:], S32[:], AF.Copy, SK, BK)
            pb, pbk = psumb()
            for dc in range(8):
                TR(pb[0:L, dc * 128:(dc + 1) * 128], kT[:, dc, c0:c0 + L], ident_bf[:], [("kT", dc), "ident_bf"], [pbk])
            for h in range(4):
                ACT(ktok[0:L, ui, h * 256:(h + 1) * 256], pb[0:L, h * 256:(h + 1) * 256], AF.Copy, [pbk, "cst"], [("ktok", ui, h)],
                    scale=C(kdn, h, h + 1)[0:L, :])
            for h in range(4):
                if mode == "full":
                    ps, pk = psum()
                    for dc in range(2):
                        MM(ps[0:L, 0:L], kT[:, 2 * h + dc, c0:c0 + L], qT[:, 2 * h + dc, c0:c0 + L], dc == 0, dc == 1,
                           [("kT", 2 * h + dc), ("qT", 2 * h + dc)], [pk])
                    TTn("dve", sTt[0:L, h, 0:L], ps[0:L, 0:L], C("decT", h * 128, h * 128 + L)[0:L, :], ALU.mult, [pk, "cst"], [("sT", h)])
                    TTn("pool", qd[:, 2 * h:2 * h + 2, 0:L], qT[:, 2 * h:2 * h + 2, c0:c0 + L],
                        C("qdec", h * 128, h * 128 + L).unsqueeze(1).to_broadcast([128, 2, L]), ALU.mult,
                        [("qT", 2 * h), ("qT", 2 * h + 1), "cst"], [("qd", h)])
                    po, pok = psum()
                    po3 = po[:, :].rearrange("p (e n) -> p e n", e=4)
                    for ec in range(4):
                        MM(po3[:, ec, 0:L], vtok[0:L, ui, h * 512 + ec * 128:h * 512 + (ec + 1) * 128], sTt[0:L, h, 0:L], True, False,
                           [("vtok", ui, h), ("sT", h)], [pok])
                        for dc in range(2):
                            MM(po3[:, ec, 0:L], Sbf[:, 2 * h + dc, ec * 128:(ec + 1) * 128], qd[:, 2 * h + dc, 0:L], False, dc == 1,
                               [("Sbf", 2 * h + dc), ("qd", h)], [pok])
                    ob = h % 2
                    ACT(osq[ob][:, :, 0:L], po3[:, :, 0:L], AF.Square, [pok], [("osq", ob)])
                    pr, prk = psum()
                    for ec in range(4):
                        MM(pr[:, 0:L], ones_bf[:], osq[ob][:, ec, 0:L], ec == 0, ec == 3, [("osq", ob), "ones_bf"], [prk])
                    ACT(orstd[ob][:, 0:L], pr[:, 0:L], AF.Sqrt, [prk, "cst"], [("orstd", ob)], bias=C("eps"), scale=1.0 / 512)
                    RCP(orstd[ob][:, 0:L], orstd[ob][:, 0:L], [("orstd", ob)], [("orstd", ob)])
                    TTn("dve", otmp[ob][:, :, 0:L], po3[:, :, 0:L], big[:, 4 * h:4 * h + 4, c0:c0 + L], ALU.mult,
                        [pok] + [sgK(4 * h + i) for i in range(4)], [("otmp", ob)])
                    TTn("pool", big[:, 16 + 4 * h:16 + 4 * h + 4, c0:c0 + L], otmp[ob][:, :, 0:L],
                        orstd[ob][:, 0:L].unsqueeze(1).to_broadcast([128, 4, L]), ALU.mult,
                        [("otmp", ob), ("orstd", ob)], [ogK(4 * h + i) for i in range(4)])
                for dc in range(2):
                    idx = 2 * h + dc
                    pS, pSk = psum()
                    MM(pS[:, :], ktok[0:L, ui, idx * 128:(idx + 1) * 128], vtok[0:L, ui, h * 512:(h + 1) * 512], True, True,
                       [("ktok", ui, h), ("vtok", ui, h)], [pSk])
                    STT("dve", S32[:, idx, :], S32[:, idx, :], float(gl[h]), pS[:, :], ALU.mult, ALU.add, [pSk, ("S32", idx)], [("S32", idx)])
                    if mode == "full" and not sample:
                        ACT(Sbf[:, idx, :], S32[:, idx, :], AF.Copy, [("S32", idx)], [("Sbf", idx)])
            if sample:
                DMA("sp", o_ret_s[l, s].rearrange("h (c p) e -> p (h c) e", p=128), S32[:], r=SK)
        if mode == "full":
            w2o = wb_ret_out[l]
            for ob_ in range(4):
                wv, wk = wload("ret_out", l, w2o, 0, 16, ob_ * 256, 256)
                for o2 in range(2):
                    oc = ob_ * 2 + o2
                    ps, pk = psum()
                    for kc in range(16):
                        MM(ps[:, 0:N], wv[:, kc, o2 * 128:(o2 + 1) * 128], big[:, 16 + kc, 0:N], kc == 0, kc == 15, [wk, ogK(kc)], [pk])
                    TTn("dve", xT[:, oc, 0:N], xT[:, oc, 0:N], ps[:, 0:N], ALU.add, [pk, "xT"], ["xT"])
            store_x(tok0, N)

    SK8 = [("S32", i) for i in range(8)]
    BK8 = [("Sbf", i) for i in range(8)]

    import os
    KCUT = int(os.environ.get("KCUT", "99"))

    def ret_layer(l):
        P.op("pool", lambda e: e.memset(S32[:], 0.0), (), SK8)
        for (tok0, N) in MX_TILES:
            ret_tile(l, tok0, N, "state")
        if KCUT <= 1:
            return
        for hf in range(2):
            DMA("sp", sl_in[l][hf].rearrange("p (c e) -> p c e", c=4), S32[:, 4 * hf:4 * hf + 4, :], r=SK8, w=[("sl_in", l, hf)])
            allgather(sl_in[l][hf], sl_all[l][hf], [("sl_in", l, hf)], [("sl_all", l, hf)])
        P.op("pool", lambda e: e.memset(S32[:], 0.0), (), SK8)
        for r in range(4):
            for hf in range(2):
                DMA("sp", xT[:, 4 * hf:4 * hf + 4, :], sl_all[l][hf][r * 128:(r + 1) * 128, :].rearrange("p (c e) -> p c e", c=4),
                    r=[("sl_all", l, hf)], w=["xT"])
            for h in range(4):
                STT("dve", S32[:, 2 * h:2 * h + 2, :], xT[:, 2 * h:2 * h + 2, :], C("scoef", r * 4 + h, r * 4 + h + 1),
                    S32[:, 2 * h:2 * h + 2, :], ALU.mult, ALU.add,
                    ["xT", "cst", ("S32", 2 * h), ("S32", 2 * h + 1)], [("S32", 2 * h), ("S32", 2 * h + 1)])
        ACT(Sbf[:], S32[:], AF.Copy, SK8, BK8)
        if KCUT <= 2:
            return
        for (tok0, N) in MX_TILES:
            ret_tile(l, tok0, N, "full")
        if KCUT <= 3:
            return
        DMA("sp", o_ret_p[l].rearrange("h (c p) e -> p (h c) e", p=128), S32[:], r=SK8)
        for (tok0, N) in MX_SAMPLE:
            ret_tile(l, tok0, N, "full")

    def ffn_layer(l):
        ps_n[0] = 6
        DMA("sp", xh[:], xT_d[:, :, SEG - 2:SEG].rearrange("c p n -> p c n"), r=xkeys(SEG - 128, 128), w=["xh"])
        DMA("sp", hx_in[l].rearrange("p (c n) -> p c n", c=8), xh[:], r=["xh"], w=[("hx_in", l)])
        allgather(hx_in[l], hx_all[l], [("hx_in", l)], [("hx_all", l)])
        DMA("sp", xh4[:], hx_all[l].rearrange("(r p) n -> p r n", p=128), r=[("hx_all", l)], w=["xh4"])
        xhf = xh[:].rearrange("p c n -> p (c n)")
        TS("dve", xhf, xh4[:, 0, :], C("hcoef", 0, 1), None, ALU.mult, None, ["xh4", "cst", "xh"], ["xh"])
        for r in range(1, 4):
            STT("dve", xhf, xh4[:, r, :], C("hcoef", r, r + 1), xhf, ALU.mult, ALU.add, ["xh4", "cst", "xh"], ["xh"])
        rmsnorm(xh, 2, "nffn", l * 8, hTh, sqh, rstdh, "xh", "hTh", ["sqh"], "rstdh")
        HHK = [("hTh", c) for c in range(8)]
        DMA("sp", halo_s[:], st_conv[l], w=[("halo_s", i) for i in range(NFC)])
        w2i = wb_ffn_in[l]
        w2o = wb_ffn_out[l]
        for ti, (tok0, N) in enumerate(FT_TILES):
            sample = tok0 >= SEG
            load_x(tok0, N)
            rmsnorm(xT, N, "nffn", l * 8, hT, sq, rstd, "xT", "hT", SQK, "rstd")
            for cb in range(6):
                ncol = 512 if cb < 5 else 256
                wu, wuk = wload("ffn_in", l, w2i, 0, 8, cb * 512, ncol)
                wg, wgk = wload("ffn_in", l, w2i, 0, 8, DFF + cb * 512, ncol)
                for oc in range(ncol // 128):
                    i = cb * 4 + oc
                    if ti == 0:
                        ph, phk = psum()
                        for kc in range(8):
                            MM(ph[:, 0:2], wu[:, kc, oc * 128:(oc + 1) * 128], hTh[:, kc, 0:2], kc == 0, kc == 7, [wuk] + HHK, [phk])
                        ACT(halo_p[:, i, :], ph[:, 0:2], AF.Copy, [phk], [("halo_p", i)])
                    pu, puk = psum()
                    pg, pgk = psum()
                    for kc in range(8):
                        MM(pu[:, 0:N], wu[:, kc, oc * 128:(oc + 1) * 128], hT[:, kc, 0:N], kc == 0, kc == 7, [wuk] + HK, [puk])
                    for kc in range(8):
                        MM(pg[:, 0:N], wg[:, kc, oc * 128:(oc + 1) * 128], hT[:, kc, 0:N], kc == 0, kc == 7, [wgk] + HK, [pgk])
                    ub = i % 3
                    u = pre[:, ub, :]
                    f = rt[ub]
                    uk = ("pre", ub)
                    fk = ("rt", ub)
                    if sample:
                        u3 = u[:, 0:40].rearrange("p (s n) -> p s n", s=4)
                        uv = [u3[:, :, sh:sh + 8] for sh in range(3)]
                        pu_v = pu[:, 0:N].rearrange("p (s n) -> p s n", s=4)
                        pg_v = pg[:, 0:N].rearrange("p (s n) -> p s n", s=4)
                        f_v = f[:, 0:N].rearrange("p (s n) -> p s n", s=4)
                        a_v = big[:, i, 0:N].rearrange("p (s n) -> p s n", s=4)
                        halo_src = halo_s[:, i, :, :]
                        halo_dst = u3[:, :, 0:2]
                        new_halo = u3[:, :, 8:10]
                        hk = ("halo_s", i)
                    else:
                        uv = [u[:, sh:sh + N] for sh in range(3)]
                        pu_v = pu[:, 0:N]
                        pg_v = pg[:, 0:N]
                        f_v = f[:, 0:N]
                        a_v = big[:, i, 0:N]
                        halo_src = halo_p[:, i, :]
                        halo_dst = u[:, 0:2]
                        new_halo = u[:, N:N + 2]
                        hk = ("halo_p", i)
                    cw = [C("cw", (l * 3 + jj) * NFC + i, (l * 3 + jj) * NFC + i + 1) for jj in range(3)]
                    cbias = C("cb", l * NFC + i, l * NFC + i + 1)
                    ACT(uv[2], pu_v, AF.Copy, [puk], [uk])
                    CP("pool", halo_dst, halo_src, [hk], [uk])
                    ACT(f_v, pu_v, AF.Identity, [puk, "cst"], [fk], bias=cbias, scale=cw[2])
                    STT("dve", f_v, uv[1], cw[1], f_v, ALU.mult, ALU.add, [uk, fk, "cst"], [fk])
                    STT("dve", f_v, uv[0], cw[0], f_v, ALU.mult, ALU.add, [uk, fk, "cst"], [fk])
                    CP("pool", halo_src, new_halo, [uk], [hk])
                    ACT(f_v, f_v, AF.Gelu_apprx_tanh, [fk], [fk])
                    TTn("dve", a_v, f_v, pg_v, ALU.mult, [fk, pgk], [actK(i)])
            for cbk in range(4):
                pss = [psum() for _ in range(2)]
                for kh in range(2):
                    wv, wk = wload("ffn_out", l, w2o, kh * 11, 11, cbk * 256, 256)
                    for oc in range(2):
                        ps, pk = pss[oc]
                        for kc in range(11):
                            kk = kh * 11 + kc
                            MM(ps[:, 0:N], wv[:, kc, oc * 128:(oc + 1) * 128], big[:, kk, 0:N], kk == 0, kk == NFC - 1, [wk, actK(kk)], [pk])
                for oc in range(2):
                    ps, pk = pss[oc]
                    og_ = cbk * 2 + oc
                    TTn("dve", xT[:, og_, 0:N], xT[:, og_, 0:N], ps[:, 0:N], ALU.add, [pk, "xT"], ["xT"])
            store_x(tok0, N)
            if ti == len(FT_TILES) - 2:
                DMA("sp", o_conv_p[l], halo_p[:], r=[("halo_p", i) for i in range(NFC)])
        DMA("sp", o_conv_s[l], halo_s[:], r=[("halo_s", i) for i in range(NFC)])

    kvt = [rt[2][:, 0:256], rt[3][:, 0:256], rt[3][:, 256:512]]
    kvtK = [("rt", 2), ("rt", 3), ("rt", 3)]

    def kv_build():
        for (tok0, N) in FT_TILES:
            sample = tok0 >= SEG
            units = [(0, NS)] if sample else [(u * 128, 128) for u in range(N // 128)]
            load_x(tok0, N)
            rmsnorm(xT, N, "kvn", 0, hT, sq, rstd, "xT", "hT", SQK, "rstd")
            wvs = [wload("kv", 0, wb_kv, 0, 8, cb * 512, 512) for cb in range(3)]
            for (c0, L) in units:
                r0 = tok0 + c0
                DMA("sp", csB[0:L, :], ropeB_d[r0:r0 + L, :], w=["csB"])
                for cb in range(3):
                    wv, wk = wvs[cb]
                    ps, pk = psum()
                    for kc in range(8):
                        MM(ps[0:L, :], hT[:, kc, c0:c0 + L], wv[:, kc, :], kc == 0, kc == 7, [wk] + HK, [pk])
                    rf = rt[cb % 2]
                    rk = ("rt", cb % 2)
                    if cb == 0:
                        ACT(rf[0:L, :], ps[0:L, :], AF.Copy, [pk], [rk])
                    else:
                        ACT(rf[0:L, 256:512], ps[0:L, 256:512], AF.Copy, [pk], [rk])
                        ACT(kvt[0][0:L, :], ps[0:L, 0:256], AF.Square, [pk], [kvtK[0]])
                        P.op("dve", lambda e, L=L: e.tensor_reduce(out=kss[0:L, :], in_=kvt[0][0:L, :].rearrange("p (h d) -> p h d", h=4),
                                                                   axis=AX.X, op=ALU.add), [kvtK[0]], ["kss"])
                        ACT(kss[0:L, :], kss[0:L, :], AF.Sqrt, ["kss", "cst"], ["kss"], bias=C("eps")[0:L, :], scale=1.0 / 64)
                        RCP(kss[0:L, :], kss[0:L, :], ["kss"], ["kss"])
                        k3 = kvt[1][0:L, :].rearrange("p (h d) -> p h d", h=4)
                        TTn("dve", k3, ps[0:L, 0:256].rearrange("p (h d) -> p h d", h=4), kss[0:L, :].unsqueeze(2).to_broadcast([L, 4, 64]),
                            ALU.mult, [pk, "kss"], [kvtK[1]])
                        TTn("dve", k3, k3, C("gk", cb * 64, cb * 64 + 64)[0:L, :].unsqueeze(1).to_broadcast([L, 4, 64]), ALU.mult,
                            [kvtK[1], "cst"], [kvtK[1]])
                        cosb = csB[0:L, 0:32].unsqueeze(1).to_broadcast([L, 4, 32])
                        sinb = csB[0:L, 32:64].unsqueeze(1).to_broadcast([L, 4, 32])
                        x1 = k3[:, :, 0:32]
                        x2 = k3[:, :, 32:64]
                        t3 = kvt[2][0:L, :].rearrange("p (h d) -> p h d", h=4)
                        r3 = rf[0:L, 0:256].rearrange("p (h d) -> p h d", h=4)
                        TTn("pool", t3[:, :, 0:32], x1, cosb, ALU.mult, [kvtK[1], "csB"], [kvtK[2]])
                        TTn("pool", t3[:, :, 32:64], x2, sinb, ALU.mult, [kvtK[1], "csB"], [kvtK[2]])
                        TTn("dve", r3[:, :, 0:32], t3[:, :, 0:32], t3[:, :, 32:64], ALU.subtract, [kvtK[2]], [rk])
                        TTn("pool", t3[:, :, 0:32], x2, cosb, ALU.mult, [kvtK[1], "csB", rk], [kvtK[2]])
                        TTn("pool", t3[:, :, 32:64], x1, sinb, ALU.mult, [kvtK[1], "csB"], [kvtK[2]])
                        TTn("dve", r3[:, :, 32:64], t3[:, :, 0:32], t3[:, :, 32:64], ALU.add, [kvtK[2]], [rk])
                    dst = [o_cmp, o_sel, o_win][cb]
                    DMA("sp", dst[r0:r0 + L, :], rf[0:L, :], r=[rk])
                    if sample and cb == 2:
                        for s_ in range(4):
                            DMA("sp", o_wins[s_, 504:512, :], rf[8 * s_:8 * s_ + 8, :], r=[rk])
                            DMA("sp", o_wins[s_, 0:504, :], cwin_d[s_, 8:512, :])
                    ACT(rowb[0:L, cb * 512:(cb + 1) * 512], rf[0:L, :], AF.Copy, [rk], [("rowb", cb)])
                DMA("sp", kv_loc[r0:r0 + L, :], rowb[0:L, :], r=[("rowb", i) for i in range(3)], w=[("kv_loc", r0)])

    def out_y():
        for (tok0, N) in FT_TILES:
            load_x(tok0, N)
            nb = [(0, NS)] if tok0 >= SEG else [(u * 128, 128) for u in range(N // 128)]
            for (c0, L) in nb:
                i = xi[0] % 2
                xi[0] += 1
                xb = xin[i]
                for half in range(2):
                    ps, pk = psum()
                    for cc in range(4):
                        c = half * 4 + cc
                        TR(ps[0:L, cc * 128:(cc + 1) * 128], xT[:, c, c0:c0 + L], ident32, ["xT", "cst"], [pk])
                    ACT(xb[0:L, half, :], ps[0:L, :], AF.Copy, [pk], xinK[i])
                DMA("sp", o_y[tok0 + c0:tok0 + c0 + L, :].rearrange("p (a b) -> p a b", a=2), xb[0:L], r=xinK[i])

    def phase_b_build():
        esB = contextlib.ExitStack()

        def sbB(name, shape, dt):
            return esB.enter_context(nc.sbuf_tensor("sbB_" + name, list(shape), dt))

        cB = sbB("cB", [128, NCB], F32)
        DMA("sp", cB[:], cstB_d, w=["cB"])

        def CB(name, a=None, b=None):
            o, w = CSTB[name]
            if a is None:
                return cB[:, o:o + w]
            return cB[:, o + a:o + b]

        idxp = sbB("idxp", [128, 64], I32)
        DMA("sp", idxp[:], idxp_d, w=["idxp"])
        ptf = sbB("ptf", [128, 256], F32)
        pti = sbB("pti", [128, 256], I32)
        idxs = sbB("idxs", [128, 256], I32)
        DMA("sp", pti[:], ptab_d.rearrange("s r -> (s r)").partition_broadcast(128), w=["pti"])
        CP("dve", ptf[:], pti[:], ["pti"], ["ptf"])
        STT("dve", ptf[:], ptf[:], 128.0, CB("iota").to_broadcast([128, 256]), ALU.mult, ALU.add, ["ptf", "cB"], ["ptf"])
        CP("dve", idxs[:], ptf[:], ["ptf"], ["idxs"])
        gA = [sbB(f"gA{i}", [128, 512], F32) for i in range(3)]
        gB = [sbB(f"gB{i}", [128, 512], F32) for i in range(3)]
        gW = sbB("gW", [128, 512], F32)
        rb = [sbB(f"rb{i}", [128, 1536], BF16) for i in range(3)]
        cT = [[sbB(f"cT{pg}{c}", [128, 2, 2064], BF16) for c in range(2)] for pg in range(2)]
        ktile = [sbB(f"ktile{i}", [128, 2, 128], BF16) for i in range(4)]
        vaug = [sbB(f"vaug{i}", [128, 4, 65], BF16) for i in range(4)]
        cvaug = sbB("cvaug", [128, 4, 65], BF16)
        w1s = sbB("w1s", [128, 2, 32, 128], BF16)
        w2s = sbB("w2s", [128, 2, 64], BF16)
        posT = sbB("posT", [128, 64], F32)
        posTb = sbB("posTb", [128, 64], BF16)
        pbias = sbB("pbias", [128, 2], F32)
        hidg = [sbB(f"hidg{i}", [128, 128], BF16) for i in range(2)]
        csq = sbB("csq", [64, 128], BF16)
        crs = sbB("crs", [64, 128], F32)
        cko = [sbB(f"cko{i}", [64, 128], BF16) for i in range(2)]
        cvo = [sbB(f"cvo{i}", [64, 128], BF16) for i in range(2)]
        for i in range(4):
            P.op("pool", lambda e, i=i: e.memset(vaug[i][:], 1.0), (), [("vaug", i)])
        P.op("pool", lambda e: e.memset(cvaug[:], 1.0), (), ["cvaug"])
        for pg in range(2):
            for c in range(2):
                P.op("pool", lambda e, pg=pg, c=c: e.memset(cT[pg][c][:], 0.0), (), [("cT", pg, c)])
        for half in range(2):
            DMA("pool", w1s[half * 64:(half + 1) * 64], w1_d.rearrange("c (l d) h -> d c l h", d=64), w=["w1s"])
            DMA("sp", posT[half * 64:(half + 1) * 64, :], posT_d, w=["posT"])
        DMA("pool", w2s[:], w2_d.rearrange("c h d -> h c d"), w=["w2s"])
        CP("dve", posTb[:], posT[:], ["posT"], ["posTb"])
        for c in range(2):
            pp, ppk = psum()
            for l_ in range(32):
                MM(pp[:, 0:1], w1s[0:64, c, l_, :], posTb[0:64, c * 32 + l_:c * 32 + l_ + 1], l_ == 0, l_ == 31, ["w1s", "posTb"], [ppk])
            ACT(pbias[:, c:c + 1], pp[:, 0:1], AF.Copy, [ppk], ["pbias"])

        kt_i = [0]

        def compress_chunk(v, ch):
            pg = ch % 2
            for c in range(2):
                for kv in range(4):
                    pair, half = kv // 2, kv % 2
                    rows = slice(half * 64, half * 64 + 64)
                    ph, phk = psum()
                    n_mm = 0
                    for r_ in range(2):
                        for s_ in range(16):
                            o = 16 * r_ + s_
                            MM(ph[:, 0:128], w1s[rows, c, r_ * 16 + s_, :], cT[pg][c][rows, pair, o:o + 16 * 127 + 1:16], n_mm == 0, n_mm == 31,
                               ["w1s", ("cT", pg, c)], [phk])
                            n_mm += 1
                    hb = (c * 4 + kv) % 2
                    ACT(hidg[hb][:], ph[:, 0:128], AF.Gelu_apprx_tanh, [phk, "pbias"], [("hidg", hb)], bias=pbias[:, c:c + 1])
                    po, pok = psum()
                    MM(po[0:64, 0:128], w2s[:, c, :], hidg[hb][:], True, True, ["w2s", ("hidg", hb)], [pok])
                    if c == 0:
                        ACT(csq[:], po[0:64, 0:128], AF.Square, [pok], ["csq"])
                        pr, prk = psum()
                        MM(pr[0:64, 0:128], ones_bf[0:64, 0:64], csq[:], True, True, ["csq", "ones_bf"], [prk])
                        ACT(crs[:], pr[0:64, 0:128], AF.Sqrt, [prk, "cst"], ["crs"], bias=C("eps")[0:64, :], scale=1.0 / 64)
                        RCP(crs[:], crs[:], ["crs"], ["crs"])
                        TTn("dve", crs[:], crs[:], po[0:64, 0:128], ALU.mult, ["crs", pok], ["crs"])
                        TS("dve", cko[kv % 2][:], crs[:], CB("gk0col")[0:64, :], None, ALU.mult, None, ["crs", "cB"], [("cko", kv % 2)])
                        DMA("sp", cmpK_d[v][pair, rows, ch * 128:(ch + 1) * 128], cko[kv % 2][:], r=[("cko", kv % 2)], w=[("cmpK_d", v)])
                    else:
                        ACT(cvo[kv % 2][:], po[0:64, 0:128], AF.Copy, [pok], [("cvo", kv % 2)])
                        pb, pbk = psumb()
                        TR(pb[:, 0:64], cvo[kv % 2][:], ident_bf[0:64, 0:64], [("cvo", kv % 2), "ident_bf"], [pbk])
                        ACT(cvaug[:, kv, 0:64], pb[:, 0:64], AF.Copy, [pbk], ["cvaug"])
                if c == 1:
                    DMA("sp", cmpV_d[v][ch].rearrange("p (k d) -> p k d", k=4), cvaug[:], r=["cvaug"], w=[("cmpV_d", v)])

        def kv_part(v, r, rbt, rbk, c0, Kd, Vd, kcol, vt, L=128):
            i = kt_i[0] % 4
            kt_i[0] += 1
            pb, pbk = psumb()
            for pr_ in range(2):
                TR(pb[:, pr_ * 128:pr_ * 128 + L], rbt[0:L, c0 + pr_ * 128:c0 + (pr_ + 1) * 128], ident_bf[0:L, 0:L], [rbk, "ident_bf"], [pbk])
            ACT(ktile[i][:, :, 0:L], pb[:, 0:256].rearrange("p (a n) -> p a n", a=2)[:, :, 0:L], AF.Copy, [pbk], [("ktile", i)])
            DMA("sp", Kd[:, :, kcol:kcol + L].rearrange("a p n -> p a n"), ktile[i][:, :, 0:L], r=[("ktile", i)], w=[("Kd", id(Kd))])
            CP("pool", vaug[i][0:L, :, 0:64], rbt[0:L, c0 + 256:c0 + 512].rearrange("p (k d) -> p k d", k=4), [rbk], [("vaug", i)])
            DMA("sp", Vd[vt, 0:L, :].rearrange("p (k d) -> p k d", k=4), vaug[i][0:L], r=[("vaug", i)], w=[("Vd", id(Vd))])

        ri = [0]
        for v in range(NV):
            for r in range(64):
                bi = ri[0] % 3
                ri[0] += 1
                rbt = rb[bi]
                rbk = ("rb", bi)
                if v == 0:
                    src = kv_all[(r % 16) // 2]
                    P.dma("pool", lambda e, rbt=rbt, src=src, r=r: e.indirect_dma_start(
                        out=rbt[:], out_offset=None, in_=src,
                        in_offset=bass.IndirectOffsetOnAxis(ap=idxp[:, r:r + 1], axis=0)), ["idxp", "kv_all"], [rbk])
                else:
                    s = v - 1
                    P.dma("pool", lambda e, bi=bi, s=s, r=r: e.indirect_dma_start(
                        out=gA[bi][:], out_offset=None, in_=ccmp_d,
                        in_offset=bass.IndirectOffsetOnAxis(ap=idxs[:, s * 64 + r:s * 64 + r + 1], axis=0)), ["idxs"], [("gA", bi)])
                    P.dma("pool", lambda e, bi=bi, s=s, r=r: e.indirect_dma_start(
                        out=gB[bi][:], out_offset=None, in_=csel_d,
                        in_offset=bass.IndirectOffsetOnAxis(ap=idxs[:, s * 64 + r:s * 64 + r + 1], axis=0)), ["idxs"], [("gB", bi)])
                    ACT(rbt[:, 0:512], gA[bi][:], AF.Copy, [("gA", bi)], [rbk])
                    CP("dve", rbt[:, 512:1024], gB[bi][:], [("gB", bi)], [rbk])
                    if r >= 60:
                        DMA("sp", gW[:], cwin_d[s, (r - 60) * 128:(r - 59) * 128, :], w=["gW"])
                        CP("dve", rbt[:, 1024:1536], gW[:], ["gW"], [rbk])
                ch, off = r // 16, (r % 16) * 128
                pg = ch % 2
                pb, pbk = psumb()
                for q4 in range(4):
                    TR(pb[:, q4 * 128:(q4 + 1) * 128], rbt[:, q4 * 128:(q4 + 1) * 128], ident_bf[:], [rbk, "ident_bf"], [pbk])
                for c in range(2):
                    ACT(cT[pg][c][:, :, off:off + 128], pb[:, c * 256:(c + 1) * 256].rearrange("p (a n) -> p a n", a=2), AF.Copy,
                        [pbk], [("cT", pg, c)])
                    if r % 16 == 0 and r > 0:
                        CP("pool", cT[1 - pg][c][:, :, 2048:2064], cT[pg][c][:, :, 0:16], [("cT", pg, c)], [("cT", 1 - pg, c)])
                if r % 16 == 0 and r > 0:
                    compress_chunk(v, ch - 1)
                if r == 63:
                    for c in range(2):
                        P.op("pool", lambda e, pg=pg, c=c: e.memset(cT[pg][c][:, :, 2048:2064], 0.0), (), [("cT", pg, c)])
                    compress_chunk(v, 3)
                kv_part(v, r, rbt, rbk, 512, selK_d[v], selV_d[v], r * 128, r)
                if v == 0:
                    kv_part(v, r, rbt, rbk, 1024, winK_d[v], winV_d[v], r * 128, r)
                elif r >= 60:
                    kv_part(v, r, rbt, rbk, 1024, winK_d[v], winV_d[v], (r - 60) * 128, r - 60)
            if v >= 1:
                s = v - 1
                bi = ri[0] % 3
                ri[0] += 1
                rbt = rb[bi]
                rbk = ("rb", bi)
                DMA("sp", rbt[0:8, :], kv_loc[SEG + 8 * s:SEG + 8 * s + 8, :], r=[("kv_loc", SEG)], w=[rbk])
                kv_part(v, 64, rbt, rbk, 512, selK_d[v], selV_d[v], 8192, 64, L=8)
                kv_part(v, 64, rbt, rbk, 1024, winK_d[v], winV_d[v], 512, 4, L=8)
        return esB

    def phase_b_attend():
        esC = contextlib.ExitStack()

        def sbC(name, shape, dt):
            return esC.enter_context(nc.sbuf_tensor("sbC_" + name, list(shape), dt))

        cB = sbC("cB", [128, NCB], F32)
        DMA("sp", cB[:], cstB_d, w=["cB"])

        def CB(name, a=None, b=None):
            o, w = CSTB[name]
            if a is None:
                return cB[:, o:o + w]
            return cB[:, o + a:o + b]

        selK = sbC("selK", [128, 2, 8320], BF16)
        selV = sbC("selV", [128, 65, 260], BF16)
        winK = sbC("winK", [128, 2, 640], BF16)
        winV = sbC("winV", [128, 6, 260], BF16)
        cmpK = sbC("cmpK", [128, 2, 512], BF16)
        cmpV = sbC("cmpV", [128, 5, 260], BF16)
        EK = lambda i: ("big", i)
        q8 = lambda c0: big[:, c0:c0 + 2, :].rearrange("p a (b n) -> p (a b) n", b=4)
        QZ = {"c": [(q8(20), [EK(20), EK(21)]), (q8(4), [EK(4), EK(5)])],
              "r": [(q8(22), [EK(22), EK(23)]), (q8(6), [EK(6), EK(7)])]}
        wmap = sbC("wmap", [128, 4, 128], BF16)
        Dtab = CB("Dtab")
        Er = [sbC(f"Er{i}", [128, 128], BF16) for i in range(2)]
        selT4 = sbC("selT4", [128, 4, 128], BF16)
        msk = [sbC(f"msk{i}", [128, 128], BF16) for i in range(4)]
        accs = [rt[2], rt[3]]
        otok = [sbC(f"otok{i}", [128, 4, 65], F32) for i in range(2)]
        o_tok2 = [rt[0][:, :].rearrange("p (h d) -> p h d", d=64), rt[1][:, :].rearrange("p (h d) -> p h d", d=64)]
        o_bf = big[:, 16:18, :].rearrange("p a (h d) -> p (a h) d", d=64)
        oT = big[:, 18:20, :].rearrange("p a (b n) -> p (a b) n", b=4)
        gts = sbC("gts", [128, 48], F32)
        sm = sbC("sm", [128, 64], F32)
        impb = [pre[:, 3, i * 128:(i + 1) * 128] for i in range(3)]
        vmask = pre[:, 2, 0:512].rearrange("p (a n) -> p a n", a=4)
        sel01 = sbC("sel01", [128, 128], BF16)
        negB = sbC("negB", [128, 8], F32)
        mq = sbC("mq", [128, 8], F32)
        CP("dve", wmap[:].rearrange("p a b -> p (a b)"), CB("wmap"), ["cB"], ["wmap"])
        P.op("pool", lambda e: e.memset(winV[:, 5, :], 0.0), (), ["winV"])
        P.op("pool", lambda e: e.memset(cmpV[:, 4, :], 0.0), (), ["cmpV"])
        P.op("pool", lambda e: e.memset(selV[:, 64, :], 0.0), (), ["selV"])
        for i, (nm, a) in enumerate([("gq", 0), ("gq", 64), ("gk", 0), ("gk", 64), ("gk", 128)]):
            P.op("dve", lambda e, i=i, nm=nm, a=a: e.tensor_reduce(out=mq[:, i:i + 1], in_=C(nm, a, a + 64), axis=AX.X, op=ALU.max,
                                                                   apply_absolute_value=True), ["cst", "mq"], ["mq"])
        for j2 in range(2):
            for br in range(3):
                STT("dve", negB[:, j2 * 3 + br:j2 * 3 + br + 1], mq[:, j2:j2 + 1], -8.0, mq[:, 2 + br:3 + br], ALU.mult, ALU.mult,
                    ["mq", "negB"], ["negB"])
        EK = lambda i: ("big", i)
        KEYOF = {id(selK): ("selK", "selV"), id(cmpK): ("cmpK", "cmpV"), id(winK): ("winK", "winV")}
        SC = 0.125

        def load_view(v):
            DMA("sp", selK[:], selK_d[v].rearrange("a p n -> p a n"), r=[("Kd", id(selK_d[v]))], w=["selK"])
            DMA("sp", selV[:], selV_d[v].rearrange("t p n -> p t n"), r=[("Vd", id(selV_d[v]))], w=["selV"])
            DMA("sp", cmpK[:], cmpK_d[v].rearrange("a p n -> p a n"), r=[("cmpK_d", v)], w=["cmpK"])
            DMA("sp", cmpV[:, 0:4, :], cmpV_d[v].rearrange("t p n -> p t n"), r=[("cmpV_d", v)], w=["cmpV"])
            if v >= 1:
                DMA("sp", winK[:], winK_d[v].rearrange("a p n -> p a n"), r=[("Kd", id(winK_d[v]))], w=["winK"])
                DMA("sp", winV[:, 0:5, :], winV_d[v].rearrange("t p n -> p t n"), r=[("Vd", id(winV_d[v]))], w=["winV"])

        def qblock(l, j2, c0, L, v, qi, wq):
            nq = L
            NQ = 4 * nq
            sample = v >= 1
            pq = [psum(), psum()]
            pgt, pgk = psum_acc()
            for hb in range(2):
                for kc in range(8):
                    MM(pq[hb][0][0:L, :], hT[:, kc, c0:c0 + L], wq[hb][0][:, kc, :], kc == 0, kc == 7, [wq[hb][1]] + HK, [pq[hb][1]])
            for kc in range(8):
                MM(pgt[0:L, 0:48], hT[:, kc, c0:c0 + L], wq[2][0][:, kc, :], kc == 0, kc == 7, [wq[2][1]] + HK, [pgk])
            ACT(gts[0:L, :], pgt[0:L, 0:48], AF.Sigmoid, [pgk], ["gts"])
            qsq = pre[0:L, 0:2, 0:512]
            for hb in range(2):
                ACT(qsq[:, hb, :], pq[hb][0][0:L, :], AF.Square, [pq[hb][1]], [("pre", hb)])
            for hb in range(2):
                P.op("dve", lambda e, hb=hb: e.tensor_reduce(out=sm[0:L, 8 + 8 * hb:16 + 8 * hb], in_=qsq[:, hb, :].rearrange("p (h d) -> p h d", d=64),
                                                            axis=AX.X, op=ALU.add), [("pre", hb), "sm"], ["sm"])
            ACT(sm[0:L, 24:40], sm[0:L, 8:24], AF.Sqrt, ["sm", "cst"], ["sm"], bias=C("eps")[0:L, :], scale=1.0 / 64)
            RCP(sm[0:L, 24:40], sm[0:L, 24:40], ["sm"], ["sm"])
            qn = [rt[0], rt[1]]
            qr_ = [rt[2], rt[3]]
            for hb in range(2):
                q3 = qn[hb][0:L, :].rearrange("p (h d) -> p h d", d=64)
                TTn("dve", q3, pq[hb][0][0:L, :].rearrange("p (h d) -> p h d", d=64),
                    sm[0:L, 24 + 8 * hb:32 + 8 * hb].unsqueeze(2).to_broadcast([L, 8, 64]), ALU.mult, [pq[hb][1], "sm"], [("rt", hb)])
                TTn("pool", q3, q3, C("gq", j2 * 64, j2 * 64 + 64)[0:L, :].unsqueeze(1).to_broadcast([L, 8, 64]), ALU.mult,
                    [("rt", hb), "cst"], [("rt", hb)])
                r3 = qr_[hb][0:L, :].rearrange("p (h d) -> p h d", d=64)
                t3 = pre[0:L, 2 + hb, 0:512].rearrange("p (h d) -> p h d", d=64)
                cosb = csB[0:L, 0:32].unsqueeze(1).to_broadcast([L, 8, 32])
                sinb = csB[0:L, 32:64].unsqueeze(1).to_broadcast([L, 8, 32])
                x1, x2 = q3[:, :, 0:32], q3[:, :, 32:64]
                TTn("pool", t3[:, :, 0:32], x1, cosb, ALU.mult, [("rt", hb), "csB"], [("pre", 2 + hb)])
                TTn("pool", t3[:, :, 32:64], x2, sinb, ALU.mult, [("rt", hb), "csB"], [("pre", 2 + hb)])
                TTn("dve", r3[:, :, 0:32], t3[:, :, 0:32], t3[:, :, 32:64], ALU.subtract, [("pre", 2 + hb)], [("rt", 2 + hb)])
                TTn("pool", t3[:, :, 0:32], x2, cosb, ALU.mult, [("rt", hb), "csB", ("rt", 2 + hb)], [("pre", 2 + hb)])
                TTn("pool", t3[:, :, 32:64], x1, sinb, ALU.mult, [("rt", hb), "csB"], [("pre", 2 + hb)])
                TTn("dve", r3[:, :, 32:64], t3[:, :, 0:32], t3[:, :, 32:64], ALU.add, [("pre", 2 + hb)], [("rt", 2 + hb)])
            for ver, (srcs, vn) in enumerate([(qn, "c"), (qr_, "r")]):
                qp = big[0:L, 12 + 2 * ver:14 + 2 * ver, :]
                for pair in range(2):
                    src4 = srcs[pair][0:L, :].rearrange("p (a g d) -> p a g d", a=2, g=4)
                    dst4 = qp[:, pair, :].rearrange("p (g a d) -> p a g d", a=2, g=4)
                    CP("pool" if pair else "dve", dst4, src4, [("rt", 2 * ver + pair)], [EK(12 + 2 * ver + pair)])
                pb, pbk = psumb()
                for pg_ in range(8):
                    TR(pb[:, pg_ * 128:pg_ * 128 + L], qp[:, pg_ // 4, (pg_ % 4) * 128:(pg_ % 4 + 1) * 128], ident_bf[0:L, 0:L],
                       [EK(12 + 2 * ver), EK(13 + 2 * ver), "ident_bf"], [pbk])
                pb8 = pb[:, :].rearrange("p (a n) -> p a n", a=8)
                ACT(QZ[vn][0][0][0:64, :, 0:L], pb8[0:64, :, 0:L], AF.Copy, [pbk], QZ[vn][0][1])
                ACT(QZ[vn][1][0][64:128, :, 0:L], pb8[64:128, :, 0:L], AF.Copy, [pbk], QZ[vn][1][1])
            exr = CB("exrow_s" if sample else "exrow_p")[0:L, :]
            fir = CB("first_s" if sample else "first_p")[0:L, :]
            curv = 128.0 if sample else float(96 + 2 * qi)
            tt = impb[2][0:L, :]
            if sample:
                TS("dve", tt, CB("qrow")[0:L, :], curv, None, ALU.subtract, None, ["cB"], [("pre", 3)])
            else:
                TS("dve", tt, CB("qrow")[0:L, :], CB("hi")[0:L, :], curv, ALU.subtract, ALU.subtract, ["cB"], [("pre", 3)])
            VM = [("pre", 2)]
            val, av, nf, fb = vmask[0:L, 0, :], vmask[0:L, 1, :], vmask[0:L, 2, :], vmask[0:L, 3, :]
            TS("dve", val, tt, 0.0, None, ALU.is_le, None, [("pre", 3)], VM)
            TTn("dve", val, val, exr, ALU.mult, VM + ["cB"], VM)
            TS("dve", av, val, 1e6, -1e6, ALU.mult, ALU.add, VM, VM)
            TS("dve", fb, tt, 0.0, None, ALU.is_equal, None, [("pre", 3)], VM)
            TS("dve", nf, tt, -1.0, None, ALU.is_equal, None, [("pre", 3)], VM)
            TTn("dve", fb, fb, nf, ALU.add, VM, VM)
            TTn("dve", fb, fb, fir, ALU.add, VM + ["cB"], VM)
            TTn("dve", fb, fb, exr, ALU.mult, VM + ["cB"], VM)
            TS("dve", fb, fb, 1.0, None, ALU.min, None, VM, VM)
            TS("dve", nf, fb, -1.0, 1.0, ALU.mult, ALU.add, VM, VM)
            TS("dve", fb, fb, 1e6, None, ALU.mult, None, VM, VM)
            if not sample:
                r0 = 44 + qi
                DMA("sp", winK[:], winK_d[0][:, :, r0 * 128:(r0 + 5) * 128].rearrange("a p n -> p a n"), r=[("Kd", id(winK_d[0]))], w=["winK"])
                DMA("sp", winV[:, 0:5, :], winV_d[0][r0:r0 + 5].rearrange("t p n -> p t n"), r=[("Vd", id(winV_d[0]))], w=["winV"])
            qoff = 0.0 if sample else float(128 * qi)
            basen = "base_s" if sample else "base_p"
            ei = [0]

            def qsel(ver, kv):
                pair, half = kv // 2, kv % 2
                t, keys = QZ[ver][half]
                return t[:, pair * 4:pair * 4 + 4, 0:nq], keys

            def vwide(Vt, t, kv):
                flat = Vt[:, :, :].rearrange("p t n -> p (t n)")
                o = t * 260 + kv * 65
                if not sample:
                    return flat[:, o:o + 128]
                return flat[:, o:o + 65]

            def run_units(specs, D=2):
                def front(i):
                    sp = specs[i]
                    nt = sp.get("nt", 1)
                    nk = sp["nk"]
                    if nt == 2:
                        b0 = 2 * (i % 2)
                        sck = [("psf", b0), ("psf", b0 + 1)]
                        for t in range(2):
                            MM(psf[b0 + t][0:nk, 0:NQ].rearrange("p (g n) -> p g n", g=4), sp["K"][t], sp["q"][0], True, True,
                               [sp["kkey"]] + sp["q"][1], [sck[t]])
                        sc2 = psbig[0:nk, b0 * 512:(b0 + 2) * 512].rearrange("p (t n) -> p t n", t=2)[:, :, 0:NQ]
                        et = big[0:nk, sp["ti"]:sp["ti"] + 2, 0:NQ]
                        ACT(et, sc2, AF.Exp, sck + ["negB"], [EK(sp["ti"]), EK(sp["ti"] + 1)], bias=negB[0:nk, sp["bidx"]:sp["bidx"] + 1], scale=SC)
                    else:
                        sc, sck = psf[i % 4], ("psf", i % 4)
                        MM(sc[0:nk, 0:NQ].rearrange("p (g n) -> p g n", g=4), sp["K"], sp["q"][0], True, True, [sp["kkey"]] + sp["q"][1], [sck])
                        et = big[0:nk, sp["ti"], 0:NQ]
                        ACT(et, sc[0:nk, 0:NQ], AF.Exp, [sck, "negB"], [EK(sp["ti"])], bias=negB[0:nk, sp["bidx"]:sp["bidx"] + 1], scale=SC)
                    return sp["pre"]() if sp.get("pre") else None

                def back(i, aux):
                    sp = specs[i]
                    nk = sp["nk"]
                    if sp.get("nt", 1) == 2:
                        et = big[0:nk, sp["ti"]:sp["ti"] + 2, 0:NQ]
                        eks = [EK(sp["ti"]), EK(sp["ti"] + 1)]
                        sp["mask"](et.rearrange("p t (g n) -> p t g n", g=4), eks, aux)
                        for t in range(2):
                            vw = sp["V"][t]
                            MM(sp["acc"][0][0:vw.shape[1], 0:NQ], vw, big[0:nk, sp["ti"] + t, 0:NQ], sp["first"] and t == 0, sp["last"] and t == 1,
                               [sp["vkey"], eks[t]], [sp["acc"][1]])
                        return
                    et = big[0:nk, sp["ti"], 0:NQ]
                    sp["mask"](et.rearrange("p (g n) -> p g n", g=4), EK(sp["ti"]), aux)
                    vw = sp["V"]
                    MM(sp["acc"][0][0:vw.shape[1], 0:NQ], vw, et, sp["first"], sp["last"], [sp["vkey"], EK(sp["ti"])], [sp["acc"][1]])

                n_ = len(specs)
                pend = [front(k) for k in range(min(D, n_))]
                for i in range(n_):
                    if i + D < n_:
                        pend.append(front(i + D))
                    back(i, pend[i])

            def finish_branch(acc, kv, br, first_branch):
                ai = (kv * 3 + br) % 2
                ACT(accs[ai][0:65, 0:NQ], acc[0][0:65, 0:NQ], AF.Copy, [acc[1]], [("rt", 2 + ai)])
                pt_, ptk = psum()
                for g in range(4):
                    TR(pt_[0:nq, g * 65:(g + 1) * 65], accs[ai][0:65, g * nq:(g + 1) * nq], ident32[0:65, 0:65], [("rt", 2 + ai), "cst"], [ptk])
                ACT(otok[ai][0:nq].rearrange("p g d -> p (g d)"), pt_[0:nq, 0:260], AF.Copy, [ptk], [("otok", ai)])
                TS("dve", sm[0:nq, 0:4], otok[ai][0:nq, :, 64], 1e-30, None, ALU.max, None, [("otok", ai)], ["sm"])
                RCP(sm[0:nq, 0:4], sm[0:nq, 0:4], ["sm"], ["sm"])
                gsl = gts[0:nq, :].rearrange("p (h b) -> p h b", b=3)[:, 4 * kv:4 * kv + 4, br]
                TTn("dve", sm[0:nq, 4:8], sm[0:nq, 0:4], gsl, ALU.mult, ["sm", "gts"], ["sm"])
                dst = o_tok2[kv // 2][0:nq, 4 * (kv % 2):4 * (kv % 2) + 4, :]
                fbc = sm[0:nq, 4:8].unsqueeze(2).to_broadcast([nq, 4, 64])
                if first_branch:
                    TTn("dve", dst, otok[ai][0:nq, :, 0:64], fbc, ALU.mult, [("otok", ai), "sm"], [("rt", kv // 2)])
                else:
                    TTn("dve", otok[ai][0:nq, :, 0:64], otok[ai][0:nq, :, 0:64], fbc, ALU.mult, [("otok", ai), "sm"], [("otok", ai)])
                    TTn("dve", dst, dst, otok[ai][0:nq, :, 0:64], ALU.add, [("otok", ai), ("rt", kv // 2)], [("rt", kv // 2)])
                return ai

            for kv in range(4):
                pair, half = kv // 2, kv % 2
                rows = slice(half * 64, half * 64 + 64)
                acc = psum_acc()
                specs = []
                for c in range(4):
                    def pre_c(c=c):
                        mk = msk[c]
                        TS("dve", mk[:, 0:nq], CB("qrow")[:, 0:nq], qoff, CB(basen, c, c + 1), ALU.add, ALU.is_ge, ["cB"], [("msk", c)])
                        return mk

                    def m_c(e3, ek, mk, c=c):
                        TTn("dve", e3, e3, mk[:, 0:nq].unsqueeze(1).to_broadcast([128, 4, nq]), ALU.mult, [ek, ("msk", c)], [ek])
                    specs.append(dict(K=cmpK[:, pair, c * 128:(c + 1) * 128], q=qsel("c", kv), nk=128, bidx=j2 * 3 + 0, pre=pre_c, mask=m_c,
                                      V=vwide(cmpV, c, kv), acc=acc, first=(c == 0), last=(c == 3), ti=c, kkey="cmpK", vkey="cmpV"))
                run_units(specs)
                ai = finish_branch(acc, kv, 0, True)
                pim, pimk = psum()
                for g in range(4):
                    for c in range(4):
                        MM(pim[0:nq, g * 128:(g + 1) * 128], big[:, c, g * nq:(g + 1) * nq], wmap[:, c, :], c == 0, c == 3, [EK(c), "wmap"], [pimk])
                imp = impb[0][0:L, :]
                TS("dve", imp, pim[0:nq, 0:128], sm[0:nq, 0:1], None, ALU.mult, None, [pimk, "sm"], [("pre", 3)])
                for g in range(1, 4):
                    STT("dve", imp, pim[0:nq, g * 128:(g + 1) * 128], sm[0:nq, g:g + 1], imp, ALU.mult, ALU.add, [pimk, "sm", ("pre", 3)], [("pre", 3)])
                TTn("dve", imp, imp, val, ALU.mult, [("pre", 3)] + VM, [("pre", 3)])
                TTn("dve", imp, imp, av, ALU.add, [("pre", 3)] + VM, [("pre", 3)])
                TTn("dve", imp, imp, nf, ALU.mult, [("pre", 3)] + VM, [("pre", 3)])
                TTn("dve", imp, imp, fb, ALU.add, [("pre", 3)] + VM, [("pre", 3)])
                P.op("dve", lambda e: e.max(sm[0:L, 40:48], imp), [("pre", 3)], ["sm"])
                P.op("dve", lambda e: e.match_replace(impb[1][0:L, :], sm[0:L, 40:48], imp, -3e6), [("pre", 3), "sm"], [("pre", 3)])
                P.op("dve", lambda e: e.max(sm[0:L, 48:56], impb[1][0:L, :]), [("pre", 3)], ["sm"])
                kth = 54 if sample else 55
                TS("dve", impb[1][0:L, :], imp, sm[0:L, kth:kth + 1], None, ALU.is_ge, None, [("pre", 3), "sm"], [("pre", 3)])
                STT("dve", sel01[0:L, :], imp, -5e5, impb[1][0:L, :], ALU.is_gt, ALU.mult, [("pre", 3)], ["sel01"])
                pb, pbk = psumb()
                TR(pb[:, 0:L], sel01[0:L, :], ident_bf[0:L, 0:L], ["sel01", "ident_bf"], [pbk])
                ACT(selT4[:, kv, 0:L], pb[:, 0:L], AF.Copy, [pbk], ["selT4"])
            ntile = 64 if sample else 49 + qi
            for kvp in range(2):
                kvs = (2 * kvp, 2 * kvp + 1)
                pair = kvp
                accS = {kv: (psf[4 + kv % 2], ("psf", 4 + kv % 2)) for kv in kvs}
                specs = []
                pmr = {}
                DT = [0, 2, 8, 10]
                npair = ntile // 2 if sample else (ntile - 1) // 2
                for rp in range(npair):
                    r = 2 * rp
                    for kv in kvs:
                        def pre_d(r=r, kv=kv, kvs=kvs):
                            if kv == kvs[0]:
                                pbi = ps_i[1] % 2
                                ps_i[1] += 1
                                for t in range(2):
                                    eb = (r + t) % 2
                                    TS("dve", Er[eb][:], Dtab, float(2 * (r + t)), None, ALU.is_equal, None, ["cB"], [("Er", eb)])
                                    MM(psb32[pbi][:, t * 2 * nq:(t + 1) * 2 * nq].rearrange("p (k n) -> p k n", k=2), Er[eb][:],
                                       selT4[:, kvs[0]:kvs[0] + 2, 0:nq], True, True, [("Er", eb), "selT4"], [("psb", pbi)])
                                pmr[("d", r)] = (psb32[pbi][:, 0:4 * nq].rearrange("p (t k n) -> p t k n", t=2, k=2), ("psb", pbi))
                            return pmr[("d", r)]

                        def m_d(e4, eks, aux, kv=kv):
                            pm4, pmk = aux
                            TTn("dve", e4, e4, pm4[:, :, kv % 2, :].unsqueeze(2).to_broadcast([128, 2, 4, nq]), ALU.mult, eks + [pmk], eks)
                        specs.append(dict(nt=2, K=[selK[:, pair, (r + t) * 128:(r + t + 1) * 128] for t in range(2)], q=qsel("r", kv), nk=128,
                                          bidx=j2 * 3 + 1, pre=pre_d, mask=m_d, V=[vwide(selV, r + t, kv) for t in range(2)], acc=accS[kv],
                                          first=(r == 0), last=False, ti=DT[len(specs) % 4], kkey="selK", vkey="selV"))
                for r in range(2 * npair, ntile):
                    for kv in kvs:
                        def pre_s(r=r, kv=kv, kvs=kvs):
                            if kv == kvs[0]:
                                eb = r % 2
                                TS("dve", Er[eb][:], Dtab, float(2 * r), None, ALU.is_equal, None, ["cB"], [("Er", eb)])
                                pbi = ps_i[1] % 2
                                ps_i[1] += 1
                                pm2 = psb32[pbi][:, 0:2 * nq].rearrange("p (k n) -> p k n", k=2)
                                MM(pm2, Er[eb][:], selT4[:, kvs[0]:kvs[0] + 2, 0:nq], True, True, [("Er", eb), "selT4"], [("psb", pbi)])
                                pmr[r] = (pm2, ("psb", pbi))
                            return pmr[r]

                        def m_s(e3, ek, aux, r=r, kv=kv):
                            pm2, pmk = aux
                            if (not sample) and r == 48 + qi:
                                mi = kv % 2
                                TTn("dve", msk[mi][:, 0:nq], pm2[:, kv % 2, :], CB("tri_le")[:, 0:nq], ALU.mult, [pmk, "cB"], [("msk", mi)])
                                TTn("dve", e3, e3, msk[mi][:, 0:nq].unsqueeze(1).to_broadcast([128, 4, nq]), ALU.mult, [ek, ("msk", mi)], [ek])
                            else:
                                TTn("dve", e3, e3, pm2[:, kv % 2, :].unsqueeze(1).to_broadcast([128, 4, nq]), ALU.mult, [ek, pmk], [ek])
                        specs.append(dict(K=selK[:, pair, r * 128:(r + 1) * 128], q=qsel("r", kv), nk=128, bidx=j2 * 3 + 1, pre=pre_s, mask=m_s,
                                          V=vwide(selV, r, kv), acc=accS[kv], first=(r == 0), last=(r == ntile - 1 and not sample),
                                          ti=(DT[len(specs) % 4] if not sample else 8 + len(specs) % 4), kkey="selK", vkey="selV"))
                if sample:
                    for kv in kvs:
                        def m_n(e3, ek, aux):
                            TTn("dve", e3, e3, CB("tri_le")[0:8, 0:nq].unsqueeze(1).to_broadcast([8, 4, nq]), ALU.mult, [ek, "cB"], [ek])
                        specs.append(dict(K=selK[:, pair, 8192:8200], q=qsel("r", kv), nk=8, bidx=j2 * 3 + 1, pre=None, mask=m_n,
                                          V=selV[0:8, 64, kv * 65:(kv + 1) * 65], acc=accS[kv], first=False, last=True,
                                          ti=DT[len(specs) % 4], kkey="selK", vkey="selV"))
                run_units(specs)
                for kv in kvs:
                    finish_branch(accS[kv], kv, 1, False)
                specs = []
                for kv in kvs:
                    for w in range(5):
                        if sample and w == 4:
                            def m_w(e3, ek, aux):
                                TTn("dve", e3, e3, CB("tri_le")[0:8, 0:nq].unsqueeze(1).to_broadcast([8, 4, nq]), ALU.mult, [ek, "cB"], [ek])
                            specs.append(dict(K=winK[:, pair, 512:520], q=qsel("r", kv), nk=8, bidx=j2 * 3 + 2, pre=None, mask=m_w,
                                              V=winV[0:8, 4, kv * 65:(kv + 1) * 65], acc=accS[kv], first=False, last=True,
                                              ti=8 + len(specs) % 4, kkey="winK", vkey="winV"))
                            continue

                        def m_w(e3, ek, aux, w=w):
                            if sample:
                                if w == 0:
                                    TTn("dve", e3, e3, CB("tri_gt")[:, 0:nq].unsqueeze(1).to_broadcast([128, 4, nq]), ALU.mult, [ek, "cB"], [ek])
                                return
                            exc = CB("extile", 44 + qi + w, 45 + qi + w)
                            if w == 0 or w == 4:
                                tri = CB("tri_gt" if w == 0 else "tri_le")[:, 0:nq].unsqueeze(1).to_broadcast([128, 4, nq])
                                STT("dve", e3, e3, exc, tri, ALU.mult, ALU.mult, [ek, "cB"], [ek])
                            else:
                                TS("dve", e3, e3, exc, None, ALU.mult, None, [ek, "cB"], [ek])
                        specs.append(dict(K=winK[:, pair, w * 128:(w + 1) * 128], q=qsel("r", kv), nk=128, bidx=j2 * 3 + 2, pre=None, mask=m_w,
                                          V=vwide(winV, w, kv), acc=accS[kv], first=(w == 0), last=(w == 4 and not sample),
                                          ti=8 + len(specs) % 4, kkey="winK", vkey="winV"))
                run_units(specs)
                for kv in kvs:
                    finish_branch(accS[kv], kv, 2, False)
            for hb in range(2):
                CP("dve", o_bf[0:L, 8 * hb:8 * hb + 8, :], o_tok2[hb][0:L], [("rt", hb)], [EK(16 + hb)])
            pb, pbk = psumb()
            for c in range(8):
                TR(pb[:, c * 128:c * 128 + L], o_bf[0:L, 2 * c:2 * c + 2, :].rearrange("p h d -> p (h d)"), ident_bf[0:L, 0:L], [EK(16), EK(17), "ident_bf"], [pbk])
            ACT(oT[:, :, 0:L], pb[:, :].rearrange("p (a n) -> p a n", a=8)[:, :, 0:L], AF.Copy, [pbk], [EK(18), EK(19)])

        def wo_apply(l, j2, c0, L):
            for ob_ in range(2):
                wv, wk = wload("wo", j2, wb_o[j2], 0, 8, ob_ * 512, 512)
                for o4 in range(4):
                    oc = ob_ * 4 + o4
                    ps, pk = psum()
                    for kc in range(8):
                        MM(ps[:, 0:L], wv[:, kc, o4 * 128:(o4 + 1) * 128], oT[:, kc, 0:L], kc == 0, kc == 7, [wk, EK(18), EK(19)], [pk])
                    TTn("dve", xT[:, oc, c0:c0 + L], xT[:, oc, c0:c0 + L], ps[:, 0:L], ALU.add, [pk, "xT"], ["xT"])

        def nsa_layer(l):
            j2 = l - 2
            ps_n[0] = 2
            for vn in ("c", "r"):
                P.op("pool", lambda e, vn=vn: e.memset(QZ[vn][0][0][64:128], 0.0), (), QZ[vn][0][1])
                P.op("pool", lambda e, vn=vn: e.memset(QZ[vn][1][0][0:64], 0.0), (), QZ[vn][1][1])
            load_view(0)
            for ti, (tok0, N) in enumerate(FT_TILES):
                sample = tok0 >= SEG
                load_x(tok0, N)
                rmsnorm(xT, N, "nmix", l * 8, hT, sq, rstd, "xT", "hT", SQK, "rstd")
                def wqs():
                    return [wload("wqg", j2, wb_qg[j2], 0, 8, 0, 512), wload("wqg", j2, wb_qg[j2], 0, 8, 512, 512),
                            wload("wqg", j2, wb_qg[j2], 0, 8, 1024, 48)]
                if not sample:
                    for qb in range(4):
                        qi = ti * 4 + qb
                        DMA("sp", csB[:, :], ropeB_d[tok0 + qb * 128:tok0 + (qb + 1) * 128, :], w=["csB"])
                        qblock(l, j2, qb * 128, 128, 0, qi, wqs())
                        wo_apply(l, j2, qb * 128, 128)
                else:
                    for s in range(4):
                        load_view(1 + s)
                        DMA("sp", csB[0:8, :], ropeB_d[tok0 + 8 * s:tok0 + 8 * s + 8, :], w=["csB"])
                        qblock(l, j2, 8 * s, 8, 1 + s, 0, wqs())
                        wo_apply(l, j2, 8 * s, 8)
                store_x(tok0, N)

        return esC, nsa_layer

    nl = min(stage, 2)
    for l in range(nl):
        ret_layer(l)
        if KCUT <= 4:
            break
        ffn_layer(l)
    if stage >= 3:
        kv_build()
    if stage >= 4:
        for ch in range(8):
            allgather(kv_loc[256 * ch:256 * (ch + 1), :], kv_all[ch], [("kv_loc", 256 * ch + 128 * i) for i in range(2)], ["kv_all"])
        P.barrier()
        P.flush()
        esA.close()
        esB = phase_b_build()
        P.barrier()
        P.flush()
        esB.close()
        if stage >= 5:
            esC, nsa_layer = phase_b_attend()
            for l in range(2, min(stage - 3, 4)):
                nsa_layer(l)
                ffn_layer(l)
    out_y()
    P.finish()
    print("ops:", {e: P.cnt[e] for e in P.engs})
    return nc, es


def make_in_maps(inp):
    maps = []
    for c in range(8):
        b, j = c // 4, c % 4
        cst, ropeA, ropeB = host_tables(c, inp)
        sc = inp["state_conv"][:, 4 * c:4 * c + 4]
        sc = sc.reshape(4, 4, 2, NFC, 128).transpose(0, 4, 3, 1, 2)
        m = {
            "xp": np.ascontiguousarray(inp["x_prompt"][b, j * SEG:(j + 1) * SEG]),
            "xs": np.ascontiguousarray(inp["x_sample"][4 * c:4 * c + 4].reshape(NS, D)),
            "cst": cst, "ropeA": ropeA, "ropeB": ropeB,
            "ret_w_in": inp["ret_w_in"], "ret_w_out": inp["ret_w_out"],
            "ffn_w_in": inp["ffn_w_in"], "ffn_w_out": inp["ffn_w_out"], "kv_w": inp["kv_w"],
            "state_ret": np.ascontiguousarray(inp["state_ret"][:, 4 * c:4 * c + 4]),
            "state_conv": np.ascontiguousarray(sc),
            "cache_cmp": inp["cache_cmp_kv"].reshape(-1, 512), "cache_sel": inp["cache_sel_kv"].reshape(-1, 512),
            "cache_win": np.ascontiguousarray(inp["cache_win_kv"][4 * c:4 * c + 4].reshape(4, 512, 512)),
            "ptab": np.ascontiguousarray(inp["page_table"][4 * c:4 * c + 4]).astype(np.int32),
            "cmp_w1": inp["cmp_w1"], "cmp_w2": inp["cmp_w2"],
            "posT": np.ascontiguousarray(inp["cmp_pos"].transpose(2, 0, 1).reshape(64, 64)),
            "nsa_w_qg": inp["nsa_w_qg"], "nsa_w_o": inp["nsa_w_o"],
        }
        m["cstB"], m["idxp"] = host_tables_b(c, inp)
        maps.append(m)
    return maps


_CACHE = {}


def run_device(inp, stage=99):
    inp = {k: np.asarray(v) for k, v in inp.items()}
    if stage not in _CACHE:
        _CACHE[stage] = build_program(stage)
    nc, es = _CACHE[stage]
    res = run_bass_kernel_spmd(nc, make_in_maps(inp), core_ids=list(range(8)))
    return res.results


def kernel(**inputs):
    res = run_device(inputs)
    f32 = np.float32
    y_p = np.zeros((2, 8192, D), f32)
    y_s = np.zeros((32, 8, D), f32)
    ret_p = np.zeros((2, 2, 4, 256, 512), f32)
    ret_s = np.zeros((2, 32, 4, 256, 512), f32)
    conv_p = np.zeros((4, 2, 2, DFF), f32)
    conv_s = np.zeros((4, 32, 2, DFF), f32)
    cmp_p = np.zeros((2, 8192, 2, 4, 64), f32)
    cmp_s = np.zeros((32, 8, 2, 4, 64), f32)
    sel_p = np.zeros((2, 8192, 2, 4, 64), f32)
    sel_s = np.zeros((32, 8, 2, 4, 64), f32)
    win_p = np.zeros((2, 512, 2, 4, 64), f32)
    win_s = np.zeros((32, 512, 2, 4, 64), f32)
    for c in range(8):
        b, j = c // 4, c % 4
        r = res[c]
        sl = slice(j * SEG, (j + 1) * SEG)
        y_p[b, sl] = r["o_y"][:SEG]
        y_s[4 * c:4 * c + 4] = r["o_y"][SEG:].reshape(4, 8, D)
        ret_s[:, 4 * c:4 * c + 4] = r["o_ret_s"]
        conv_s[:, 4 * c:4 * c + 4] = r["o_conv_s"].transpose(0, 3, 4, 2, 1).reshape(4, 4, 2, DFF)
        cmp_p[b, sl] = r["o_cmp"][:SEG].reshape(SEG, 2, 4, 64)
        sel_p[b, sl] = r["o_sel"][:SEG].reshape(SEG, 2, 4, 64)
        cmp_s[4 * c:4 * c + 4] = r["o_cmp"][SEG:].reshape(4, 8, 2, 4, 64)
        sel_s[4 * c:4 * c + 4] = r["o_sel"][SEG:].reshape(4, 8, 2, 4, 64)
        win_s[4 * c:4 * c + 4] = r["o_wins"].reshape(4, 512, 2, 4, 64)
        if j == 3:
            ret_p[:, b] = r["o_ret_p"]
            conv_p[:, b] = r["o_conv_p"].transpose(0, 3, 2, 1).reshape(4, 2, DFF)
            win_p[b] = r["o_win"][SEG - 512:SEG].reshape(512, 2, 4, 64)
    return (y_p, y_s, ret_p, ret_s, conv_p, conv_s, cmp_p, cmp_s, sel_p, sel_s, win_p, win_s)
```
